# Optimizing a Trainium2 kernel written in Bass

```python
import math
import jax
import jax.numpy as jnp
from jax import lax
import numpy as np


D_MODEL = 2048
BATCH = 2
SEQ = 8192
DEPTH = 2

HEAD_DIM = 64
N_HEADS_A = D_MODEL // (2 * HEAD_DIM)
N_HEADS_B = D_MODEL // (2 * HEAD_DIM)
N_HEADS_C = D_MODEL // (2 * HEAD_DIM)
QUERY_BLOCK = 128
DILATED_PATTERNS = ((128, 1), (512, 4), (2048, 16))
ALIBI_MAX_EXP = 8.0
SSM_HEAD_DIM = 64
SSM_HEADS = D_MODEL // (2 * SSM_HEAD_DIM)
SSM_D_INNER = SSM_HEADS * SSM_HEAD_DIM
SSM_GROUPS = 4
SSM_STATE = 128
CONV_WIDTH = 4
SSD_CHUNK = 128
CONV_CH = SSM_D_INNER + 2 * SSM_GROUPS * SSM_STATE
D_FF = 4 * D_MODEL
AB_IN = 3 * (N_HEADS_A + N_HEADS_B) * HEAD_DIM
AB_OUT = (N_HEADS_A + N_HEADS_B) * HEAD_DIM
CD_IN = 3 * N_HEADS_C * HEAD_DIM + N_HEADS_C + SSM_D_INNER + CONV_CH + SSM_HEADS
CD_OUT = N_HEADS_C * HEAD_DIM + SSM_D_INNER
N_EVEN = (DEPTH + 1) // 2
N_ODD = DEPTH // 2
NORM_EPS = 1e-6
NEG_INF = -1e30
FORGET_BIAS_MEAN = 3.0

kernel_name = 'hybrid_sb_dilated_fox_ssd_trunk'


def _rmsnorm(x, w):
    xf = x.astype(jnp.float32)
    y = xf * lax.rsqrt(jnp.mean(xf * xf, axis=-1, keepdims=True) + NORM_EPS)
    return (y * w.astype(jnp.float32)).astype(x.dtype)


def _split(t, sizes):
    cuts = [int(c) for c in np.cumsum(sizes)[:-1]]
    return jnp.split(t, cuts, axis=-1)


def _to_heads(t, n_heads):
    b, s, _ = t.shape
    return t.reshape(b, s, n_heads, -1).transpose(0, 2, 1, 3)


def _from_heads(t):
    b, h, s, d = t.shape
    return t.transpose(0, 2, 1, 3).reshape(b, s, h * d)


def _sweep_query_blocks(block_fn, q):
    b, h, s, d = q.shape
    out = lax.map(block_fn, jnp.arange(s // QUERY_BLOCK))
    return jnp.moveaxis(out, 0, 2).reshape(b, h, s, out.shape[-1])


def _block_queries(q, i):
    t0 = i * QUERY_BLOCK
    qb = lax.dynamic_slice_in_dim(q, t0, QUERY_BLOCK, axis=2)
    return qb, t0 + jnp.arange(QUERY_BLOCK)


def _stick_breaking_attention(q, k, v):
    seq = q.shape[2]
    scale = HEAD_DIM ** -0.5
    key_pos = jnp.arange(seq)

    def block(i):
        qb, t = _block_queries(q, i)
        z = jnp.einsum('bhqd,bhkd->bhqk', qb, k).astype(jnp.float32) * scale
        strict = key_pos[None, :] < t[:, None]
        log_keep = jnp.where(strict, jax.nn.log_sigmoid(-z), 0.0)
        later = lax.cumsum(log_keep, axis=3, reverse=True) - log_keep
        weight = jnp.where(strict, jnp.exp(jax.nn.log_sigmoid(z) + later), 0.0)
        return jnp.einsum('bhqk,bhkd->bhqd', weight.astype(v.dtype), v)

    return _sweep_query_blocks(block, q)


def _alibi_slopes(n_heads):
    return jnp.exp2(-ALIBI_MAX_EXP * jnp.arange(1, n_heads + 1, dtype=jnp.float32) / n_heads)


def _dilated_window_attention(q, k, v):
    slopes = _alibi_slopes(q.shape[1])[None, :, None, None]
    scale = HEAD_DIM ** -0.5

    def block(i):
        qb, t = _block_queries(q, i)
        lses, outs = [], []
        for window, dilation in DILATED_PATTERNS:
            dist = dilation * jnp.arange(window // dilation + 1)
            idx = t[:, None] - dist[None, :]
            valid = idx >= 0
            idx = jnp.maximum(idx, 0)
            kg = jnp.take(k, idx, axis=2)
            vg = jnp.take(v, idx, axis=2)
            logits = (jnp.einsum('bhqd,bhqjd->bhqj', qb, kg).astype(jnp.float32) * scale
                      - slopes * dist.astype(jnp.float32))
            logits = jnp.where(valid, logits, NEG_INF)
            lse = jax.nn.logsumexp(logits, axis=-1, keepdims=True)
            p = jnp.exp(logits - lse)
            lses.append(lse)
            outs.append(jnp.einsum('bhqj,bhqjd->bhqd', p.astype(v.dtype), vg).astype(jnp.float32))
        alpha = jax.nn.softmax(jnp.stack(lses), axis=0)
        return jnp.sum(alpha * jnp.stack(outs), axis=0).astype(v.dtype)

    return _sweep_query_blocks(block, q)


def _forgetting_attention(q, k, v, log_f_cum):
    seq = q.shape[2]
    scale = HEAD_DIM ** -0.5
    key_pos = jnp.arange(seq)

    def block(i):
        qb, t = _block_queries(q, i)
        f_q = lax.dynamic_slice_in_dim(log_f_cum, i * QUERY_BLOCK, QUERY_BLOCK, axis=2)
        logits = (jnp.einsum('bhqd,bhkd->bhqk', qb, k).astype(jnp.float32) * scale
                  + f_q[..., :, None] - log_f_cum[..., None, :])
        logits = jnp.where(key_pos[None, :] <= t[:, None], logits, NEG_INF)
        p = jax.nn.softmax(logits, axis=-1)
        return jnp.einsum('bhqk,bhkd->bhqd', p.astype(v.dtype), v)

    return _sweep_query_blocks(block, q)


def _causal_depthwise_conv(x, w, b):
    y = lax.conv_general_dilated(x, w[:, None, :], window_strides=(1,),
                                 padding=[(CONV_WIDTH - 1, 0)],
                                 dimension_numbers=('NWC', 'WIO', 'NWC'),
                                 feature_group_count=x.shape[-1])
    return y + b


def _ssd_chunked(xh, dt, a_neg, b_mat, c_mat):
    bsz, seqlen, n_heads, p = xh.shape
    g, n = b_mat.shape[-2:]
    e = n_heads // g
    q = SSD_CHUNK
    nc = seqlen // q
    f32 = jnp.float32
    x = xh.astype(f32).reshape(bsz, nc, q, g, e, p)
    dtc = dt.reshape(bsz, nc, q, g, e)
    bc = b_mat.astype(f32).reshape(bsz, nc, q, g, n)
    cc = c_mat.astype(f32).reshape(bsz, nc, q, g, n)
    a_cum = jnp.cumsum(dtc * a_neg.reshape(g, e), axis=2)
    xdt = x * dtc[..., None]
    seg = a_cum[:, :, :, None] - a_cum[:, :, None, :]
    causal = jnp.tril(jnp.ones((q, q), dtype=bool))[None, None, :, :, None, None]
    decay = jnp.exp(jnp.where(causal, seg, NEG_INF))
    cb = jnp.einsum('bclgn,bcsgn->bclsg', cc, bc)
    y_diag = jnp.einsum('bclsge,bcsgep->bclgep', cb[..., None] * decay, xdt)
    decay_to_end = jnp.exp(a_cum[:, :, -1:] - a_cum)
    states = jnp.einsum('bclgn,bclgep->bcgepn', bc, xdt * decay_to_end[..., None])
    chunk_decay = jnp.exp(a_cum[:, :, -1])

    def step(h, inp):
        dec, st = inp
        return dec[..., None, None] * h + st, h

    h0 = jnp.zeros((bsz, g, e, p, n), f32)
    _, h_in = lax.scan(step, h0, (jnp.moveaxis(chunk_decay, 1, 0), jnp.moveaxis(states, 1, 0)))
    h_in = jnp.moveaxis(h_in, 0, 1)
    y_off = jnp.einsum('bclgn,bcgepn->bclgep', cc, h_in) * jnp.exp(a_cum)[..., None]
    return (y_diag + y_off).reshape(bsz, seqlen, n_heads, p)


def _gated_rmsnorm(y, z, w):
    gated = y.astype(jnp.float32) * jax.nn.silu(z.astype(jnp.float32))
    shape = gated.shape
    gg = gated.reshape(shape[:-1] + (SSM_GROUPS, shape[-1] // SSM_GROUPS))
    gg = gg * lax.rsqrt(jnp.mean(gg * gg, axis=-1, keepdims=True) + NORM_EPS)
    return gg.reshape(shape) * w.astype(jnp.float32)


def _even_mixer(h, w_in, w_out):
    proj = h @ w_in
    qa, ka, va, qb, kb, vb = _split(proj, [N_HEADS_A * HEAD_DIM] * 3 + [N_HEADS_B * HEAD_DIM] * 3)
    oa = _stick_breaking_attention(_to_heads(qa, N_HEADS_A), _to_heads(ka, N_HEADS_A), _to_heads(va, N_HEADS_A))
    ob = _dilated_window_attention(_to_heads(qb, N_HEADS_B), _to_heads(kb, N_HEADS_B), _to_heads(vb, N_HEADS_B))
    return jnp.concatenate([_from_heads(oa), _from_heads(ob)], axis=-1) @ w_out


def _odd_mixer(h, w_in, b_f, conv_w, conv_b, dt_bias, a_log, d_skip, gate_norm, w_out):
    bsz, seq, _ = h.shape
    proj = h @ w_in
    qc, kc, vc, f_raw, z, xbc, dt_raw = _split(
        proj, [N_HEADS_C * HEAD_DIM] * 3 + [N_HEADS_C, SSM_D_INNER, CONV_CH, SSM_HEADS])
    log_f = jax.nn.log_sigmoid((f_raw + b_f).astype(jnp.float32))
    log_f_cum = jnp.cumsum(log_f, axis=1).transpose(0, 2, 1)
    oc = _forgetting_attention(_to_heads(qc, N_HEADS_C), _to_heads(kc, N_HEADS_C),
                               _to_heads(vc, N_HEADS_C), log_f_cum)
    xbc = jax.nn.silu(_causal_depthwise_conv(xbc, conv_w, conv_b))
    xs, b_mat, c_mat = _split(xbc, [SSM_D_INNER, SSM_GROUPS * SSM_STATE, SSM_GROUPS * SSM_STATE])
    xh = xs.reshape(bsz, seq, SSM_HEADS, SSM_HEAD_DIM)
    dt = jax.nn.softplus((dt_raw + dt_bias).astype(jnp.float32))
    a_neg = -jnp.exp(a_log.astype(jnp.float32))
    y = _ssd_chunked(xh, dt, a_neg,
                     b_mat.reshape(bsz, seq, SSM_GROUPS, SSM_STATE),
                     c_mat.reshape(bsz, seq, SSM_GROUPS, SSM_STATE))
    y = y + d_skip.astype(jnp.float32)[:, None] * xh.astype(jnp.float32)
    y = _gated_rmsnorm(y.reshape(bsz, seq, SSM_D_INNER), z, gate_norm).astype(h.dtype)
    return jnp.concatenate([_from_heads(oc), y], axis=-1) @ w_out


def _sq_relu_mlp(h, w_up, w_down):
    return jnp.square(jax.nn.relu(h @ w_up)) @ w_down


def setup_inputs(seed: int = 0) -> dict:
    key = jax.random.key(seed)
    ks = jax.random.split(key, 18)
    f32 = jnp.float32

    def normal(k, shape, scale):
        return jax.random.normal(k, shape, f32) * scale

    def gain(k, shape):
        return 1.0 + 0.05 * jax.random.normal(k, shape, f32)

    dt0 = jnp.exp(jax.random.uniform(ks[11], (N_ODD, SSM_HEADS), f32, math.log(1e-3), math.log(1e-1)))
    return {
        'x': normal(ks[0], (BATCH, SEQ, D_MODEL), 1.0),
        'mix_norm_pre': gain(ks[1], (DEPTH, D_MODEL)),
        'mix_norm_post': gain(ks[2], (DEPTH, D_MODEL)),
        'mlp_norm_pre': gain(ks[3], (DEPTH, D_MODEL)),
        'mlp_norm_post': gain(ks[4], (DEPTH, D_MODEL)),
        'ab_w_in': normal(ks[5], (N_EVEN, D_MODEL, AB_IN), D_MODEL ** -0.5),
        'ab_w_out': normal(ks[6], (N_EVEN, AB_OUT, D_MODEL), AB_OUT ** -0.5),
        'cd_w_in': normal(ks[7], (N_ODD, D_MODEL, CD_IN), D_MODEL ** -0.5),
        'cd_b_f': FORGET_BIAS_MEAN + 0.5 * jax.random.normal(ks[8], (N_ODD, N_HEADS_C), f32),
        'cd_conv_w': normal(ks[9], (N_ODD, CONV_WIDTH, CONV_CH), CONV_WIDTH ** -0.5),
        'cd_conv_b': normal(ks[10], (N_ODD, CONV_CH), 0.01),
        'cd_dt_bias': dt0 + jnp.log(-jnp.expm1(-dt0)),
        'cd_a_log': jnp.log(jax.random.uniform(ks[12], (N_ODD, SSM_HEADS), f32, 1.0, 16.0)),
        'cd_d_skip': 1.0 + 0.1 * jax.random.normal(ks[13], (N_ODD, SSM_HEADS), f32),
        'cd_gate_norm': gain(ks[14], (N_ODD, SSM_D_INNER)),
        'cd_w_out': normal(ks[15], (N_ODD, CD_OUT, D_MODEL), CD_OUT ** -0.5),
        'mlp_w_up': normal(ks[16], (DEPTH, D_MODEL, D_FF), D_MODEL ** -0.5),
        'mlp_w_down': normal(ks[17], (DEPTH, D_FF, D_MODEL), D_FF ** -0.5),
    }


def reference(x, mix_norm_pre, mix_norm_post, mlp_norm_pre, mlp_norm_post, ab_w_in, ab_w_out,
              cd_w_in, cd_b_f, cd_conv_w, cd_conv_b, cd_dt_bias, cd_a_log, cd_d_skip,
              cd_gate_norm, cd_w_out, mlp_w_up, mlp_w_down):
    for layer in range(DEPTH):
        i = layer // 2
        h = _rmsnorm(x, mix_norm_pre[layer])
        if layer % 2 == 0:
            m = _even_mixer(h, ab_w_in[i], ab_w_out[i])
        else:
            m = _odd_mixer(h, cd_w_in[i], cd_b_f[i], cd_conv_w[i], cd_conv_b[i], cd_dt_bias[i],
                           cd_a_log[i], cd_d_skip[i], cd_gate_norm[i], cd_w_out[i])
        x = x + _rmsnorm(m, mix_norm_post[layer])
        h = _rmsnorm(x, mlp_norm_pre[layer])
        x = x + _rmsnorm(_sq_relu_mlp(h, mlp_w_up[layer], mlp_w_down[layer]), mlp_norm_post[layer])
    return x
```

```python
import os as _os
import numpy as np
import concourse.bass as bass
import concourse.mybir as mybir
from concourse.bass_utils import run_bass_kernel_spmd

F32 = mybir.dt.float32
BF16 = mybir.dt.bfloat16
AF = mybir.ActivationFunctionType
ALU = mybir.AluOpType
AX = mybir.AxisListType

ENGS = ("pe", "act", "dve", "pool", "sp")
SAME_ENGINE_SYNC = True


class Res:
    __slots__ = ("name", "t", "lw", "rd", "excl")

    def __init__(self, name, t=None, excl=False):
        self.name = name
        self.t = t
        self.excl = excl
        self.lw = None
        self.rd = []


class Op:
    __slots__ = ("eng", "fn", "deps", "dma", "chan", "cval", "signal", "sval", "out", "wres", "hard")

    def __init__(self, eng, fn, dma=False):
        self.eng = eng
        self.fn = fn
        self.deps = set()
        self.hard = set()
        self.dma = dma
        self.chan = None
        self.cval = 0
        self.signal = False
        self.sval = 0
        self.out = False


class Prog:
    def __init__(self, nc, n_dma_chan=24):
        self.nc = nc
        self.ops = []
        self.ctx = []
        self.drams = {}
        self.n_dma_chan = n_dma_chan
        self.chan_last = {}
        self.chan_cnt = {}
        self.rr = 0
        self.rrq = {}
        self._names = 0
        self.bar = set()
        self.last_eng = {}

    def _enter(self, guard):
        t = guard.__enter__()
        self.ctx.append(guard)
        return t

    def sb(self, name, shape, dt):
        return Res(name, self._enter(self.nc.sbuf_tensor(name, list(shape), dt)))

    def ps(self, name, shape, dt):
        return Res(name, self._enter(self.nc.psum_tensor(name, list(shape), dt)), excl=True)

    def dram(self, name):
        if name not in self.drams:
            self.drams[name] = Res(name)
        return self.drams[name]

    def res(self, name):
        return Res(name)

    def _add(self, op, r, w):
        i = len(self.ops)
        r = list(r)
        w = list(w)
        for x in r:
            if x.excl and x not in w:
                w.append(x)
        for x in r:
            if x.lw is not None:
                op.deps.add(x.lw)
                op.hard.add(x.lw)
        for x in w:
            if x.lw is not None:
                op.deps.add(x.lw)
                op.hard.add(x.lw)
            for j in x.rd:
                op.deps.add(j)
        for x in r:
            x.rd.append(i)
        for x in w:
            x.lw = i
            x.rd = []
        op.deps |= self.bar
        op.deps.discard(i)
        self.ops.append(op)
        self.last_eng[op.eng] = i
        return i

    def mark_outputs(self, res):
        for op in self.ops:
            if op.dma and res in getattr(op, "wres", ()):
                op.out = True

    def barrier(self):
        self.bar = set(self.last_eng.values()) | set(self.chan_last.values())

    def scope_mark(self):
        return len(self.ctx)

    def scope_free(self, mark):
        self.barrier()
        while len(self.ctx) > mark:
            self.ctx.pop().__exit__(None, None, None)

    def pe(self, fn, r=(), w=()):
        return self._add(Op("pe", fn), r, w)

    def act(self, fn, r=(), w=()):
        return self._add(Op("act", fn), r, w)

    def dve(self, fn, r=(), w=()):
        return self._add(Op("dve", fn), r, w)

    def pool(self, fn, r=(), w=()):
        return self._add(Op("pool", fn), r, w)

    def dma(self, q, out_ap, in_ap, r=(), w=(), out=False, chan=None, **kw):
        op = Op(q, lambda e: e.dma_start(out=out_ap, in_=in_ap, **kw), dma=True)
        if chan is None:
            nch = 8 if q == "pool" else self.n_dma_chan
            k = self.rrq.get(q, 0)
            self.rrq[q] = k + 1
            chan = (0 if q == "pool" else 1) * 100 + (k % nch)
        op.chan = chan
        op.out = out
        op.wres = tuple(w)
        prev = self.chan_last.get(chan)
        if prev is not None:
            op.deps.add(prev)
        self.chan_cnt[chan] = self.chan_cnt.get(chan, 0) + 16
        op.cval = self.chan_cnt[chan]
        i = self._add(op, r, w)
        self.chan_last[chan] = i
        return i

    def coll(self, kind, in_ap, out_ap, groups, r=(), w=()):
        op = Op("pool", lambda e: e.collective_compute(kind, ALU.bypass, replica_groups=groups,
                                                        ins=[in_ap], outs=[out_ap]), dma=True)
        k = self.rrq.get("coll", 0)
        self.rrq["coll"] = k + 1
        chan = 200 + (k % 4)
        op.chan = chan
        op.wres = tuple(w)
        prev = self.chan_last.get(chan)
        if prev is not None:
            op.deps.add(prev)
        self.chan_cnt[chan] = self.chan_cnt.get(chan, 0) + 16
        op.cval = self.chan_cnt[chan]
        i = self._add(op, r, w)
        self.chan_last[chan] = i
        return i

    def emit(self):
        nc = self.nc
        ops = self.ops
        def needs_sem(p, op, d=None):
            if p.dma:
                return False
            if op.dma:
                return True
            if p.eng != op.eng:
                return True
            if p.eng == "pe":
                return False
            if d is not None and d not in op.hard:
                return False
            return SAME_ENGINE_SYNC

        for i, op in enumerate(ops):
            for d in op.deps:
                if needs_sem(ops[d], op, d):
                    ops[d].signal = True
        cnt = {e: 0 for e in ENGS}
        for op in ops:
            if not op.dma and op.signal:
                cnt[op.eng] += 1
                op.sval = cnt[op.eng]
        sems = {}
        for e in ENGS:
            sems[e] = self._enter(nc.semaphore("s_" + e))
        csems = {}
        for c in sorted(self.chan_cnt):
            csems[c] = self._enter(nc.semaphore("c_%d" % c))
        final_waits = [(csems[op.chan], op.cval) for op in ops if op.dma and op.out]
        per_eng = {e: [] for e in ENGS}
        for i, op in enumerate(ops):
            per_eng[op.eng].append(i)

        def run(eng_name, eng):
            waited = {}
            for i in per_eng[eng_name]:
                op = ops[i]
                need = {}
                for d in op.deps:
                    p = ops[d]
                    if p.dma:
                        key, val = ("c", p.chan), p.cval
                    else:
                        if not needs_sem(p, op, d):
                            continue
                        key, val = ("e", p.eng), p.sval
                    if need.get(key, 0) < val:
                        need[key] = val
                for key, val in need.items():
                    if waited.get(key, 0) >= val:
                        continue
                    waited[key] = val
                    s = csems[key[1]] if key[0] == "c" else sems[key[1]]
                    eng.wait_ge(s, val)
                ins = op.fn(eng)
                if op.dma:
                    ins.then_inc(csems[op.chan], 16)
                elif op.signal:
                    ins.then_inc(sems[eng_name], 1)
            if eng_name == "sp":
                done = {}
                for s, v in final_waits:
                    k = id(s)
                    if k not in done or done[k][1] < v:
                        done[k] = (s, v)
                for s, v in done.values():
                    eng.wait_ge(s, v)

        with nc.Block() as block:
            @block.sync
            def _(e):
                run("sp", e)

            @block.scalar
            def _(e):
                run("act", e)

            @block.vector
            def _(e):
                run("dve", e)

            @block.gpsimd
            def _(e):
                run("pool", e)

            @block.tensor
            def _(e):
                run("pe", e)
        for g in reversed(self.ctx):
            g.__exit__(None, None, None)
        self.ctx = []


D_MODEL = 2048
D_FF = 8192
SEQ = 8192
BATCH = 2
NCORES = 8
EPS = 1e-6
TG = 512


class Ctx:
    pass


def make_common(p, consts_ap):
    c = Ctx()
    c.banks = [p.ps("bank%d" % i, [128, 512], F32) for i in range(8)]
    c.bi = 0
    c.ident = p.sb("ident", [128, 128], F32)
    c.ones_bf = p.sb("ones_bf", [128, 128], BF16)
    c.ones32 = p.sb("ones32", [128, 128], F32)
    p.dma("sp", c.ident.t[:], consts_ap[:, 0:128], w=[c.ident])
    p.dma("sp", c.ones32.t[:], consts_ap[:, 128:256], w=[c.ones32])
    p.dve(lambda e: e.tensor_copy(c.ones_bf.t[:], c.ones32.t[:]), r=[c.ones32], w=[c.ones_bf])
    c.eps = p.sb("eps_col", [128, 1], F32)
    p.dve(lambda e: e.memset(c.eps.t[:], EPS), w=[c.eps])
    return c


def next_bank(c, n=5):
    b = c.banks[c.bi % n]
    c.bi += 1
    return b


def phase_c(p, c, *, D, DFF, F, NT, OT_src, OT_res, xT, xT_res, w_out, w_up, w_down, nv,
            v_post, v_pre, v_mpost, v_next, hT_dst=None, hT_res=None, out_dst=None, out_res=None,
            pfx="c", stop=99, x_src=None, x_res=None, bufs=None, xT_out=None, xT_out_res=None, wq="pool", nwb=3):
    if xT_out is None:
        xT_out, xT_out_res = xT, xT_res
    KC = D // 128
    FKC = F // 128
    FC = DFF // 128
    NH = 2
    FCH = FC // NH
    UW = min(4, FCH)
    DW = min(2, KC)
    OW = min(4, KC)
    WELEMS = max(FKC * OW * 128, KC * UW * 128, FCH * DW * 128)
    if bufs is not None and "xg" in bufs:
        xg, ag, mg, ug, sq, tmp, rbc, wb = (bufs[k] for k in ("xg", "ag", "mg", "ug", "sq", "tmp", "rbc", "wb"))
        NWB = len(wb)
    else:
        xg = p.sb(pfx + "xg", [128, KC, TG], F32)
        ag = p.sb(pfx + "ag", [128, max(KC, FKC), TG], BF16)
        mg = p.sb(pfx + "mg", [128, KC, TG], F32)
        ug = p.sb(pfx + "ug", [128, FCH, TG], BF16)
        sq = [p.sb(pfx + "sq%d" % i, [128, TG], BF16) for i in range(2)]
        tmp = [p.sb(pfx + "tmp%d" % i, [128, TG], F32) for i in range(2)]
        rbc = p.sb(pfx + "rbc", [128, TG], F32)
        NWB = nwb
        wb = [p.sb(pfx + "wb%d" % i, [128, WELEMS], BF16) for i in range(NWB)]
        if bufs is not None:
            bufs.update(xg=xg, ag=ag, mg=mg, ug=ug, sq=sq, tmp=tmp, rbc=rbc, wb=wb)
    st = {"w": 0, "s": 0, "t": 0}
    stat_bank = c.banks[5]
    misc_bank = c.banks[7]

    def wbuf():
        b = wb[st["w"] % NWB]
        st["w"] += 1
        return b

    def stats_accum(src_ap, src_res, first, last):
        s = sq[st["s"] % 2]
        st["s"] += 1
        p.act(lambda e: e.activation(s.t[:], src_ap, AF.Square), r=src_res, w=[s])
        p.pe(lambda e: e.matmul(stat_bank.t[:], c.ones_bf.t[:], s.t[:], start=first, stop=last),
             r=[s, c.ones_bf] + ([] if first else [stat_bank]), w=[stat_bank])

    def rstd_from_stats():
        p.dve(lambda e: e.tensor_scalar(rbc.t[:], stat_bank.t[:], 1.0 / D, None, ALU.mult),
              r=[stat_bank], w=[rbc])
        p.dve(lambda e: e.tensor_scalar(rbc.t[:], rbc.t[:], EPS, None, ALU.add),
              r=[rbc], w=[rbc])
        p.act(lambda e: e.activation(rbc.t[:], rbc.t[:], AF.Sqrt), r=[rbc], w=[rbc])
        p.dve(lambda e: e.reciprocal(rbc.t[:], rbc.t[:]), r=[rbc], w=[rbc])

    def residual_update(vcol):
        for cc in range(KC):
            t = tmp[st["t"] % 2]
            st["t"] += 1
            p.dve(lambda e, cc=cc, t=t: e.scalar_tensor_tensor(
                t.t[:], mg.t[:, cc, :], nv.t[:, vcol + cc:vcol + cc + 1], rbc.t[:], ALU.mult, ALU.mult),
                r=[mg, nv, rbc], w=[t])
            p.pool(lambda e, cc=cc, t=t: e.tensor_tensor(xg.t[:, cc, :], xg.t[:, cc, :], t.t[:], ALU.add),
                   r=[xg, t], w=[xg])
            stats_accum(xg.t[:, cc, :], [xg], cc == 0, cc == KC - 1)

    def norm_to_ag(vcol):
        rstd_from_stats()
        for cc in range(KC):
            p.dve(lambda e, cc=cc: e.scalar_tensor_tensor(
                ag.t[:, cc, :], xg.t[:, cc, :], nv.t[:, vcol + cc:vcol + cc + 1], rbc.t[:], ALU.mult, ALU.mult),
                r=[xg, nv, rbc], w=[ag])

    for g in range(NT // TG):
        t0 = g * TG
        if x_src is not None:
            ov = mg.t[:].rearrange("p a b -> p (a b)").rearrange("p (tt d) -> p tt d", d=D)
            p.dma("sp", ov, x_src(t0).rearrange("(tt p) d -> p tt d", p=128), r=[x_res], w=[mg])
            for tt in range(TG // 128):
                for k4 in range(max(1, KC // 4)):
                    nk = min(4, KC)
                    bk = next_bank(c)
                    for j in range(nk):
                        kc = k4 * 4 + j
                        p.pe(lambda e, kc=kc, j=j, tt=tt, bk=bk: e.transpose(
                            bk.t[:, j * 128:(j + 1) * 128], ov[:, tt, kc * 128:(kc + 1) * 128], c.ident.t[:]),
                            r=[mg, c.ident], w=[bk])
                    p.act(lambda e, tt=tt, k4=k4, nk=nk, bk=bk: e.copy(
                        xg.t[:, k4 * 4:k4 * 4 + nk, tt * 128:(tt + 1) * 128],
                        bk.t[:, 0:nk * 128].rearrange("p (a b) -> p a b", a=nk)), r=[bk], w=[xg])
            for cc in range(KC):
                stats_accum(xg.t[:, cc, :], [xg], cc == 0, cc == KC - 1)
            norm_to_ag(v_next)
            p.dma("sp", hT_dst(t0).rearrange("(kc p) t -> p kc t", p=128), ag.t[:, 0:KC, :], r=[ag], w=[hT_res])
            p.dma("sp", xT_out[:, t0:t0 + TG].rearrange("(kc p) t -> p kc t", p=128), xg.t[:], r=[xg], w=[xT_out_res])
            continue
        p.dma("sp", xg.t[:], xT[:, t0:t0 + TG].rearrange("(kc p) t -> p kc t", p=128), r=[xT_res], w=[xg])
        p.dma("sp", ag.t[:, 0:FKC, :], OT_src(t0).rearrange("(kc p) t -> p kc t", p=128), r=[OT_res], w=[ag])
        if stop < 1:
            return
        for q in range(KC // OW):
            wt = wbuf()
            wv = wt.t[:, 0:FKC * OW * 128].rearrange("p (kc c) -> p kc c", kc=FKC)
            p.dma(wq, wv, w_out[:, q * OW * 128:(q + 1) * OW * 128].rearrange("(kc p) c -> p kc c", p=128),
                  w=[wt])
            for j in range(OW):
                cc = q * OW + j
                bk = next_bank(c)
                for kc in range(FKC):
                    p.pe(lambda e, kc=kc, j=j, bk=bk, wv=wv: e.matmul(
                        bk.t[:], wv[:, kc, j * 128:(j + 1) * 128], ag.t[:, kc, :], start=(kc == 0), stop=(kc == FKC - 1)),
                        r=[wt, ag] + ([] if kc == 0 else [bk]), w=[bk])
                p.dve(lambda e, cc=cc, bk=bk: e.tensor_copy(mg.t[:, cc, :], bk.t[:]), r=[bk], w=[mg])
                stats_accum(bk.t[:], [bk], cc == 0, cc == KC - 1)
        if stop < 2:
            return
        rstd_from_stats()
        if stop < 3:
            return
        residual_update(v_post)
        if stop < 4:
            return
        norm_to_ag(v_pre)
        if stop < 5:
            return
        for h in range(NH):
            for q in range(FCH // UW):
                wt = wbuf()
                wv = wt.t[:, 0:KC * UW * 128].rearrange("p (kc c) -> p kc c", kc=KC)
                f0 = (h * FCH + q * UW) * 128
                p.dma(wq, wv, w_up[:, f0:f0 + UW * 128].rearrange("(kc p) c -> p kc c", p=128), w=[wt])
                for j in range(UW):
                    fl = q * UW + j
                    bk = next_bank(c)
                    for kc in range(KC):
                        p.pe(lambda e, kc=kc, j=j, bk=bk, wv=wv: e.matmul(
                            bk.t[:], wv[:, kc, j * 128:(j + 1) * 128], ag.t[:, kc, :], start=(kc == 0), stop=(kc == KC - 1)),
                            r=[wt, ag] + ([] if kc == 0 else [bk]), w=[bk])
                    t = tmp[st["t"] % 2]
                    st["t"] += 1
                    p.act(lambda e, bk=bk, t=t: e.activation(t.t[:], bk.t[:], AF.Relu), r=[bk], w=[t])
                    p.dve(lambda e, fl=fl, t=t: e.tensor_tensor(ug.t[:, fl, :], t.t[:], t.t[:], ALU.mult), r=[t], w=[ug])
            for q in range(KC // DW):
                wt = wbuf()
                wv = wt.t[:, 0:FCH * DW * 128].rearrange("p (fc c) -> p fc c", fc=FCH)
                r0 = h * FCH * 128
                p.dma(wq, wv, w_down[r0:r0 + FCH * 128, q * DW * 128:(q + 1) * DW * 128].rearrange(
                    "(fc p) c -> p fc c", p=128), w=[wt])
                for j in range(DW):
                    cc = q * DW + j
                    bk = next_bank(c)
                    for fc in range(FCH):
                        p.pe(lambda e, fc=fc, j=j, bk=bk, wv=wv: e.matmul(
                            bk.t[:], wv[:, fc, j * 128:(j + 1) * 128], ug.t[:, fc, :], start=(fc == 0), stop=(fc == FCH - 1)),
                            r=[wt, ug] + ([] if fc == 0 else [bk]), w=[bk])
                    if h == 0:
                        p.dve(lambda e, cc=cc, bk=bk: e.tensor_copy(mg.t[:, cc, :], bk.t[:]), r=[bk], w=[mg])
                    else:
                        p.dve(lambda e, cc=cc, bk=bk: e.tensor_tensor(mg.t[:, cc, :], mg.t[:, cc, :], bk.t[:], ALU.add),
                              r=[bk, mg], w=[mg])
                        stats_accum(mg.t[:, cc, :], [mg], cc == 0, cc == KC - 1)
        rstd_from_stats()
        residual_update(v_mpost)
        if hT_dst is not None:
            norm_to_ag(v_next)
            p.dma("sp", hT_dst(t0).rearrange("(kc p) t -> p kc t", p=128), ag.t[:, 0:KC, :], r=[ag], w=[hT_res])
            p.dma("sp", xT_out[:, t0:t0 + TG].rearrange("(kc p) t -> p kc t", p=128), xg.t[:], r=[xg], w=[xT_out_res])
        if out_dst is not None:
            ov = mg.t[:].rearrange("p a b -> p (a b)").rearrange("p (tt d) -> p tt d", d=D)
            for tt in range(TG // 128):
                for k4 in range(KC // 4 if KC >= 4 else 1):
                    nk = min(4, KC)
                    bk = next_bank(c)
                    for j in range(nk):
                        kc = k4 * 4 + j
                        p.pe(lambda e, kc=kc, j=j, tt=tt, bk=bk: e.transpose(
                            bk.t[:, j * 128:(j + 1) * 128], xg.t[:, kc, tt * 128:(tt + 1) * 128], c.ident.t[:]),
                            r=[xg, c.ident], w=[bk])
                    p.act(lambda e, tt=tt, k4=k4, nk=nk, bk=bk: e.copy(
                        ov[:, tt, k4 * 512:k4 * 512 + nk * 128], bk.t[:, 0:nk * 128]), r=[bk], w=[mg])
            p.dma("sp", out_dst(t0).rearrange("(tt p) d -> p tt d", p=128), ov, r=[mg], w=[out_res], out=True)


def _consts():
    return np.concatenate([np.eye(128, dtype=np.float32), np.ones((128, 128), np.float32),
                           np.zeros((128, 128), np.float32)], axis=1)


def _nv_layout(vecs):
    return np.ascontiguousarray(np.concatenate([np.asarray(v, np.float32).reshape(-1, 128).T for v in vecs], axis=1))


def build_program_skeleton(NT=SEQ * BATCH // NCORES, D=D_MODEL, DFF=D_FF):
    KC = D // 128
    nc = bass.Bass("TRN2", target_bir_lowering=False)
    x = nc.dram_tensor("x", [NT, D], F32, kind="ExternalInput").ap()
    cst = nc.dram_tensor("consts", [128, 384], F32, kind="ExternalInput").ap()
    nvd = nc.dram_tensor("nv", [128, 8 * KC], F32, kind="ExternalInput").ap()
    w_out = [nc.dram_tensor("w_out%d" % l, [D, D], F32, kind="ExternalInput").ap() for l in range(2)]
    w_up = [nc.dram_tensor("w_up%d" % l, [D, DFF], F32, kind="ExternalInput").ap() for l in range(2)]
    w_down = [nc.dram_tensor("w_down%d" % l, [DFF, D], F32, kind="ExternalInput").ap() for l in range(2)]
    out = nc.dram_tensor("out", [NT, D], F32, kind="ExternalOutput").ap()
    xT = nc.dram_tensor("xT_scr", [D, NT], F32).ap()
    hT = [nc.dram_tensor("hT_scr%d" % l, [D, NT], BF16).ap() for l in range(2)]
    p = Prog(nc)
    c = make_common(p, cst)
    nv = p.sb("nv_sb", [128, 8 * KC], F32)
    p.dma("sp", nv.t[:], nvd, w=[nv])
    bufs = {}
    common = dict(D=D, DFF=DFF, F=D, NT=NT, xT=xT, xT_res=p.dram("xT"), nv=nv, bufs=bufs)
    V = lambda i: i * KC
    phase_c(p, c, OT_src=None, OT_res=None, w_out=None, w_up=None, w_down=None,
            v_post=0, v_pre=0, v_mpost=0, v_next=V(0),
            hT_dst=lambda t0: hT[0][:, t0:t0 + TG], hT_res=p.dram("hT0"),
            x_src=lambda t0: x[t0:t0 + TG, :], x_res=p.dram("x"), pfx="c", **common)
    phase_c(p, c, OT_src=lambda t0: hT[0][:, t0:t0 + TG], OT_res=p.dram("hT0"),
            w_out=w_out[0], w_up=w_up[0], w_down=w_down[0],
            v_post=V(1), v_pre=V(2), v_mpost=V(3), v_next=V(4),
            hT_dst=lambda t0: hT[1][:, t0:t0 + TG], hT_res=p.dram("hT1"), pfx="c", **common)
    phase_c(p, c, OT_src=lambda t0: hT[1][:, t0:t0 + TG], OT_res=p.dram("hT1"),
            w_out=w_out[1], w_up=w_up[1], w_down=w_down[1],
            v_post=V(5), v_pre=V(6), v_mpost=V(7), v_next=0,
            out_dst=lambda t0: out[t0:t0 + TG, :], out_res=p.dram("out"), pfx="c", **common)
    p.emit()
    return nc


N_DUMMY = int(_os.environ.get("DUMMY", "0"))


def pe_warm(p, c, W, n=None):
    for _ in range(N_DUMMY if n is None else n):
        p.pe(lambda e: e.matmul(c.banks[6].t[:], c.ones_bf.t[:], W.wbf[0].t[:], start=True, stop=True,
                                skip_group_check=True), r=[], w=[])


def mm(p, out_ap, lhsT_ap, rhs_ap, start, stop, r, w):
    return p.pe(lambda e: e.matmul(out_ap, lhsT_ap, rhs_ap, start=start, stop=stop, skip_group_check=True), r=r, w=w)


def phase_a(p, c, *, S, D, hT_all, hT_res, w_fm, w_tm, NFM, NTM, fm_chunks, fm_dst, fm_res,
            v_dst, v_res, nvh, arena, fd=None, nfd=0, convw=None):
    KC = D // 128
    NG = S // TG
    o1 = KC * NFM
    o2 = o1 + KC * NTM
    o3 = o2 + 2 * KC * TG
    assert o3 <= arena.t.shape[1], (o3, arena.t.shape)
    wfm = Res("a_wfm", arena.t[:, 0:o1].rearrange("p (kc c) -> p kc c", kc=KC))
    wtm = Res("a_wtm", arena.t[:, o1:o2].rearrange("p (kc c) -> p kc c", kc=KC))
    KP = min(4, KC)
    wfm_parts = [Res("a_wfm_p%d" % i, wfm.t) for i in range(KC // KP)]
    wtm_parts = [Res("a_wtm_p%d" % i, wtm.t) for i in range(KC // KP)]
    for i in range(KC // KP):
        p.dma("pool", wfm.t[:, i * KP:(i + 1) * KP, :],
              w_fm[i * KP * 128:(i + 1) * KP * 128, :].rearrange("(kc p) c -> p kc c", p=128), w=[wfm_parts[i]])
        p.dma("pool", wtm.t[:, i * KP:(i + 1) * KP, :],
              w_tm[i * KP * 128:(i + 1) * KP * 128, :].rearrange("(kc p) c -> p kc c", p=128), w=[wtm_parts[i]])
    hgs = [Res("a_hg%d" % i, arena.t[:, o2 + i * KC * TG:o2 + (i + 1) * KC * TG].rearrange("p (kc t) -> p kc t", kc=KC))
           for i in range(2)]
    vgs = [p.sb("a_vg%d" % i, [128, TG // 128, nvh, 65], BF16) for i in range(2)]
    ots = [p.sb("a_ot%d" % i, [128, TG], BF16) for i in range(3)]
    nconv = sum(1 for ch in fm_chunks if ch.get("conv") is not None)
    cbs = [p.sb("a_cb%d" % i, [128, TG + 3], F32) for i in range(nconv)]
    accs = [p.sb("a_acc%d" % i, [128, TG], F32) for i in range(2)]
    for cb in cbs:
        p.dve(lambda e, cb=cb: e.memset(cb.t[:], 0.0), w=[cb])
    for vg in vgs:
        p.dve(lambda e, vg=vg: e.memset(vg.t[:], 1.0), w=[vg])
    st = {"o": 0, "a": 0}
    v_view = v_dst.rearrange("h p (b x) -> p b h x", x=65)
    for g in range(NG):
        t0 = g * TG
        hg = hgs[g % 2]
        p.dma("sp", hg.t[:], hT_all[:, t0:t0 + TG].rearrange("(kc p) t -> p kc t", p=128), r=[hT_res], w=[hg])
        for ch in fm_chunks:
            wd, col0, row0 = ch["w"], ch["col"], ch["row"]
            bk = next_bank(c)
            for kc in range(KC):
                mm(p, bk.t[0:wd, :], wfm.t[:, kc, col0:col0 + wd], hg.t[:, kc, :], kc == 0, kc == KC - 1,
                   r=[wfm_parts[kc // KP], hg] + ([] if kc == 0 else [bk]), w=[bk])
            ot = ots[st["o"] % 3]
            st["o"] += 1
            ci = ch.get("conv")
            if ci is None:
                sc = ch.get("scale", 1.0)
                if sc == 1.0:
                    p.act(lambda e, ot=ot, bk=bk, wd=wd: e.copy(ot.t[0:wd, :], bk.t[0:wd, :]), r=[bk], w=[ot])
                else:
                    p.act(lambda e, ot=ot, bk=bk, wd=wd, sc=sc: e.mul(ot.t[0:wd, :], bk.t[0:wd, :], sc), r=[bk], w=[ot])
            else:
                cb = cbs[ci]
                acc = accs[st["a"] % 2]
                st["a"] += 1
                p.dve(lambda e, cb=cb, wd=wd: e.tensor_copy(cb.t[0:wd, 0:3], cb.t[0:wd, TG:TG + 3]), r=[cb], w=[cb])
                p.act(lambda e, cb=cb, bk=bk, wd=wd: e.copy(cb.t[0:wd, 3:TG + 3], bk.t[0:wd, :]), r=[bk], w=[cb])
                p.dve(lambda e, cb=cb, acc=acc, wd=wd, ci=ci: e.tensor_scalar(
                    acc.t[0:wd, :], cb.t[0:wd, 3:TG + 3], convw.t[0:wd, ci, 3:4], convw.t[0:wd, ci, 4:5],
                    ALU.mult, ALU.add), r=[cb, convw], w=[acc])
                for j in (2, 1, 0):
                    p.dve(lambda e, cb=cb, acc=acc, wd=wd, ci=ci, j=j: e.scalar_tensor_tensor(
                        acc.t[0:wd, :], cb.t[0:wd, j:j + TG], convw.t[0:wd, ci, j:j + 1], acc.t[0:wd, :],
                        ALU.mult, ALU.add), r=[cb, convw, acc], w=[acc])
                p.act(lambda e, ot=ot, acc=acc, wd=wd: e.activation(ot.t[0:wd, :], acc.t[0:wd, :], AF.Silu),
                      r=[acc], w=[ot])
            p.dma("sp", fm_dst[row0:row0 + wd, t0:t0 + TG], ot.t[0:wd, :], r=[ot], w=[fm_res])
        vg = vgs[g % 2]
        for tt in range(TG // 128):
            blk = g * (TG // 128) + tt
            bk = next_bank(c)
            for kc in range(KC):
                mm(p, bk.t[:, 0:NTM], hg.t[:, kc, tt * 128:(tt + 1) * 128], wtm.t[:, kc, :], kc == 0, kc == KC - 1,
                   r=[wtm_parts[kc // KP], hg] + ([] if kc == 0 else [bk]), w=[bk])
            p.act(lambda e, tt=tt, bk=bk, vg=vg: e.copy(
                vg.t[:, tt, 0:nvh, 0:64], bk.t[:, 0:nvh * 64].rearrange("p (h d) -> p h d", h=nvh)), r=[bk], w=[vg])
            if fd is not None:
                p.dve(lambda e, blk=blk, bk=bk: e.tensor_copy(fd.t[:, blk, :], bk.t[:, nvh * 64:nvh * 64 + nfd]),
                      r=[bk], w=[fd])
        nb = TG // 128
        for h in range(nvh):
            p.dma("sp", v_view[:, g * nb:(g + 1) * nb, h, :], vg.t[:, :, h, :], r=[vg], w=[v_res])


class AttnWork:
    def __init__(self, p, c, masks_sb, sid=0, zbanks=(0, 1, 2, 3), obanks=(4, 5), misc=7, nt32=3, light=False):
        sfx = "_s%d" % sid
        self.t32 = [p.sb("w_t32_%d%s" % (i, sfx), [128, TG], F32) for i in range(nt32)]
        self.e32 = [p.sb("w_e32_%d%s" % (i, sfx), [128, TG], F32) for i in range(1 if light else 3)]
        self.wbf = [p.sb("w_wbf_%d%s" % (i, sfx), [128, TG], BF16) for i in range(4)]
        self.spb = [p.sb("w_spb_%d%s" % (i, sfx), [128, TG], BF16) for i in range(1 if light else 3)]
        self.accs = [p.sb("w_acc%d%s" % (i, sfx), [128, TG], BF16) for i in range(1 if light else 2)]
        self.osb = [p.sb("w_osb_%d%s" % (i, sfx), [128, TG], F32) for i in range(2)]
        self.obf = [p.sb("w_obf_%d%s" % (i, sfx), [128, TG], BF16) for i in range(2)]
        self.rden = p.sb("w_rden" + sfx, [128, TG], F32)
        self.masks = masks_sb
        self.onecol = p.sb("w_onecol" + sfx, [128, 1], F32)
        p.dve(lambda e: e.memset(self.onecol.t[:], 1.0), w=[self.onecol])
        self.n = {"t": 0, "e": 0, "w": 0, "s": 0, "o": 0, "z": 0, "d": 0, "g": 0}
        self.zb = [c.banks[i] for i in zbanks]
        self.ob = [c.banks[i] for i in obanks]
        self.misc = c.banks[misc]

    def nxt(self, lst, key):
        x = lst[self.n[key] % len(lst)]
        self.n[key] += 1
        return x


def load_masks(p, masks_ap, nmask):
    m = p.sb("w_masks", [128, nmask, 896], F32)
    p.dma("sp", m.t[:], masks_ap.rearrange("p (m x) -> p m x", m=nmask), w=[m])
    return m


def run_streams(gens):
    gens = list(gens)
    while gens:
        for g in list(gens):
            try:
                next(g)
            except StopIteration:
                gens.remove(g)


def chain(*gens):
    for g in gens:
        yield from g


def run_pipeline(n, stages):
    lo = min(o for o, _ in stages)
    hi = max(o for o, _ in stages)
    for s_ in range(-hi, n - lo):
        for o, fn in stages:
            t = s_ + o
            if 0 <= t < n:
                fn(t)


def make_tiles(NG, W, kb_fn):
    T = []
    for qg in range(NG):
        kbs = kb_fn(qg)
        n = len(kbs)
        for idx, kb in enumerate(kbs):
            T.append(dict(qg=qg, q0=qg * TG, idx=idx, n=n, kb=kb, diag=(kb >= 4 * qg), dl=qg * TG - kb * 128,
                          ob=W.ob[qg % len(W.ob)]))
    return T


def attn_sb(p, c, W, *, S, kT, qT, qk_res, vt, negU, negones, ot_dst, ot_res, M_RS=0, M_NEGS=1):
    NG = S // TG
    T = make_tiles(NG, W, lambda qg: list(range(4 * qg + 3, -1, -1)))
    accst = {"acc": None}

    def s_z(i):
        t = T[i]
        t["zb"] = W.zb[i % len(W.zb)]
        kb, q0 = t["kb"], t["q0"]
        mm(p, t["zb"].t[:], kT[:, kb * 128:(kb + 1) * 128], qT[:, q0:q0 + TG], True, False, r=list(qk_res), w=[t["zb"]])

    def s_ln(i):
        t = T[i]
        zb, dl = t["zb"], t["dl"]
        e = W.nxt(W.e32, "e")
        p.act(lambda e_: e_.activation(e.t[:], zb.t[:], AF.Exp), r=[zb], w=[e])
        spb = W.nxt(W.spb, "s")
        t["spb"] = spb
        if t["diag"]:
            tt = W.nxt(W.t32, "t")
            p.act(lambda e_: e_.activation(tt.t[:], e.t[:], AF.Ln, bias=W.onecol.t[:, 0:1]), r=[e, W.onecol], w=[tt])
            p.dve(lambda e_: e_.tensor_tensor(
                spb.t[:], tt.t[:], W.masks.t[:, M_RS, dl + 384:dl + 384 + TG], ALU.mult), r=[tt, W.masks], w=[spb])
        else:
            p.act(lambda e_: e_.activation(spb.t[:], e.t[:], AF.Ln, bias=W.onecol.t[:, 0:1]), r=[e, W.onecol], w=[spb])

    def s_cs(i):
        t = T[i]
        pe_warm(p, c, W)
        zb, spb, idx, n = t["zb"], t["spb"], t["idx"], t["n"]
        mm(p, zb.t[:], negU.t[:], spb.t[:], False, idx == 0, r=[negU, spb, zb], w=[zb])
        acc = accst["acc"]
        if idx > 0:
            mm(p, zb.t[:], negones.t[:], acc.t[:], False, True, r=[negones, acc, zb], w=[zb])
        if idx < n - 1:
            if idx == 0:
                accst["acc"] = spb
            else:
                nacc = W.nxt(W.accs, "g")
                p.pool(lambda e_: e_.tensor_tensor(nacc.t[:], acc.t[:], spb.t[:], ALU.add), r=[spb, acc], w=[nacc])
                accst["acc"] = nacc

    def s_fin(i):
        t = T[i]
        zb, dl = t["zb"], t["dl"]
        wt = W.nxt(W.wbf, "w")
        t["wt"] = wt
        if t["diag"]:
            tt = W.nxt(W.t32, "t")
            p.dve(lambda e_: e_.tensor_tensor(tt.t[:], zb.t[:], W.masks.t[:, M_NEGS, dl + 384:dl + 384 + TG], ALU.add),
                  r=[zb, W.masks], w=[tt])
            p.act(lambda e_: e_.activation(wt.t[:], tt.t[:], AF.Exp), r=[tt], w=[wt])
        else:
            p.act(lambda e_: e_.activation(wt.t[:], zb.t[:], AF.Exp), r=[zb], w=[wt])

    def s_pv(i):
        t = T[i]
        ob, idx, n, q0 = t["ob"], t["idx"], t["n"], t["q0"]
        mm(p, ob.t[:, :], vt.t[:, t["kb"] * 65:t["kb"] * 65 + 128], t["wt"].t[:],
           idx == 0, idx == n - 1, r=[vt, t["wt"]] + ([] if idx == 0 else [ob]), w=[ob])
        if idx == n - 1:
            obf = W.nxt(W.obf, "o")
            p.act(lambda e_: e_.copy(obf.t[0:64, :], ob.t[0:64, :]), r=[ob], w=[obf])
            p.dma("sp", ot_dst[:, q0:q0 + TG], obf.t[0:64, :], r=[obf], w=[ot_res])

    run_pipeline(len(T), [(2, s_z), (1, s_ln), (0, s_cs), (-1, s_fin), (-2, s_pv)])


def norm_epilogue(p, c, W, ob, ot_dst_ap, ot_res):
    osb = W.nxt(W.osb, "o")
    p.dve(lambda e: e.tensor_copy(osb.t[0:65, :], ob.t[0:65, :]), r=[ob], w=[osb])
    p.dve(lambda e: e.reciprocal(W.rden.t[64:65, :], osb.t[64:65, :]), r=[osb], w=[W.rden])
    mm(p, W.misc.t[0:64, :], c.ones32.t[64:65, 0:64], W.rden.t[64:65, :], True, True, r=[c.ones32, W.rden], w=[W.misc])
    obf = W.obf[W.n["o"] % 2]
    p.dve(lambda e: e.tensor_tensor(obf.t[0:64, :], osb.t[0:64, :], W.misc.t[0:64, :], ALU.mult),
          r=[osb, W.misc], w=[obf])
    p.dma("sp", ot_dst_ap, obf.t[0:64, :], r=[obf], w=[ot_res])


def attn_win(p, c, W, *, S, kT, qT, qk_res, vt, rd, ot_dst, ot_res):
    NG = S // TG
    T = make_tiles(NG, W, lambda qg: [kb for kb in range(4 * qg + 3, 4 * qg - 17, -1) if kb >= 0])

    def s_z(i):
        t = T[i]
        t["zb"] = W.zb[i % len(W.zb)]
        kb, q0 = t["kb"], t["q0"]
        mm(p, t["zb"].t[:], kT[:, kb * 128:(kb + 1) * 128], qT[:, q0:q0 + TG], True, True, r=list(qk_res), w=[t["zb"]])

    def s_exp(i):
        t = T[i]
        zb = t["zb"]
        e = W.nxt(W.e32, "e")
        t["e"] = e
        p.act(lambda e_: e_.activation(e.t[:], zb.t[:], AF.Exp), r=[zb], w=[e])

    def s_mul(i):
        t = T[i]
        e, dl = t["e"], t["dl"]
        wt = W.nxt(W.wbf, "w")
        t["wt"] = wt
        p.dve(lambda e_: e_.tensor_tensor(wt.t[:], e.t[:], rd.t[:, dl + 384:dl + 384 + TG], ALU.mult), r=[e, rd], w=[wt])

    def s_pv(i):
        t = T[i]
        ob, idx, n, q0 = t["ob"], t["idx"], t["n"], t["q0"]
        mm(p, ob.t[:, :], vt.t[:, t["kb"] * 65:t["kb"] * 65 + 128], t["wt"].t[:],
           idx == 0, idx == n - 1, r=[vt, t["wt"]] + ([] if idx == 0 else [ob]), w=[ob])
        if idx == n - 1:
            norm_epilogue(p, c, W, ob, ot_dst[:, q0:q0 + TG], ot_res)

    run_pipeline(len(T), [(2, s_z), (1, s_exp), (0, s_mul), (-1, s_pv)])


HEAD_DIM = 64
NEG = -30000.0


def _mask_strips():
    k = np.arange(128)[:, None]
    x = np.arange(896)[None, :]
    d = x - 384 - k
    rs = (d >= 1).astype(np.float32)
    rc = (d >= 0).astype(np.float32)
    return np.ascontiguousarray(np.concatenate([rs, (rs - 1) * (-NEG), rc, (rc - 1) * (-NEG)], axis=1).astype(np.float32))


def _dilated_strip(head, n_heads=16):
    slope = 2.0 ** (-8.0 * (head + 1) / n_heads)
    k = np.arange(128)[:, None]
    x = np.arange(2944)[None, :]
    d = (x - 384 - k).astype(np.int64)
    mult = ((d >= 0) & (d <= 128)).astype(np.float64) + ((d >= 0) & (d <= 512) & (d % 4 == 0)) \
        + ((d >= 0) & (d <= 2048) & (d % 16 == 0))
    g = mult * np.exp(-slope * np.maximum(d, 0).astype(np.float64))
    return np.ascontiguousarray(g.astype(np.float32))


def _aconst():
    kp = np.arange(128)[:, None]
    k = np.arange(128)[None, :]
    return np.ascontiguousarray(np.concatenate([-(kp >= k).astype(np.float32), -np.ones((128, 128), np.float32),
                                                (kp <= k).astype(np.float32)], axis=1))


ARENA_ELEMS = 41344
K128 = bool(int(_os.environ.get("K128", "1")))
KQ = 128 if K128 else 64


def build_ab0(S=SEQ, D=D_MODEL):
    KC = D // 128
    NB = S // 128
    nc = bass.Bass("TRN2", target_bir_lowering=False)
    hT_all = nc.dram_tensor("hT_all", [D, S], BF16, kind="ExternalInput").ap()
    w_fm = nc.dram_tensor("w_fm", [D, 1024], F32, kind="ExternalInput").ap()
    w_tm = nc.dram_tensor("w_tm", [D, 512], F32, kind="ExternalInput").ap()
    cst = nc.dram_tensor("consts", [128, 384], F32, kind="ExternalInput").ap()
    acst = nc.dram_tensor("aconst", [128, 384], F32, kind="ExternalInput").ap()
    masks = nc.dram_tensor("masks", [128, 4 * 896], F32, kind="ExternalInput").ap()
    rdd = nc.dram_tensor("rd", [4, 128, 2944], F32, kind="ExternalInput").ap()
    OT = nc.dram_tensor("OT", [512, S], BF16, kind="ExternalOutput").ap()
    fm = nc.dram_tensor("fm_scr", [1024, S], BF16).ap()
    vd = nc.dram_tensor("v_scr", [8, 128, NB * 65], BF16).ap()
    p = Prog(nc)
    c = make_common(p, cst)
    arena = p.sb("arena", [128, ARENA_ELEMS], BF16)
    a32 = p.sb("ac32", [128, 256], F32)
    negU = p.sb("negU", [128, 128], BF16)
    negones = p.sb("negones", [128, 128], BF16)
    p.dma("sp", a32.t[:], acst[:, 0:256], w=[a32])
    p.dve(lambda e: e.tensor_copy(negU.t[:], a32.t[:, 0:128]), r=[a32], w=[negU])
    p.dve(lambda e: e.tensor_copy(negones.t[:], a32.t[:, 128:256]), r=[a32], w=[negones])
    chunks = [dict(w=128, col=i * 128, row=i * 128, scale=(0.125 if (i // 2) % 2 == 0 else 1.0)) for i in range(8)]
    fm_res, v_res = p.dram("fm"), p.dram("v")
    phase_a(p, c, S=S, D=D, hT_all=hT_all, hT_res=p.dram("hT_all"), w_fm=w_fm, w_tm=w_tm, NFM=1024, NTM=512,
            fm_chunks=chunks, fm_dst=fm, fm_res=fm_res, v_dst=vd, v_res=v_res, nvh=8, arena=arena)
    p.barrier()
    msb = load_masks(p, masks, 4)
    W = AttnWork(p, c, msb)
    rds = [p.sb("rd%d" % i, [128, 2944], F32) for i in range(2)]
    sets = []
    off = 0
    for i in range(2):
        qv = Res("qT%d" % i, arena.t[:, off:off + S]); off += S
        kv = Res("kT%d" % i, arena.t[:, off:off + S]); off += S
        if K128:
            p.pool(lambda e, qv=qv: e.memset(qv.t[64:128, :], 0.0), w=[qv])
            p.pool(lambda e, kv=kv: e.memset(kv.t[64:128, :], 0.0), w=[kv])
        vv = Res("vt%d" % i, arena.t[:, off:off + NB * 65 + 64]); off += NB * 65 + 64
        sets.append((qv, kv, vv))
    assert off <= ARENA_ELEMS
    ot_res = p.dram("OT")
    order = [0, 4, 1, 5, 2, 6, 3, 7]

    def loads(k):
        hh = order[k]
        qv, kv, vv = sets[k % 2]
        e = hh % 4
        if hh < 4:
            qrow, krow = e * 64, 256 + e * 64
        else:
            qrow, krow = 512 + e * 64, 768 + e * 64
        p.dma("sp", qv.t[0:64, :], fm[qrow:qrow + 64, :], r=[fm_res], w=[qv])
        p.dma("sp", kv.t[0:64, :], fm[krow:krow + 64, :], r=[fm_res], w=[kv])
        p.dma("sp", vv.t[:, 0:NB * 65], vd[hh], r=[v_res], w=[vv])
        if hh >= 4:
            p.dma("sp", rds[k % 2].t[:], rdd[e], w=[rds[k % 2]])

    loads(0)
    for k, hh in enumerate(order):
        if k + 1 < len(order):
            loads(k + 1)
        qv, kv, vv = sets[k % 2]
        if hh < 4:
            attn_sb(p, c, W, S=S, kT=kv.t[0:KQ, :], qT=qv.t[0:KQ, :], qk_res=(qv, kv), vt=vv, negU=negU, negones=negones,
                    ot_dst=OT[hh * 64:(hh + 1) * 64, :], ot_res=ot_res)
        else:
            attn_win(p, c, W, S=S, kT=kv.t[0:KQ, :], qT=qv.t[0:KQ, :], qk_res=(qv, kv), vt=vv, rd=rds[k % 2],
                     ot_dst=OT[hh * 64:(hh + 1) * 64, :], ot_res=ot_res)
    p.mark_outputs(ot_res)
    p.emit()
    return nc


def build_cq(p, c, W, cpos, ci, qg, slot):
    for j in range(TG // 128):
        blk = qg * (TG // 128) + j
        dg = W.dg[W.n["d"] % 2]
        W.n["d"] += 1
        p.dve(lambda e, dg=dg, blk=blk: e.tensor_scalar(
            dg.t[:], c.ident.t[:], cpos.t[:, blk, ci:ci + 1], -1.0, ALU.mult, ALU.mult), r=[c.ident, cpos], w=[dg])
        mm(p, W.misc.t[:, j * 128:(j + 1) * 128], c.ones32.t[:], dg.t[:], True, True, r=[c.ones32, dg], w=[W.misc])
    cq = W.cqb[slot]
    p.act(lambda e: e.copy(cq.t[:], W.misc.t[:]), r=[W.misc], w=[cq])
    return cq


def attn_fox(p, c, W, *, S, kT, qT, qk_res, vt, cpos, ci, ot_dst, ot_res, M_NEGC=3):
    NG = S // TG
    T = make_tiles(NG, W, lambda qg: list(range(4 * qg + 3, -1, -1)))
    cqs = {}

    def s_z(i):
        t = T[i]
        if t["idx"] == 0:
            cqs[t["qg"]] = build_cq(p, c, W, cpos, ci, t["qg"], t["qg"] % 2)
        t["zb"] = W.zb[i % len(W.zb)]
        kb, q0 = t["kb"], t["q0"]
        mm(p, t["zb"].t[:], kT[:, kb * 128:(kb + 1) * 128], qT[:, q0:q0 + TG], True, True, r=list(qk_res), w=[t["zb"]])

    def s_add(i):
        t = T[i]
        zb, dl = t["zb"], t["dl"]
        cq = cqs[t["qg"]]
        tt = W.nxt(W.t32, "t")
        t["t"] = tt
        p.dve(lambda e: e.tensor_tensor(tt.t[:], zb.t[:], cq.t[:], ALU.add), r=[zb, cq], w=[tt])
        if t["diag"]:
            p.pool(lambda e: e.tensor_tensor(
                tt.t[:], tt.t[:], W.masks.t[:, M_NEGC, dl + 384:dl + 384 + TG], ALU.add), r=[tt, W.masks], w=[tt])

    def s_exp(i):
        t = T[i]
        tt, kb = t["t"], t["kb"]
        wt = W.nxt(W.wbf, "w")
        t["wt"] = wt
        p.act(lambda e: e.activation(wt.t[:], tt.t[:], AF.Exp, bias=cpos.t[:, kb, ci:ci + 1]), r=[tt, cpos], w=[wt])

    def s_pv(i):
        t = T[i]
        ob, idx, n, q0 = t["ob"], t["idx"], t["n"], t["q0"]
        mm(p, ob.t[:, :], vt.t[:, t["kb"] * 65:t["kb"] * 65 + 128], t["wt"].t[:],
           idx == 0, idx == n - 1, r=[vt, t["wt"]] + ([] if idx == 0 else [ob]), w=[ob])
        if idx == n - 1:
            norm_epilogue(p, c, W, ob, ot_dst[:, q0:q0 + TG], ot_res)

    run_pipeline(len(T), [(2, s_z), (1, s_add), (0, s_exp), (-1, s_pv)])


def attn_ssd(p, c, W, *, S, BT, CT, bc_res, xdt, xdt_flat, cpos, fm, fm_res, zrow, xrow, svec, ygs, ot_dst, ot_res,
             G=4, M_NEGC=3):
    NG = S // TG
    zbs = [c.banks[0], c.banks[1]]
    obs = [c.banks[2], c.banks[3], c.banks[4], c.banks[5]]

    for qg in range(NG):
        q0 = qg * TG
        cqs = [build_cq(p, c, W, cpos, 4 + e, qg, 2 + e) for e in range(4)]
        kbs = list(range(4 * qg + 3, -1, -1))
        n = len(kbs)
        T = [dict(kb=kb, diag=(kb >= 4 * qg), dl=q0 - kb * 128) for kb in kbs]

        def s_z(i):
            t = T[i]
            t["zb"] = zbs[i % 2]
            kb = t["kb"]
            mm(p, t["zb"].t[:], BT[:, kb * 128:(kb + 1) * 128], CT[:, q0:q0 + TG], True, True, r=list(bc_res), w=[t["zb"]])

        def s_dec(i):
            t = T[i]
            kb, dl = t["kb"], t["dl"]
            t["dec"] = []
            for e in range(4):
                if t["diag"]:
                    tt = W.nxt(W.t32, "t")
                    p.pool(lambda e_, tt=tt, e=e: e_.tensor_tensor(
                        tt.t[:], cqs[e].t[:], W.masks.t[:, M_NEGC, dl + 384:dl + 384 + TG], ALU.add),
                        r=[cqs[e], W.masks], w=[tt])
                    src = tt
                else:
                    src = cqs[e]
                dec = W.nxt(W.dec, "d2")
                t["dec"].append(dec)
                p.act(lambda e_, dec=dec, src=src, e=e: e_.activation(
                    dec.t[:], src.t[:], AF.Exp, bias=cpos.t[:, kb, 4 + e:5 + e]), r=[src, cpos], w=[dec])

        def s_mul(i):
            t = T[i]
            zb = t["zb"]
            t["wt"] = []
            for e in range(4):
                wt = W.nxt(W.wts, "w2")
                t["wt"].append(wt)
                dec = t["dec"][e]
                p.dve(lambda e_, wt=wt, dec=dec: e_.tensor_tensor(wt.t[:], zb.t[:], dec.t[:], ALU.mult),
                      r=[zb, dec], w=[wt])

        def s_pv(i):
            t = T[i]
            for e in range(4):
                wt = t["wt"][e]
                o_ = (t["kb"] * 4 + e) * 64
                mm(p, obs[e].t[:, :], xdt_flat[:, o_:o_ + 128], wt.t[:], i == 0, i == n - 1,
                   r=[xdt, wt] + ([] if i == 0 else [obs[e]]), w=[obs[e]])

        run_pipeline(n, [(1, s_z), (1, s_dec), (0, s_mul), (-1, s_pv)])
        for e in range(4):
            zs = W.nxt(W.wbf, "w")
            xsl = W.nxt(W.wbf, "w")
            p.dma("sp", zs.t[0:64, :], fm[zrow + 64 * e:zrow + 64 * e + 64, q0:q0 + TG], r=[fm_res], w=[zs])
            p.dma("sp", xsl.t[0:64, :], fm[xrow + 64 * e:xrow + 64 * e + 64, q0:q0 + TG], r=[fm_res], w=[xsl])
            y = W.nxt(W.osb, "o")
            p.dve(lambda e_, y=y, xsl=xsl, e=e: e_.scalar_tensor_tensor(
                y.t[0:64, :], xsl.t[0:64, :], svec.t[0:64, 12 + e:13 + e], obs[e].t[0:64, :], ALU.mult, ALU.add),
                r=[xsl, svec, obs[e]], w=[y])
            sz = W.nxt(W.t32, "t")
            p.act(lambda e_, sz=sz, zs=zs: e_.activation(sz.t[0:64, :], zs.t[0:64, :], AF.Silu), r=[zs], w=[sz])
            yg = ygs[e]
            p.dve(lambda e_, yg=yg, y=y, sz=sz: e_.tensor_tensor(yg.t[0:64, :], y.t[0:64, :], sz.t[0:64, :], ALU.mult),
                  r=[y, sz], w=[yg])
            sq = W.nxt(W.spb, "s")
            p.act(lambda e_, sq=sq, yg=yg: e_.activation(sq.t[0:64, :], yg.t[0:64, :], AF.Square), r=[yg], w=[sq])
            mm(p, W.misc.t[0:64, :], c.ones_bf.t[0:64, 0:64], sq.t[0:64, :], e == 0, e == 3,
               r=[c.ones_bf, sq] + ([] if e == 0 else [W.misc]), w=[W.misc])
        p.dve(lambda e_: e_.tensor_scalar(W.rden.t[0:64, :], W.misc.t[0:64, :], 1.0 / (64 * G), None, ALU.mult),
              r=[W.misc], w=[W.rden])
        p.dve(lambda e_: e_.tensor_scalar(W.rden.t[0:64, :], W.rden.t[0:64, :], EPS, None, ALU.add),
              r=[W.rden], w=[W.rden])
        p.act(lambda e_: e_.activation(W.rden.t[0:64, :], W.rden.t[0:64, :], AF.Sqrt), r=[W.rden], w=[W.rden])
        p.dve(lambda e_: e_.reciprocal(W.rden.t[0:64, :], W.rden.t[0:64, :]), r=[W.rden], w=[W.rden])
        for e in range(4):
            obf = W.nxt(W.obf, "o")
            p.dve(lambda e_, obf=obf, e=e: e_.scalar_tensor_tensor(
                obf.t[0:64, :], ygs[e].t[0:64, :], svec.t[0:64, 16 + e:17 + e], W.rden.t[0:64, :], ALU.mult, ALU.mult),
                r=[ygs[e], svec, W.rden], w=[obf])
            p.dma("sp", ot_dst[64 * e:64 * e + 64, q0:q0 + TG], obf.t[0:64, :], r=[obf], w=[ot_res])


def build_ab1(S=SEQ, D=D_MODEL):
    KC = D // 128
    NB = S // 128
    nc = bass.Bass("TRN2", target_bir_lowering=False)
    hT_all = nc.dram_tensor("hT_all", [D, S], BF16, kind="ExternalInput").ap()
    w_fm = nc.dram_tensor("w_fm", [D, 1280], F32, kind="ExternalInput").ap()
    w_tm = nc.dram_tensor("w_tm", [D, 264], F32, kind="ExternalInput").ap()
    cst = nc.dram_tensor("consts", [128, 384], F32, kind="ExternalInput").ap()
    acst = nc.dram_tensor("aconst", [128, 384], F32, kind="ExternalInput").ap()
    masks = nc.dram_tensor("masks", [128, 4 * 896], F32, kind="ExternalInput").ap()
    convd = nc.dram_tensor("convw", [128, 20], F32, kind="ExternalInput").ap()
    svd = nc.dram_tensor("svec", [128, 32], F32, kind="ExternalInput").ap()
    OT = nc.dram_tensor("OT", [512, S], BF16, kind="ExternalOutput").ap()
    fm = nc.dram_tensor("fm_scr", [1280, S], BF16).ap()
    vd = nc.dram_tensor("v_scr", [4, 128, NB * 65], BF16).ap()
    p = Prog(nc)
    c = make_common(p, cst)
    arena = p.sb("arena", [128, ARENA_ELEMS], BF16)
    tri32 = p.sb("tri32", [128, 128], F32)
    p.dma("sp", tri32.t[:], acst[:, 256:384], w=[tri32])
    ident_bf = p.sb("ident_bf", [128, 128], BF16)
    p.dve(lambda e: e.tensor_copy(ident_bf.t[:], c.ident.t[:]), r=[c.ident], w=[ident_bf])
    convw = p.sb("convw_sb", [128, 4, 5], F32)
    p.dma("sp", convw.t[:], convd.rearrange("p (a b) -> p a b", a=4), w=[convw])
    svec = p.sb("svec_sb", [128, 32], F32)
    p.dma("sp", svec.t[:], svd, w=[svec])
    fd = p.sb("fd", [128, NB, 8], F32)
    chunks = [dict(w=128, col=0, row=0, scale=0.125), dict(w=128, col=128, row=128, scale=0.125),
              dict(w=128, col=256, row=256), dict(w=128, col=384, row=384),
              dict(w=128, col=512, row=512, conv=0), dict(w=128, col=640, row=640, conv=1)]
    chunks += [dict(w=128, col=768 + 128 * j, row=768 + 128 * j) for j in range(2)]
    chunks += [dict(w=128, col=1024 + 128 * j, row=1024 + 128 * j, conv=2 + j) for j in range(2)]
    fm_res, v_res = p.dram("fm"), p.dram("v")
    phase_a(p, c, S=S, D=D, hT_all=hT_all, hT_res=p.dram("hT_all"), w_fm=w_fm, w_tm=w_tm, NFM=1280, NTM=264,
            fm_chunks=chunks, fm_dst=fm, fm_res=fm_res, v_dst=vd, v_res=v_res, nvh=4, arena=arena,
            fd=fd, nfd=8, convw=convw)
    p.barrier()
    msb = load_masks(p, masks, 4)
    W = AttnWork(p, c, msb, nt32=4)
    W.dg = [p.sb("w_dg%d" % j, [128, 128], F32) for j in range(2)]
    W.cqb = [p.sb("w_cqb%d" % j, [128, TG], F32) for j in range(6)]
    W.dec = [p.sb("w_dec%d" % j, [128, TG], BF16) for j in range(8)]
    W.wts = [p.sb("w_wts%d" % j, [128, TG], BF16) for j in range(8)]
    W.n["d2"] = 0
    W.n["w2"] = 0
    t1 = p.sb("s_t1", [128, NB, 8], F32)
    l8 = p.sb("s_l8", [128, NB, 8], F32)
    vals = p.sb("s_vals", [128, NB, 8], F32)
    cpos = p.sb("s_cpos", [128, NB, 8], F32)
    vsum = p.sb("s_vsum", [128, 8], F32)
    ea = p.sb("s_ea", [128, 4], F32)
    p.dve(lambda e: e.tensor_tensor(t1.t[:], fd.t[:], svec.t[:, 0:8].unsqueeze(1).to_broadcast([128, NB, 8]), ALU.add),
          r=[fd, svec], w=[t1])
    p.act(lambda e: e.activation(t1.t[:, :, 0:4], t1.t[:, :, 0:4], AF.Exp, scale=-1.0), r=[t1], w=[t1])
    p.act(lambda e: e.activation(t1.t[:, :, 4:8], t1.t[:, :, 4:8], AF.Exp), r=[t1], w=[t1])
    p.act(lambda e: e.activation(l8.t[:], t1.t[:], AF.Ln, bias=W.onecol.t[:, 0:1]), r=[t1, W.onecol], w=[l8])
    p.act(lambda e: e.activation(ea.t[:], svec.t[:, 8:12], AF.Exp), r=[svec], w=[ea])
    p.dve(lambda e: e.tensor_copy(vals.t[:, :, 0:4], l8.t[:, :, 0:4]), r=[l8], w=[vals])
    p.dve(lambda e: e.tensor_tensor(vals.t[:, :, 4:8], l8.t[:, :, 4:8],
                                    ea.t[:].unsqueeze(1).to_broadcast([128, NB, 4]), ALU.mult), r=[l8, ea], w=[vals])
    for blk in range(NB):
        mm(p, W.misc.t[:, 0:8], tri32.t[:], vals.t[:, blk, :], True, blk == 0, r=[tri32, vals], w=[W.misc])
        if blk > 0:
            mm(p, W.misc.t[:, 0:8], c.ones32.t[:], vsum.t[:], False, True, r=[c.ones32, vsum, W.misc], w=[W.misc])
        p.dve(lambda e, blk=blk: e.tensor_copy(cpos.t[:, blk, :], W.misc.t[:, 0:8]), r=[W.misc], w=[cpos])
        if blk == 0:
            p.dve(lambda e, blk=blk: e.tensor_copy(vsum.t[:], vals.t[:, blk, :]), r=[vals], w=[vsum])
        elif blk < NB - 1:
            p.dve(lambda e, blk=blk: e.tensor_tensor(vsum.t[:], vsum.t[:], vals.t[:, blk, :], ALU.add),
                  r=[vals, vsum], w=[vsum])
    sets = []
    off = 0
    for i in range(2):
        qv = Res("qT%d" % i, arena.t[:, off:off + S]); off += S
        kv = Res("kT%d" % i, arena.t[:, off:off + S]); off += S
        if K128:
            p.pool(lambda e, qv=qv: e.memset(qv.t[64:128, :], 0.0), w=[qv])
            p.pool(lambda e, kv=kv: e.memset(kv.t[64:128, :], 0.0), w=[kv])
        vv = Res("vt%d" % i, arena.t[:, off:off + NB * 65 + 64]); off += NB * 65 + 64
        sets.append((qv, kv, vv))
    assert off <= ARENA_ELEMS
    ot_res = p.dram("OT")

    def loads(e):
        qv, kv, vv = sets[e % 2]
        p.dma("sp", qv.t[0:64, :], fm[e * 64:e * 64 + 64, :], r=[fm_res], w=[qv])
        p.dma("sp", kv.t[0:64, :], fm[256 + e * 64:256 + e * 64 + 64, :], r=[fm_res], w=[kv])
        p.dma("sp", vv.t[:, 0:NB * 65], vd[e], r=[v_res], w=[vv])

    loads(0)
    for e in range(4):
        if e + 1 < 4:
            loads(e + 1)
        qv, kv, vv = sets[e % 2]
        attn_fox(p, c, W, S=S, kT=kv.t[0:KQ, :], qT=qv.t[0:KQ, :], qk_res=(qv, kv), vt=vv, cpos=cpos, ci=e,
                 ot_dst=OT[e * 64:(e + 1) * 64, :], ot_res=ot_res)
    p.barrier()
    BT = Res("BT", arena.t[:, 0:S])
    CT = Res("CT", arena.t[:, S:2 * S])
    xdt = Res("xdt", arena.t[:, 2 * S:2 * S + NB * 256].rearrange("p (b e d) -> p b e d", e=4, d=64))
    xdt_flat = arena.t[:, 2 * S:2 * S + NB * 256 + 64]
    xst = Res("xst", arena.t[0:64, 2 * S + NB * 256 + 64:3 * S + NB * 256 + 64])
    assert 3 * S + NB * 256 + 64 <= ARENA_ELEMS
    p.dma("sp", BT.t, fm[512:640, :], r=[fm_res], w=[BT])
    p.dma("sp", CT.t, fm[640:768, :], r=[fm_res], w=[CT])
    mbf = W.misc.t[:].bitcast(BF16)
    for e in range(4):
        p.dma("sp", xst.t, fm[1024 + 64 * e:1024 + 64 * e + 64, :], r=[fm_res], w=[xst])
        for blk in range(NB):
            p.pe(lambda e_, blk=blk: e_.transpose(mbf[:, 0:64], xst.t[:, blk * 128:(blk + 1) * 128], ident_bf.t[0:64, 0:64]),
                 r=[xst, ident_bf], w=[W.misc])
            p.dve(lambda e_, blk=blk, e=e: e_.tensor_scalar(
                xdt.t[:, blk, e, :], mbf[:, 0:64], l8.t[:, blk, 4 + e:5 + e], None, ALU.mult), r=[W.misc, l8], w=[xdt])
    ygs = [p.sb("w_yg%d" % i, [128, TG], F32) for i in range(4)]
    attn_ssd(p, c, W, S=S, BT=BT.t, CT=CT.t, bc_res=(BT, CT), xdt=xdt, xdt_flat=xdt_flat, cpos=cpos, fm=fm, fm_res=fm_res,
             zrow=768, xrow=1024, svec=svec, ygs=ygs, ot_dst=OT[256:512, :], ot_res=ot_res)
    p.mark_outputs(ot_res)
    p.emit()
    return nc


WCAST = (("w_out", 2048, 2048), ("w_up", 2048, 8192), ("w_down", 8192, 2048))


def build_p0(NT, D, wcast=True):
    KC = D // 128
    nc = bass.Bass("TRN2", target_bir_lowering=False)
    x = nc.dram_tensor("x", [NT, D], F32, kind="ExternalInput").ap()
    cst = nc.dram_tensor("consts", [128, 384], F32, kind="ExternalInput").ap()
    nvd = nc.dram_tensor("nv", [128, 8 * KC], F32, kind="ExternalInput").ap()
    xT = nc.dram_tensor("xT", [D, NT], F32, kind="ExternalOutput").ap()
    hT = nc.dram_tensor("hT", [D, NT], BF16, kind="ExternalOutput").ap()
    p = Prog(nc)
    c = make_common(p, cst)
    nv = p.sb("nv_sb", [128, 8 * KC], F32)
    p.dma("sp", nv.t[:], nvd, w=[nv])
    phase_c(p, c, D=D, DFF=4 * D, F=D, NT=NT, OT_src=None, OT_res=None, xT=xT, xT_res=p.dram("xT"), w_out=None,
            w_up=None, w_down=None, nv=nv, v_post=0, v_pre=0, v_mpost=0, v_next=0,
            hT_dst=lambda t0: hT[:, t0:t0 + TG], hT_res=p.dram("hT"),
            x_src=lambda t0: x[t0:t0 + TG, :], x_res=p.dram("x"), nwb=1)
    p.mark_outputs(p.dram("xT"))
    p.mark_outputs(p.dram("hT"))
    if wcast:
        stg = [p.sb("wc_stg%d" % i, [128, 8192], BF16) for i in range(2)]
        k = 0
        for l in range(2):
            for name, rows, cols in WCAST:
                rs = rows // NCORES
                src = nc.dram_tensor("%s%d_f32" % (name, l), [rs, cols], F32, kind="ExternalInput").ap()
                dst = nc.dram_tensor("%s%d_bf" % (name, l), [rs, cols], BF16, kind="ExternalOutput").ap()
                sv = src.rearrange("(a p) c -> p a c", p=128)
                dv = dst.rearrange("(a p) c -> p a c", p=128)
                na = rs // 128
                per = max(1, 8192 // cols)
                for a0 in range(0, na, per):
                    a1 = min(na, a0 + per)
                    t = stg[k % 2]
                    k += 1
                    tv = t.t[:, 0:(a1 - a0) * cols].rearrange("p (a c) -> p a c", c=cols)
                    p.dma("pool", tv, sv[:, a0:a1, :], w=[t])
                    p.dma("sp", dv[:, a0:a1, :], tv, r=[t], w=[p.dram("wcast")], out=True)
    p.emit()
    return nc


def build_c(NT, D, DFF, layer, last, wbf=True):
    KC = D // 128
    nc = bass.Bass("TRN2", target_bir_lowering=False)
    OT = nc.dram_tensor("OT", [D, NT], BF16, kind="ExternalInput").ap()
    xT = nc.dram_tensor("xT", [D, NT], F32, kind="ExternalInput").ap()
    cst = nc.dram_tensor("consts", [128, 384], F32, kind="ExternalInput").ap()
    nvd = nc.dram_tensor("nv", [128, 8 * KC], F32, kind="ExternalInput").ap()
    WDT = BF16 if wbf else F32
    w_out = nc.dram_tensor("w_out", [D, D], WDT, kind="ExternalInput").ap()
    w_up = nc.dram_tensor("w_up", [D, DFF], WDT, kind="ExternalInput").ap()
    w_down = nc.dram_tensor("w_down", [DFF, D], WDT, kind="ExternalInput").ap()
    p = Prog(nc)
    c = make_common(p, cst)
    nv = p.sb("nv_sb", [128, 8 * KC], F32)
    p.dma("sp", nv.t[:], nvd, w=[nv])
    V = lambda i: i * KC
    b = 4 * layer
    kw = dict(D=D, DFF=DFF, F=D, NT=NT, OT_src=lambda t0: OT[:, t0:t0 + TG], OT_res=p.dram("OT"),
              xT=xT, xT_res=p.dram("xT"), w_out=w_out, w_up=w_up, w_down=w_down, nv=nv,
              v_post=V(b + 1), v_pre=V(b + 2), v_mpost=V(b + 3), v_next=V((b + 4) % 8),
              wq="pool", nwb=5)
    if last:
        out = nc.dram_tensor("out", [NT, D], F32, kind="ExternalOutput").ap()
        phase_c(p, c, out_dst=lambda t0: out[t0:t0 + TG, :], out_res=p.dram("out"), **kw)
    else:
        xTo = nc.dram_tensor("xTo", [D, NT], F32, kind="ExternalOutput").ap()
        hT = nc.dram_tensor("hT", [D, NT], BF16, kind="ExternalOutput").ap()
        phase_c(p, c, hT_dst=lambda t0: hT[:, t0:t0 + TG], hT_res=p.dram("hT"),
                xT_out=xTo, xT_out_res=p.dram("xTo"), **kw)
        p.mark_outputs(p.dram("xTo"))
        p.mark_outputs(p.dram("hT"))
    p.emit()
    return nc


def _c32(a):
    return np.ascontiguousarray(np.asarray(a, np.float32))


def kernel(x, mix_norm_pre, mix_norm_post, mlp_norm_pre, mlp_norm_post, ab_w_in, ab_w_out,
           cd_w_in, cd_b_f, cd_conv_w, cd_conv_b, cd_dt_bias, cd_a_log, cd_d_skip,
           cd_gate_norm, cd_w_out, mlp_w_up, mlp_w_down):
    x = np.asarray(x, np.float32)
    B, S, D = x.shape
    G = NCORES // B
    NT = S // G
    DFF = mlp_w_up.shape[-1]
    cores = list(range(NCORES))
    xs = x.reshape(NCORES, NT, D)
    vecs = []
    for l in range(2):
        vecs += [mix_norm_pre[l], mix_norm_post[l], mlp_norm_pre[l], mlp_norm_post[l]]
    nv = _nv_layout(vecs)
    consts = _consts()
    aconst = _aconst()
    masks = _mask_strips()

    def gather_h(res, key):
        return [np.ascontiguousarray(np.concatenate([res[b * G + g][key] for g in range(G)], axis=1)) for b in range(B)]

    def scatter_ot(res):
        outs = []
        for b in range(B):
            full = np.concatenate([res[b * G + g]["OT"][0:256] for g in range(G)]
                                  + [res[b * G + g]["OT"][256:512] for g in range(G)], axis=0)
            for g in range(G):
                outs.append(np.ascontiguousarray(full[:, g * NT:(g + 1) * NT]))
        return outs

    wsrc = {"w_out0": ab_w_out[0], "w_out1": cd_w_out[0], "w_up0": mlp_w_up[0], "w_up1": mlp_w_up[1],
            "w_down0": mlp_w_down[0], "w_down1": mlp_w_down[1]}
    im = []
    for i in cores:
        d = dict(x=np.ascontiguousarray(xs[i]), consts=consts, nv=nv)
        for k, wfull in wsrc.items():
            rs = wfull.shape[0] // NCORES
            d[k + "_f32"] = _c32(wfull[i * rs:(i + 1) * rs])
        im.append(d)
    r1 = run_bass_kernel_spmd(build_p0(NT, D), im, core_ids=cores).results
    wbf = {k: np.ascontiguousarray(np.concatenate([r1[i][k + "_bf"] for i in cores], axis=0)) for k in wsrc}
    xT = [r1[i]["xT"] for i in cores]
    hT_all = gather_h(r1, "hT")
    W0 = np.asarray(ab_w_in[0], np.float32)
    sec = [W0[:, i * 1024:(i + 1) * 1024] for i in range(6)]
    im = []
    for i in cores:
        b, g = divmod(i, G)
        sl = slice(256 * g, 256 * g + 256)
        im.append(dict(hT_all=hT_all[b], consts=consts, aconst=aconst, masks=masks,
                       w_fm=np.ascontiguousarray(np.concatenate([sec[0][:, sl], sec[1][:, sl], sec[3][:, sl], sec[4][:, sl]], axis=1)),
                       w_tm=np.ascontiguousarray(np.concatenate([sec[2][:, sl], sec[5][:, sl]], axis=1)),
                       rd=np.stack([_dilated_strip(4 * g + e) for e in range(4)])))
    r2 = run_bass_kernel_spmd(build_ab0(S, D), im, core_ids=cores).results
    ot = scatter_ot(r2)
    r3 = run_bass_kernel_spmd(build_c(NT, D, DFF, 0, False),
                              [dict(OT=ot[i], xT=xT[i], consts=consts, nv=nv, w_out=wbf["w_out0"],
                                    w_up=wbf["w_up0"], w_down=wbf["w_down0"]) for i in cores],
                              core_ids=cores).results
    xT = [r3[i]["xTo"] for i in cores]
    hT_all = gather_h(r3, "hT")
    W1 = np.asarray(cd_w_in[0], np.float32)
    qc, kc, vc = W1[:, 0:1024], W1[:, 1024:2048], W1[:, 2048:3072]
    fr, zz = W1[:, 3072:3088], W1[:, 3088:4112]
    xsw, Bw, Cw, dtw = W1[:, 4112:5136], W1[:, 5136:5648], W1[:, 5648:6160], W1[:, 6160:6176]
    cw = np.asarray(cd_conv_w[0], np.float32)
    cb = np.asarray(cd_conv_b[0], np.float32)
    im = []
    for i in cores:
        b, g = divmod(i, G)
        sl = slice(256 * g, 256 * g + 256)
        s128 = slice(128 * g, 128 * g + 128)
        s4 = slice(4 * g, 4 * g + 4)
        convw = np.zeros((128, 4, 5), np.float32)

        def cwl(ch0, n):
            return np.concatenate([cw[:, ch0:ch0 + n].T, cb[ch0:ch0 + n, None]], axis=1)
        convw[:, 0] = cwl(1024 + 128 * g, 128)
        convw[:, 1] = cwl(1536 + 128 * g, 128)
        svec = np.zeros((128, 32), np.float32)
        svec[:, 0:4] = np.asarray(cd_b_f[0], np.float32)[s4]
        svec[:, 4:8] = np.asarray(cd_dt_bias[0], np.float32)[s4]
        svec[:, 8:12] = np.asarray(cd_a_log[0], np.float32)[s4]
        svec[:, 12:16] = np.asarray(cd_d_skip[0], np.float32)[s4]
        for j in range(2):
            convw[:, 2 + j] = cwl(256 * g + 128 * j, 128)
        for e in range(4):
            svec[0:64, 16 + e] = np.asarray(cd_gate_norm[0], np.float32)[256 * g + 64 * e:256 * g + 64 * e + 64]
        im.append(dict(hT_all=hT_all[b], consts=consts, aconst=aconst, masks=masks,
                       w_fm=np.ascontiguousarray(np.concatenate([qc[:, sl], kc[:, sl], Bw[:, s128], Cw[:, s128],
                                                                 zz[:, sl], xsw[:, sl]], axis=1)),
                       w_tm=np.ascontiguousarray(np.concatenate([vc[:, sl], fr[:, s4], dtw[:, s4]], axis=1)),
                       convw=np.ascontiguousarray(convw.reshape(128, 20)), svec=svec))
    r4 = run_bass_kernel_spmd(build_ab1(S, D), im, core_ids=cores).results
    ot = scatter_ot(r4)
    r5 = run_bass_kernel_spmd(build_c(NT, D, DFF, 1, True),
                              [dict(OT=ot[i], xT=xT[i], consts=consts, nv=nv, w_out=wbf["w_out1"],
                                    w_up=wbf["w_up1"], w_down=wbf["w_down1"]) for i in cores],
                              core_ids=cores).results
    out = np.stack([np.asarray(r5[i]["out"], np.float32) for i in cores], axis=0)
    return out.reshape(B, S, D)
```

```python
import os as _os
import numpy as np
import concourse.bass as bass
import concourse.mybir as mybir
from concourse.bass_utils import run_bass_kernel_spmd

F32 = mybir.dt.float32
BF16 = mybir.dt.bfloat16
AF = mybir.ActivationFunctionType
ALU = mybir.AluOpType
AX = mybir.AxisListType

ENGS = ("pe", "act", "dve", "pool", "sp")
SAME_ENGINE_SYNC = True


class Res:
    __slots__ = ("name", "t", "lw", "rd", "excl")

    def __init__(self, name, t=None, excl=False):
        self.name = name
        self.t = t
        self.excl = excl
        self.lw = None
        self.rd = []


class Op:
    __slots__ = ("eng", "fn", "deps", "dma", "chan", "cval", "signal", "sval", "out", "wres", "hard")

    def __init__(self, eng, fn, dma=False):
        self.eng = eng
        self.fn = fn
        self.deps = set()
        self.hard = set()
        self.dma = dma
        self.chan = None
        self.cval = 0
        self.signal = False
        self.sval = 0
        self.out = False


class Prog:
    def __init__(self, nc, n_dma_chan=24):
        self.nc = nc
        self.ops = []
        self.ctx = []
        self.drams = {}
        self.n_dma_chan = n_dma_chan
        self.chan_last = {}
        self.chan_cnt = {}
        self.rr = 0
        self.rrq = {}
        self._names = 0
        self.bar = set()
        self.last_eng = {}

    def _enter(self, guard):
        t = guard.__enter__()
        self.ctx.append(guard)
        return t

    def sb(self, name, shape, dt):
        return Res(name, self._enter(self.nc.sbuf_tensor(name, list(shape), dt)))

    def ps(self, name, shape, dt):
        return Res(name, self._enter(self.nc.psum_tensor(name, list(shape), dt)), excl=True)

    def dram(self, name):
        if name not in self.drams:
            self.drams[name] = Res(name)
        return self.drams[name]

    def res(self, name):
        return Res(name)

    def _add(self, op, r, w):
        i = len(self.ops)
        r = list(r)
        w = list(w)
        for x in r:
            if x.excl and x not in w:
                w.append(x)
        for x in r:
            if x.lw is not None:
                op.deps.add(x.lw)
                op.hard.add(x.lw)
        for x in w:
            if x.lw is not None:
                op.deps.add(x.lw)
                op.hard.add(x.lw)
            for j in x.rd:
                op.deps.add(j)
        for x in r:
            x.rd.append(i)
        for x in w:
            x.lw = i
            x.rd = []
        op.deps |= self.bar
        op.deps.discard(i)
        self.ops.append(op)
        self.last_eng[op.eng] = i
        return i

    def mark_outputs(self, res):
        for op in self.ops:
            if op.dma and res in getattr(op, "wres", ()):
                op.out = True

    def barrier(self):
        self.bar = set(self.last_eng.values()) | set(self.chan_last.values())

    def scope_mark(self):
        return len(self.ctx)

    def scope_free(self, mark):
        self.barrier()
        while len(self.ctx) > mark:
            self.ctx.pop().__exit__(None, None, None)

    def pe(self, fn, r=(), w=()):
        return self._add(Op("pe", fn), r, w)

    def act(self, fn, r=(), w=()):
        return self._add(Op("act", fn), r, w)

    def dve(self, fn, r=(), w=()):
        return self._add(Op("dve", fn), r, w)

    def pool(self, fn, r=(), w=()):
        return self._add(Op("pool", fn), r, w)

    def dma(self, q, out_ap, in_ap, r=(), w=(), out=False, chan=None, **kw):
        op = Op(q, lambda e: e.dma_start(out=out_ap, in_=in_ap, **kw), dma=True)
        if chan is None:
            nch = 8 if q == "pool" else self.n_dma_chan
            k = self.rrq.get(q, 0)
            self.rrq[q] = k + 1
            chan = (0 if q == "pool" else 1) * 100 + (k % nch)
        op.chan = chan
        op.out = out
        op.wres = tuple(w)
        prev = self.chan_last.get(chan)
        if prev is not None:
            op.deps.add(prev)
        self.chan_cnt[chan] = self.chan_cnt.get(chan, 0) + 16
        op.cval = self.chan_cnt[chan]
        i = self._add(op, r, w)
        self.chan_last[chan] = i
        return i

    def coll(self, kind, in_ap, out_ap, groups, r=(), w=()):
        op = Op("pool", lambda e: e.collective_compute(kind, ALU.bypass, replica_groups=groups,
                                                        ins=[in_ap], outs=[out_ap]), dma=True)
        k = self.rrq.get("coll", 0)
        self.rrq["coll"] = k + 1
        chan = 200 + (k % 4)
        op.chan = chan
        op.wres = tuple(w)
        prev = self.chan_last.get(chan)
        if prev is not None:
            op.deps.add(prev)
        self.chan_cnt[chan] = self.chan_cnt.get(chan, 0) + 16
        op.cval = self.chan_cnt[chan]
        i = self._add(op, r, w)
        self.chan_last[chan] = i
        return i

    def emit(self):
        nc = self.nc
        ops = self.ops
        def needs_sem(p, op, d=None):
            if p.dma:
                return False
            if op.dma:
                return True
            if p.eng != op.eng:
                return True
            if p.eng == "pe":
                return False
            if d is not None and d not in op.hard:
                return False
            return SAME_ENGINE_SYNC

        for i, op in enumerate(ops):
            for d in op.deps:
                if needs_sem(ops[d], op, d):
                    ops[d].signal = True
        cnt = {e: 0 for e in ENGS}
        for op in ops:
            if not op.dma and op.signal:
                cnt[op.eng] += 1
                op.sval = cnt[op.eng]
        sems = {}
        for e in ENGS:
            sems[e] = self._enter(nc.semaphore("s_" + e))
        csems = {}
        for c in sorted(self.chan_cnt):
            csems[c] = self._enter(nc.semaphore("c_%d" % c))
        final_waits = [(csems[op.chan], op.cval) for op in ops if op.dma and op.out]
        per_eng = {e: [] for e in ENGS}
        for i, op in enumerate(ops):
            per_eng[op.eng].append(i)

        def run(eng_name, eng):
            waited = {}
            for i in per_eng[eng_name]:
                op = ops[i]
                need = {}
                for d in op.deps:
                    p = ops[d]
                    if p.dma:
                        key, val = ("c", p.chan), p.cval
                    else:
                        if not needs_sem(p, op, d):
                            continue
                        key, val = ("e", p.eng), p.sval
                    if need.get(key, 0) < val:
                        need[key] = val
                for key, val in need.items():
                    if waited.get(key, 0) >= val:
                        continue
                    waited[key] = val
                    s = csems[key[1]] if key[0] == "c" else sems[key[1]]
                    eng.wait_ge(s, val)
                ins = op.fn(eng)
                if op.dma:
                    ins.then_inc(csems[op.chan], 16)
                elif op.signal:
                    ins.then_inc(sems[eng_name], 1)
            if eng_name == "sp":
                done = {}
                for s, v in final_waits:
                    k = id(s)
                    if k not in done or done[k][1] < v:
                        done[k] = (s, v)
                for s, v in done.values():
                    eng.wait_ge(s, v)

        with nc.Block() as block:
            @block.sync
            def _(e):
                run("sp", e)

            @block.scalar
            def _(e):
                run("act", e)

            @block.vector
            def _(e):
                run("dve", e)

            @block.gpsimd
            def _(e):
                run("pool", e)

            @block.tensor
            def _(e):
                run("pe", e)
        for g in reversed(self.ctx):
            g.__exit__(None, None, None)
        self.ctx = []


D_MODEL = 2048
D_FF = 8192
SEQ = 8192
BATCH = 2
NCORES = 8
EPS = 1e-6
TG = 512


class Ctx:
    pass


def make_common(p, consts_ap):
    c = Ctx()
    c.banks = [p.ps("bank%d" % i, [128, 512], F32) for i in range(8)]
    c.bi = 0
    c.ident = p.sb("ident", [128, 128], F32)
    c.ones_bf = p.sb("ones_bf", [128, 128], BF16)
    c.ones32 = p.sb("ones32", [128, 128], F32)
    p.dma("sp", c.ident.t[:], consts_ap[:, 0:128], w=[c.ident])
    p.dma("sp", c.ones32.t[:], consts_ap[:, 128:256], w=[c.ones32])
    p.dve(lambda e: e.tensor_copy(c.ones_bf.t[:], c.ones32.t[:]), r=[c.ones32], w=[c.ones_bf])
    c.eps = p.sb("eps_col", [128, 1], F32)
    p.dve(lambda e: e.memset(c.eps.t[:], EPS), w=[c.eps])
    return c


def next_bank(c, n=5):
    b = c.banks[c.bi % n]
    c.bi += 1
    return b


def phase_c(p, c, *, D, DFF, F, NT, OT_src, OT_res, xT, xT_res, w_out, w_up, w_down, nv,
            v_post, v_pre, v_mpost, v_next, hT_dst=None, hT_res=None, out_dst=None, out_res=None,
            pfx="c", stop=99, x_src=None, x_res=None, bufs=None, xT_out=None, xT_out_res=None, wq="pool", nwb=3):
    if xT_out is None:
        xT_out, xT_out_res = xT, xT_res
    KC = D // 128
    FKC = F // 128
    FC = DFF // 128
    NH = 2
    FCH = FC // NH
    UW = min(4, FCH)
    DW = min(2, KC)
    OW = min(4, KC)
    WELEMS = max(FKC * OW * 128, KC * UW * 128, FCH * DW * 128)
    if bufs is not None and "xg" in bufs:
        xg, ag, mg, ug, sq, tmp, rbc, wb = (bufs[k] for k in ("xg", "ag", "mg", "ug", "sq", "tmp", "rbc", "wb"))
        NWB = len(wb)
    else:
        xg = p.sb(pfx + "xg", [128, KC, TG], F32)
        ag = p.sb(pfx + "ag", [128, max(KC, FKC), TG], BF16)
        mg = p.sb(pfx + "mg", [128, KC, TG], F32)
        ug = p.sb(pfx + "ug", [128, FCH, TG], BF16)
        sq = [p.sb(pfx + "sq%d" % i, [128, TG], BF16) for i in range(2)]
        tmp = [p.sb(pfx + "tmp%d" % i, [128, TG], F32) for i in range(2)]
        rbc = p.sb(pfx + "rbc", [128, TG], F32)
        NWB = nwb
        wb = [p.sb(pfx + "wb%d" % i, [128, WELEMS], BF16) for i in range(NWB)]
        if bufs is not None:
            bufs.update(xg=xg, ag=ag, mg=mg, ug=ug, sq=sq, tmp=tmp, rbc=rbc, wb=wb)
    st = {"w": 0, "s": 0, "t": 0}
    stat_bank = c.banks[5]
    misc_bank = c.banks[7]

    def wbuf():
        b = wb[st["w"] % NWB]
        st["w"] += 1
        return b

    def stats_accum(src_ap, src_res, first, last):
        s = sq[st["s"] % 2]
        st["s"] += 1
        p.act(lambda e: e.activation(s.t[:], src_ap, AF.Square), r=src_res, w=[s])
        p.pe(lambda e: e.matmul(stat_bank.t[:], c.ones_bf.t[:], s.t[:], start=first, stop=last),
             r=[s, c.ones_bf] + ([] if first else [stat_bank]), w=[stat_bank])

    def rstd_from_stats():
        p.dve(lambda e: e.tensor_scalar(rbc.t[:], stat_bank.t[:], 1.0 / D, None, ALU.mult),
              r=[stat_bank], w=[rbc])
        p.dve(lambda e: e.tensor_scalar(rbc.t[:], rbc.t[:], EPS, None, ALU.add),
              r=[rbc], w=[rbc])
        p.act(lambda e: e.activation(rbc.t[:], rbc.t[:], AF.Sqrt), r=[rbc], w=[rbc])
        p.dve(lambda e: e.reciprocal(rbc.t[:], rbc.t[:]), r=[rbc], w=[rbc])

    def residual_update(vcol):
        for cc in range(KC):
            t = tmp[st["t"] % 2]
            st["t"] += 1
            p.dve(lambda e, cc=cc, t=t: e.scalar_tensor_tensor(
                t.t[:], mg.t[:, cc, :], nv.t[:, vcol + cc:vcol + cc + 1], rbc.t[:], ALU.mult, ALU.mult),
                r=[mg, nv, rbc], w=[t])
            p.pool(lambda e, cc=cc, t=t: e.tensor_tensor(xg.t[:, cc, :], xg.t[:, cc, :], t.t[:], ALU.add),
                   r=[xg, t], w=[xg])
            stats_accum(xg.t[:, cc, :], [xg], cc == 0, cc == KC - 1)

    def norm_to_ag(vcol):
        rstd_from_stats()
        for cc in range(KC):
            p.dve(lambda e, cc=cc: e.scalar_tensor_tensor(
                ag.t[:, cc, :], xg.t[:, cc, :], nv.t[:, vcol + cc:vcol + cc + 1], rbc.t[:], ALU.mult, ALU.mult),
                r=[xg, nv, rbc], w=[ag])

    for g in range(NT // TG):
        t0 = g * TG
        if x_src is not None:
            ov = mg.t[:].rearrange("p a b -> p (a b)").rearrange("p (tt d) -> p tt d", d=D)
            p.dma("sp", ov, x_src(t0).rearrange("(tt p) d -> p tt d", p=128), r=[x_res], w=[mg])
            for tt in range(TG // 128):
                for k4 in range(max(1, KC // 4)):
                    nk = min(4, KC)
                    bk = next_bank(c)
                    for j in range(nk):
                        kc = k4 * 4 + j
                        p.pe(lambda e, kc=kc, j=j, tt=tt, bk=bk: e.transpose(
                            bk.t[:, j * 128:(j + 1) * 128], ov[:, tt, kc * 128:(kc + 1) * 128], c.ident.t[:]),
                            r=[mg, c.ident], w=[bk])
                    p.act(lambda e, tt=tt, k4=k4, nk=nk, bk=bk: e.copy(
                        xg.t[:, k4 * 4:k4 * 4 + nk, tt * 128:(tt + 1) * 128],
                        bk.t[:, 0:nk * 128].rearrange("p (a b) -> p a b", a=nk)), r=[bk], w=[xg])
            for cc in range(KC):
                stats_accum(xg.t[:, cc, :], [xg], cc == 0, cc == KC - 1)
            norm_to_ag(v_next)
            p.dma("sp", hT_dst(t0).rearrange("(kc p) t -> p kc t", p=128), ag.t[:, 0:KC, :], r=[ag], w=[hT_res])
            p.dma("sp", xT_out[:, t0:t0 + TG].rearrange("(kc p) t -> p kc t", p=128), xg.t[:], r=[xg], w=[xT_out_res])
            continue
        p.dma("sp", xg.t[:], xT[:, t0:t0 + TG].rearrange("(kc p) t -> p kc t", p=128), r=[xT_res], w=[xg])
        p.dma("sp", ag.t[:, 0:FKC, :], OT_src(t0).rearrange("(kc p) t -> p kc t", p=128), r=[OT_res], w=[ag])
        if stop < 1:
            return
        for q in range(KC // OW):
            wt = wbuf()
            wv = wt.t[:, 0:FKC * OW * 128].rearrange("p (kc c) -> p kc c", kc=FKC)
            p.dma(wq, wv, w_out[:, q * OW * 128:(q + 1) * OW * 128].rearrange("(kc p) c -> p kc c", p=128),
                  w=[wt])
            for j in range(OW):
                cc = q * OW + j
                bk = next_bank(c)
                for kc in range(FKC):
                    p.pe(lambda e, kc=kc, j=j, bk=bk, wv=wv: e.matmul(
                        bk.t[:], wv[:, kc, j * 128:(j + 1) * 128], ag.t[:, kc, :], start=(kc == 0), stop=(kc == FKC - 1)),
                        r=[wt, ag] + ([] if kc == 0 else [bk]), w=[bk])
                p.dve(lambda e, cc=cc, bk=bk: e.tensor_copy(mg.t[:, cc, :], bk.t[:]), r=[bk], w=[mg])
                stats_accum(bk.t[:], [bk], cc == 0, cc == KC - 1)
        if stop < 2:
            return
        rstd_from_stats()
        if stop < 3:
            return
        residual_update(v_post)
        if stop < 4:
            return
        norm_to_ag(v_pre)
        if stop < 5:
            return
        for h in range(NH):
            for q in range(FCH // UW):
                wt = wbuf()
                wv = wt.t[:, 0:KC * UW * 128].rearrange("p (kc c) -> p kc c", kc=KC)
                f0 = (h * FCH + q * UW) * 128
                p.dma(wq, wv, w_up[:, f0:f0 + UW * 128].rearrange("(kc p) c -> p kc c", p=128), w=[wt])
                for j in range(UW):
                    fl = q * UW + j
                    bk = next_bank(c)
                    for kc in range(KC):
                        p.pe(lambda e, kc=kc, j=j, bk=bk, wv=wv: e.matmul(
                            bk.t[:], wv[:, kc, j * 128:(j + 1) * 128], ag.t[:, kc, :], start=(kc == 0), stop=(kc == KC - 1)),
                            r=[wt, ag] + ([] if kc == 0 else [bk]), w=[bk])
                    t = tmp[st["t"] % 2]
                    st["t"] += 1
                    p.act(lambda e, bk=bk, t=t: e.activation(t.t[:], bk.t[:], AF.Relu), r=[bk], w=[t])
                    p.dve(lambda e, fl=fl, t=t: e.tensor_tensor(ug.t[:, fl, :], t.t[:], t.t[:], ALU.mult), r=[t], w=[ug])
            for q in range(KC // DW):
                wt = wbuf()
                wv = wt.t[:, 0:FCH * DW * 128].rearrange("p (fc c) -> p fc c", fc=FCH)
                r0 = h * FCH * 128
                p.dma(wq, wv, w_down[r0:r0 + FCH * 128, q * DW * 128:(q + 1) * DW * 128].rearrange(
                    "(fc p) c -> p fc c", p=128), w=[wt])
                for j in range(DW):
                    cc = q * DW + j
                    bk = next_bank(c)
                    for fc in range(FCH):
                        p.pe(lambda e, fc=fc, j=j, bk=bk, wv=wv: e.matmul(
                            bk.t[:], wv[:, fc, j * 128:(j + 1) * 128], ug.t[:, fc, :], start=(fc == 0), stop=(fc == FCH - 1)),
                            r=[wt, ug] + ([] if fc == 0 else [bk]), w=[bk])
                    if h == 0:
                        p.dve(lambda e, cc=cc, bk=bk: e.tensor_copy(mg.t[:, cc, :], bk.t[:]), r=[bk], w=[mg])
                    else:
                        p.dve(lambda e, cc=cc, bk=bk: e.tensor_tensor(mg.t[:, cc, :], mg.t[:, cc, :], bk.t[:], ALU.add),
                              r=[bk, mg], w=[mg])
                        stats_accum(mg.t[:, cc, :], [mg], cc == 0, cc == KC - 1)
        rstd_from_stats()
        residual_update(v_mpost)
        if hT_dst is not None:
            norm_to_ag(v_next)
            p.dma("sp", hT_dst(t0).rearrange("(kc p) t -> p kc t", p=128), ag.t[:, 0:KC, :], r=[ag], w=[hT_res])
            p.dma("sp", xT_out[:, t0:t0 + TG].rearrange("(kc p) t -> p kc t", p=128), xg.t[:], r=[xg], w=[xT_out_res])
        if out_dst is not None:
            ov = mg.t[:].rearrange("p a b -> p (a b)").rearrange("p (tt d) -> p tt d", d=D)
            for tt in range(TG // 128):
                for k4 in range(KC // 4 if KC >= 4 else 1):
                    nk = min(4, KC)
                    bk = next_bank(c)
                    for j in range(nk):
                        kc = k4 * 4 + j
                        p.pe(lambda e, kc=kc, j=j, tt=tt, bk=bk: e.transpose(
                            bk.t[:, j * 128:(j + 1) * 128], xg.t[:, kc, tt * 128:(tt + 1) * 128], c.ident.t[:]),
                            r=[xg, c.ident], w=[bk])
                    p.act(lambda e, tt=tt, k4=k4, nk=nk, bk=bk: e.copy(
                        ov[:, tt, k4 * 512:k4 * 512 + nk * 128], bk.t[:, 0:nk * 128]), r=[bk], w=[mg])
            p.dma("sp", out_dst(t0).rearrange("(tt p) d -> p tt d", p=128), ov, r=[mg], w=[out_res], out=True)


def _consts():
    return np.concatenate([np.eye(128, dtype=np.float32), np.ones((128, 128), np.float32),
                           np.zeros((128, 128), np.float32)], axis=1)


def _nv_layout(vecs):
    return np.ascontiguousarray(np.concatenate([np.asarray(v, np.float32).reshape(-1, 128).T for v in vecs], axis=1))


def build_program_skeleton(NT=SEQ * BATCH // NCORES, D=D_MODEL, DFF=D_FF):
    KC = D // 128
    nc = bass.Bass("TRN2", target_bir_lowering=False)
    x = nc.dram_tensor("x", [NT, D], F32, kind="ExternalInput").ap()
    cst = nc.dram_tensor("consts", [128, 384], F32, kind="ExternalInput").ap()
    nvd = nc.dram_tensor("nv", [128, 8 * KC], F32, kind="ExternalInput").ap()
    w_out = [nc.dram_tensor("w_out%d" % l, [D, D], F32, kind="ExternalInput").ap() for l in range(2)]
    w_up = [nc.dram_tensor("w_up%d" % l, [D, DFF], F32, kind="ExternalInput").ap() for l in range(2)]
    w_down = [nc.dram_tensor("w_down%d" % l, [DFF, D], F32, kind="ExternalInput").ap() for l in range(2)]
    out = nc.dram_tensor("out", [NT, D], F32, kind="ExternalOutput").ap()
    xT = nc.dram_tensor("xT_scr", [D, NT], F32).ap()
    hT = [nc.dram_tensor("hT_scr%d" % l, [D, NT], BF16).ap() for l in range(2)]
    p = Prog(nc)
    c = make_common(p, cst)
    nv = p.sb("nv_sb", [128, 8 * KC], F32)
    p.dma("sp", nv.t[:], nvd, w=[nv])
    bufs = {}
    common = dict(D=D, DFF=DFF, F=D, NT=NT, xT=xT, xT_res=p.dram("xT"), nv=nv, bufs=bufs)
    V = lambda i: i * KC
    phase_c(p, c, OT_src=None, OT_res=None, w_out=None, w_up=None, w_down=None,
            v_post=0, v_pre=0, v_mpost=0, v_next=V(0),
            hT_dst=lambda t0: hT[0][:, t0:t0 + TG], hT_res=p.dram("hT0"),
            x_src=lambda t0: x[t0:t0 + TG, :], x_res=p.dram("x"), pfx="c", **common)
    phase_c(p, c, OT_src=lambda t0: hT[0][:, t0:t0 + TG], OT_res=p.dram("hT0"),
            w_out=w_out[0], w_up=w_up[0], w_down=w_down[0],
            v_post=V(1), v_pre=V(2), v_mpost=V(3), v_next=V(4),
            hT_dst=lambda t0: hT[1][:, t0:t0 + TG], hT_res=p.dram("hT1"), pfx="c", **common)
    phase_c(p, c, OT_src=lambda t0: hT[1][:, t0:t0 + TG], OT_res=p.dram("hT1"),
            w_out=w_out[1], w_up=w_up[1], w_down=w_down[1],
            v_post=V(5), v_pre=V(6), v_mpost=V(7), v_next=0,
            out_dst=lambda t0: out[t0:t0 + TG, :], out_res=p.dram("out"), pfx="c", **common)
    p.emit()
    return nc


N_DUMMY = int(_os.environ.get("DUMMY", "0"))


def pe_warm(p, c, W, n=None):
    for _ in range(N_DUMMY if n is None else n):
        p.pe(lambda e: e.matmul(c.banks[6].t[:], c.ones_bf.t[:], W.wbf[0].t[:], start=True, stop=True,
                                skip_group_check=True), r=[], w=[])


def mm(p, out_ap, lhsT_ap, rhs_ap, start, stop, r, w):
    return p.pe(lambda e: e.matmul(out_ap, lhsT_ap, rhs_ap, start=start, stop=stop, skip_group_check=True), r=r, w=w)


def phase_a(p, c, *, S, D, hT_all, hT_res, w_fm, w_tm, NFM, NTM, fm_chunks, fm_dst, fm_res,
            v_dst, v_res, nvh, arena, fd=None, nfd=0, convw=None):
    KC = D // 128
    NG = S // TG
    o1 = KC * NFM
    o2 = o1 + KC * NTM
    o3 = o2 + 2 * KC * TG
    assert o3 <= arena.t.shape[1], (o3, arena.t.shape)
    wfm = Res("a_wfm", arena.t[:, 0:o1].rearrange("p (kc c) -> p kc c", kc=KC))
    wtm = Res("a_wtm", arena.t[:, o1:o2].rearrange("p (kc c) -> p kc c", kc=KC))
    KP = min(4, KC)
    wfm_parts = [Res("a_wfm_p%d" % i, wfm.t) for i in range(KC // KP)]
    wtm_parts = [Res("a_wtm_p%d" % i, wtm.t) for i in range(KC // KP)]
    for i in range(KC // KP):
        p.dma("pool", wfm.t[:, i * KP:(i + 1) * KP, :],
              w_fm[i * KP * 128:(i + 1) * KP * 128, :].rearrange("(kc p) c -> p kc c", p=128), w=[wfm_parts[i]])
        p.dma("pool", wtm.t[:, i * KP:(i + 1) * KP, :],
              w_tm[i * KP * 128:(i + 1) * KP * 128, :].rearrange("(kc p) c -> p kc c", p=128), w=[wtm_parts[i]])
    hgs = [Res("a_hg%d" % i, arena.t[:, o2 + i * KC * TG:o2 + (i + 1) * KC * TG].rearrange("p (kc t) -> p kc t", kc=KC))
           for i in range(2)]
    vgs = [p.sb("a_vg%d" % i, [128, TG // 128, nvh, 65], BF16) for i in range(2)]
    ots = [p.sb("a_ot%d" % i, [128, TG], BF16) for i in range(3)]
    nconv = sum(1 for ch in fm_chunks if ch.get("conv") is not None)
    cbs = [p.sb("a_cb%d" % i, [128, TG + 3], F32) for i in range(nconv)]
    accs = [p.sb("a_acc%d" % i, [128, TG], F32) for i in range(2)]
    for cb in cbs:
        p.dve(lambda e, cb=cb: e.memset(cb.t[:], 0.0), w=[cb])
    for vg in vgs:
        p.dve(lambda e, vg=vg: e.memset(vg.t[:], 1.0), w=[vg])
    st = {"o": 0, "a": 0}
    v_view = v_dst.rearrange("h p (b x) -> p b h x", x=65)
    for g in range(NG):
        t0 = g * TG
        hg = hgs[g % 2]
        p.dma("sp", hg.t[:], hT_all[:, t0:t0 + TG].rearrange("(kc p) t -> p kc t", p=128), r=[hT_res], w=[hg])
        for ch in fm_chunks:
            wd, col0, row0 = ch["w"], ch["col"], ch["row"]
            bk = next_bank(c)
            for kc in range(KC):
                mm(p, bk.t[0:wd, :], wfm.t[:, kc, col0:col0 + wd], hg.t[:, kc, :], kc == 0, kc == KC - 1,
                   r=[wfm_parts[kc // KP], hg] + ([] if kc == 0 else [bk]), w=[bk])
            ot = ots[st["o"] % 3]
            st["o"] += 1
            ci = ch.get("conv")
            if ci is None:
                sc = ch.get("scale", 1.0)
                if sc == 1.0:
                    p.act(lambda e, ot=ot, bk=bk, wd=wd: e.copy(ot.t[0:wd, :], bk.t[0:wd, :]), r=[bk], w=[ot])
                else:
                    p.act(lambda e, ot=ot, bk=bk, wd=wd, sc=sc: e.mul(ot.t[0:wd, :], bk.t[0:wd, :], sc), r=[bk], w=[ot])
            else:
                cb = cbs[ci]
                acc = accs[st["a"] % 2]
                st["a"] += 1
                p.dve(lambda e, cb=cb, wd=wd: e.tensor_copy(cb.t[0:wd, 0:3], cb.t[0:wd, TG:TG + 3]), r=[cb], w=[cb])
                p.act(lambda e, cb=cb, bk=bk, wd=wd: e.copy(cb.t[0:wd, 3:TG + 3], bk.t[0:wd, :]), r=[bk], w=[cb])
                p.dve(lambda e, cb=cb, acc=acc, wd=wd, ci=ci: e.tensor_scalar(
                    acc.t[0:wd, :], cb.t[0:wd, 3:TG + 3], convw.t[0:wd, ci, 3:4], convw.t[0:wd, ci, 4:5],
                    ALU.mult, ALU.add), r=[cb, convw], w=[acc])
                for j in (2, 1, 0):
                    p.dve(lambda e, cb=cb, acc=acc, wd=wd, ci=ci, j=j: e.scalar_tensor_tensor(
                        acc.t[0:wd, :], cb.t[0:wd, j:j + TG], convw.t[0:wd, ci, j:j + 1], acc.t[0:wd, :],
                        ALU.mult, ALU.add), r=[cb, convw, acc], w=[acc])
                p.act(lambda e, ot=ot, acc=acc, wd=wd: e.activation(ot.t[0:wd, :], acc.t[0:wd, :], AF.Silu),
                      r=[acc], w=[ot])
            p.dma("sp", fm_dst[row0:row0 + wd, t0:t0 + TG], ot.t[0:wd, :], r=[ot], w=[fm_res])
        vg = vgs[g % 2]
        for tt in range(TG // 128):
            blk = g * (TG // 128) + tt
            bk = next_bank(c)
            for kc in range(KC):
                mm(p, bk.t[:, 0:NTM], hg.t[:, kc, tt * 128:(tt + 1) * 128], wtm.t[:, kc, :], kc == 0, kc == KC - 1,
                   r=[wtm_parts[kc // KP], hg] + ([] if kc == 0 else [bk]), w=[bk])
            p.act(lambda e, tt=tt, bk=bk, vg=vg: e.copy(
                vg.t[:, tt, 0:nvh, 0:64], bk.t[:, 0:nvh * 64].rearrange("p (h d) -> p h d", h=nvh)), r=[bk], w=[vg])
            if fd is not None:
                p.dve(lambda e, blk=blk, bk=bk: e.tensor_copy(fd.t[:, blk, :], bk.t[:, nvh * 64:nvh * 64 + nfd]),
                      r=[bk], w=[fd])
        nb = TG // 128
        for h in range(nvh):
            p.dma("sp", v_view[:, g * nb:(g + 1) * nb, h, :], vg.t[:, :, h, :], r=[vg], w=[v_res])


class AttnWork:
    def __init__(self, p, c, masks_sb, sid=0, zbanks=(0, 1, 2, 3), obanks=(4, 5), misc=7, nt32=3, light=False):
        sfx = "_s%d" % sid
        self.t32 = [p.sb("w_t32_%d%s" % (i, sfx), [128, TG], F32) for i in range(nt32)]
        self.e32 = [p.sb("w_e32_%d%s" % (i, sfx), [128, TG], F32) for i in range(1 if light else 3)]
        self.wbf = [p.sb("w_wbf_%d%s" % (i, sfx), [128, TG], BF16) for i in range(4)]
        self.spb = [p.sb("w_spb_%d%s" % (i, sfx), [128, TG], BF16) for i in range(1 if light else 3)]
        self.accs = [p.sb("w_acc%d%s" % (i, sfx), [128, TG], BF16) for i in range(1 if light else 2)]
        self.osb = [p.sb("w_osb_%d%s" % (i, sfx), [128, TG], F32) for i in range(2)]
        self.obf = [p.sb("w_obf_%d%s" % (i, sfx), [128, TG], BF16) for i in range(2)]
        self.rden = p.sb("w_rden" + sfx, [128, TG], F32)
        self.masks = masks_sb
        self.onecol = p.sb("w_onecol" + sfx, [128, 1], F32)
        p.dve(lambda e: e.memset(self.onecol.t[:], 1.0), w=[self.onecol])
        self.n = {"t": 0, "e": 0, "w": 0, "s": 0, "o": 0, "z": 0, "d": 0, "g": 0}
        self.zb = [c.banks[i] for i in zbanks]
        self.ob = [c.banks[i] for i in obanks]
        self.misc = c.banks[misc]

    def nxt(self, lst, key):
        x = lst[self.n[key] % len(lst)]
        self.n[key] += 1
        return x


def load_masks(p, masks_ap, nmask):
    m = p.sb("w_masks", [128, nmask, 896], F32)
    p.dma("sp", m.t[:], masks_ap.rearrange("p (m x) -> p m x", m=nmask), w=[m])
    return m


def run_streams(gens):
    gens = list(gens)
    while gens:
        for g in list(gens):
            try:
                next(g)
            except StopIteration:
                gens.remove(g)


def chain(*gens):
    for g in gens:
        yield from g


def run_pipeline(n, stages):
    lo = min(o for o, _ in stages)
    hi = max(o for o, _ in stages)
    for s_ in range(-hi, n - lo):
        for o, fn in stages:
            t = s_ + o
            if 0 <= t < n:
                fn(t)


def make_tiles(NG, W, kb_fn):
    T = []
    for qg in range(NG):
        kbs = kb_fn(qg)
        n = len(kbs)
        for idx, kb in enumerate(kbs):
            T.append(dict(qg=qg, q0=qg * TG, idx=idx, n=n, kb=kb, diag=(kb >= 4 * qg), dl=qg * TG - kb * 128,
                          ob=W.ob[qg % len(W.ob)]))
    return T


def attn_sb(p, c, W, *, S, kT, qT, qk_res, vt, negU, negones, ot_dst, ot_res, M_RS=0, M_NEGS=1):
    NG = S // TG
    T = make_tiles(NG, W, lambda qg: list(range(4 * qg + 3, -1, -1)))
    accst = {"acc": None}

    def s_z(i):
        t = T[i]
        t["zb"] = W.zb[i % len(W.zb)]
        kb, q0 = t["kb"], t["q0"]
        mm(p, t["zb"].t[:], kT[:, kb * 128:(kb + 1) * 128], qT[:, q0:q0 + TG], True, False, r=list(qk_res), w=[t["zb"]])

    def s_ln(i):
        t = T[i]
        zb, dl = t["zb"], t["dl"]
        e = W.nxt(W.e32, "e")
        p.act(lambda e_: e_.activation(e.t[:], zb.t[:], AF.Exp), r=[zb], w=[e])
        spb = W.nxt(W.spb, "s")
        t["spb"] = spb
        if t["diag"]:
            tt = W.nxt(W.t32, "t")
            p.act(lambda e_: e_.activation(tt.t[:], e.t[:], AF.Ln, bias=W.onecol.t[:, 0:1]), r=[e, W.onecol], w=[tt])
            p.dve(lambda e_: e_.tensor_tensor(
                spb.t[:], tt.t[:], W.masks.t[:, M_RS, dl + 384:dl + 384 + TG], ALU.mult), r=[tt, W.masks], w=[spb])
        else:
            p.act(lambda e_: e_.activation(spb.t[:], e.t[:], AF.Ln, bias=W.onecol.t[:, 0:1]), r=[e, W.onecol], w=[spb])

    def s_cs(i):
        t = T[i]
        pe_warm(p, c, W)
        zb, spb, idx, n = t["zb"], t["spb"], t["idx"], t["n"]
        mm(p, zb.t[:], negU.t[:], spb.t[:], False, idx == 0, r=[negU, spb, zb], w=[zb])
        acc = accst["acc"]
        if idx > 0:
            mm(p, zb.t[:], negones.t[:], acc.t[:], False, True, r=[negones, acc, zb], w=[zb])
        if idx < n - 1:
            if idx == 0:
                accst["acc"] = spb
            else:
                nacc = W.nxt(W.accs, "g")
                p.pool(lambda e_: e_.tensor_tensor(nacc.t[:], acc.t[:], spb.t[:], ALU.add), r=[spb, acc], w=[nacc])
                accst["acc"] = nacc

    def s_fin(i):
        t = T[i]
        zb, dl = t["zb"], t["dl"]
        wt = W.nxt(W.wbf, "w")
        t["wt"] = wt
        if t["diag"]:
            tt = W.nxt(W.t32, "t")
            p.dve(lambda e_: e_.tensor_tensor(tt.t[:], zb.t[:], W.masks.t[:, M_NEGS, dl + 384:dl + 384 + TG], ALU.add),
                  r=[zb, W.masks], w=[tt])
            p.act(lambda e_: e_.activation(wt.t[:], tt.t[:], AF.Exp), r=[tt], w=[wt])
        else:
            p.act(lambda e_: e_.activation(wt.t[:], zb.t[:], AF.Exp), r=[zb], w=[wt])

    def s_pv(i):
        t = T[i]
        ob, idx, n, q0 = t["ob"], t["idx"], t["n"], t["q0"]
        mm(p, ob.t[:, :], vt.t[:, t["kb"] * 65:t["kb"] * 65 + 128], t["wt"].t[:],
           idx == 0, idx == n - 1, r=[vt, t["wt"]] + ([] if idx == 0 else [ob]), w=[ob])
        if idx == n - 1:
            obf = W.nxt(W.obf, "o")
            p.act(lambda e_: e_.copy(obf.t[0:64, :], ob.t[0:64, :]), r=[ob], w=[obf])
            p.dma("sp", ot_dst[:, q0:q0 + TG], obf.t[0:64, :], r=[obf], w=[ot_res])

    run_pipeline(len(T), [(2, s_z), (1, s_ln), (0, s_cs), (-1, s_fin), (-2, s_pv)])


def norm_epilogue(p, c, W, ob, ot_dst_ap, ot_res):
    osb = W.nxt(W.osb, "o")
    p.dve(lambda e: e.tensor_copy(osb.t[0:65, :], ob.t[0:65, :]), r=[ob], w=[osb])
    p.act(lambda e: e.activation(W.rden.t[64:65, :], osb.t[64:65, :], AF.Ln), r=[osb], w=[W.rden])
    p.act(lambda e: e.activation(W.rden.t[64:65, :], W.rden.t[64:65, :], AF.Exp, scale=-1.0), r=[W.rden], w=[W.rden])
    mm(p, W.misc.t[0:64, :], c.ones32.t[64:65, 0:64], W.rden.t[64:65, :], True, True, r=[c.ones32, W.rden], w=[W.misc])
    obf = W.obf[W.n["o"] % 2]
    p.dve(lambda e: e.tensor_tensor(obf.t[0:64, :], osb.t[0:64, :], W.misc.t[0:64, :], ALU.mult),
          r=[osb, W.misc], w=[obf])
    p.dma("sp", ot_dst_ap, obf.t[0:64, :], r=[obf], w=[ot_res])


def attn_win(p, c, W, *, S, kT, qT, qk_res, vt, rd, ot_dst, ot_res):
    NG = S // TG
    T = make_tiles(NG, W, lambda qg: [kb for kb in range(4 * qg + 3, 4 * qg - 17, -1) if kb >= 0])

    def s_z(i):
        t = T[i]
        t["zb"] = W.zb[i % len(W.zb)]
        kb, q0 = t["kb"], t["q0"]
        mm(p, t["zb"].t[:], kT[:, kb * 128:(kb + 1) * 128], qT[:, q0:q0 + TG], True, True, r=list(qk_res), w=[t["zb"]])

    def s_exp(i):
        t = T[i]
        zb = t["zb"]
        e = W.nxt(W.e32, "e")
        t["e"] = e
        p.act(lambda e_: e_.activation(e.t[:], zb.t[:], AF.Exp), r=[zb], w=[e])

    def s_mul(i):
        t = T[i]
        e, dl = t["e"], t["dl"]
        wt = W.nxt(W.wbf, "w")
        t["wt"] = wt
        p.dve(lambda e_: e_.tensor_tensor(wt.t[:], e.t[:], rd.t[:, dl + 384:dl + 384 + TG], ALU.mult), r=[e, rd], w=[wt])

    def s_pv(i):
        t = T[i]
        ob, idx, n, q0 = t["ob"], t["idx"], t["n"], t["q0"]
        mm(p, ob.t[:, :], vt.t[:, t["kb"] * 65:t["kb"] * 65 + 128], t["wt"].t[:],
           idx == 0, idx == n - 1, r=[vt, t["wt"]] + ([] if idx == 0 else [ob]), w=[ob])
        if idx == n - 1:
            norm_epilogue(p, c, W, ob, ot_dst[:, q0:q0 + TG], ot_res)

    run_pipeline(len(T), [(2, s_z), (1, s_exp), (0, s_mul), (-1, s_pv)])


HEAD_DIM = 64
NEG = -30000.0


def _mask_strips():
    k = np.arange(128)[:, None]
    x = np.arange(896)[None, :]
    d = x - 384 - k
    rs = (d >= 1).astype(np.float32)
    rc = (d >= 0).astype(np.float32)
    return np.ascontiguousarray(np.concatenate([rs, (rs - 1) * (-NEG), rc, (rc - 1) * (-NEG)], axis=1).astype(np.float32))


def _dilated_strip(head, n_heads=16):
    slope = 2.0 ** (-8.0 * (head + 1) / n_heads)
    k = np.arange(128)[:, None]
    x = np.arange(2944)[None, :]
    d = (x - 384 - k).astype(np.int64)
    mult = ((d >= 0) & (d <= 128)).astype(np.float64) + ((d >= 0) & (d <= 512) & (d % 4 == 0)) \
        + ((d >= 0) & (d <= 2048) & (d % 16 == 0))
    g = mult * np.exp(-slope * np.maximum(d, 0).astype(np.float64))
    return np.ascontiguousarray(g.astype(np.float32))


def _aconst():
    kp = np.arange(128)[:, None]
    k = np.arange(128)[None, :]
    return np.ascontiguousarray(np.concatenate([-(kp >= k).astype(np.float32), -np.ones((128, 128), np.float32),
                                                (kp <= k).astype(np.float32)], axis=1))


ARENA_ELEMS = 41344
K128 = bool(int(_os.environ.get("K128", "1")))
KQ = 128 if K128 else 64


def build_ab0(S=SEQ, D=D_MODEL):
    KC = D // 128
    NB = S // 128
    nc = bass.Bass("TRN2", target_bir_lowering=False)
    hT_all = nc.dram_tensor("hT_all", [D, S], BF16, kind="ExternalInput").ap()
    w_fm = nc.dram_tensor("w_fm", [D, 1024], F32, kind="ExternalInput").ap()
    w_tm = nc.dram_tensor("w_tm", [D, 512], F32, kind="ExternalInput").ap()
    cst = nc.dram_tensor("consts", [128, 384], F32, kind="ExternalInput").ap()
    acst = nc.dram_tensor("aconst", [128, 384], F32, kind="ExternalInput").ap()
    masks = nc.dram_tensor("masks", [128, 4 * 896], F32, kind="ExternalInput").ap()
    rdd = nc.dram_tensor("rd", [4, 128, 2944], F32, kind="ExternalInput").ap()
    OT = nc.dram_tensor("OT", [512, S], BF16, kind="ExternalOutput").ap()
    fm = nc.dram_tensor("fm_scr", [1024, S], BF16).ap()
    vd = nc.dram_tensor("v_scr", [8, 128, NB * 65], BF16).ap()
    p = Prog(nc)
    c = make_common(p, cst)
    arena = p.sb("arena", [128, ARENA_ELEMS], BF16)
    a32 = p.sb("ac32", [128, 256], F32)
    negU = p.sb("negU", [128, 128], BF16)
    negones = p.sb("negones", [128, 128], BF16)
    p.dma("sp", a32.t[:], acst[:, 0:256], w=[a32])
    p.dve(lambda e: e.tensor_copy(negU.t[:], a32.t[:, 0:128]), r=[a32], w=[negU])
    p.dve(lambda e: e.tensor_copy(negones.t[:], a32.t[:, 128:256]), r=[a32], w=[negones])
    chunks = [dict(w=128, col=i * 128, row=i * 128, scale=(0.125 if (i // 2) % 2 == 0 else 1.0)) for i in range(8)]
    fm_res, v_res = p.dram("fm"), p.dram("v")
    phase_a(p, c, S=S, D=D, hT_all=hT_all, hT_res=p.dram("hT_all"), w_fm=w_fm, w_tm=w_tm, NFM=1024, NTM=512,
            fm_chunks=chunks, fm_dst=fm, fm_res=fm_res, v_dst=vd, v_res=v_res, nvh=8, arena=arena)
    p.barrier()
    msb = load_masks(p, masks, 4)
    W = AttnWork(p, c, msb)
    rds = [p.sb("rd%d" % i, [128, 2944], F32) for i in range(2)]
    sets = []
    off = 0
    for i in range(2):
        qv = Res("qT%d" % i, arena.t[:, off:off + S]); off += S
        kv = Res("kT%d" % i, arena.t[:, off:off + S]); off += S
        if K128:
            p.pool(lambda e, qv=qv: e.memset(qv.t[64:128, :], 0.0), w=[qv])
            p.pool(lambda e, kv=kv: e.memset(kv.t[64:128, :], 0.0), w=[kv])
        vv = Res("vt%d" % i, arena.t[:, off:off + NB * 65 + 64]); off += NB * 65 + 64
        sets.append((qv, kv, vv))
    assert off <= ARENA_ELEMS
    ot_res = p.dram("OT")
    order = [0, 4, 1, 5, 2, 6, 3, 7]

    def loads(k):
        hh = order[k]
        qv, kv, vv = sets[k % 2]
        e = hh % 4
        if hh < 4:
            qrow, krow = e * 64, 256 + e * 64
        else:
            qrow, krow = 512 + e * 64, 768 + e * 64
        p.dma("sp", qv.t[0:64, :], fm[qrow:qrow + 64, :], r=[fm_res], w=[qv])
        p.dma("sp", kv.t[0:64, :], fm[krow:krow + 64, :], r=[fm_res], w=[kv])
        p.dma("sp", vv.t[:, 0:NB * 65], vd[hh], r=[v_res], w=[vv])
        if hh >= 4:
            p.dma("sp", rds[k % 2].t[:], rdd[e], w=[rds[k % 2]])

    loads(0)
    for k, hh in enumerate(order):
        if k + 1 < len(order):
            loads(k + 1)
        qv, kv, vv = sets[k % 2]
        if hh < 4:
            attn_sb(p, c, W, S=S, kT=kv.t[0:KQ, :], qT=qv.t[0:KQ, :], qk_res=(qv, kv), vt=vv, negU=negU, negones=negones,
                    ot_dst=OT[hh * 64:(hh + 1) * 64, :], ot_res=ot_res)
        else:
            attn_win(p, c, W, S=S, kT=kv.t[0:KQ, :], qT=qv.t[0:KQ, :], qk_res=(qv, kv), vt=vv, rd=rds[k % 2],
                     ot_dst=OT[hh * 64:(hh + 1) * 64, :], ot_res=ot_res)
    p.mark_outputs(ot_res)
    p.emit()
    return nc


def build_cq(p, c, W, cpos, ci, qg, slot):
    for j in range(TG // 128):
        blk = qg * (TG // 128) + j
        dg = W.dg[W.n["d"] % 2]
        W.n["d"] += 1
        p.dve(lambda e, dg=dg, blk=blk: e.tensor_scalar(
            dg.t[:], c.ident.t[:], cpos.t[:, blk, ci:ci + 1], -1.0, ALU.mult, ALU.mult), r=[c.ident, cpos], w=[dg])
        mm(p, W.misc.t[:, j * 128:(j + 1) * 128], c.ones32.t[:], dg.t[:], True, True, r=[c.ones32, dg], w=[W.misc])
    cq = W.cqb[slot]
    p.act(lambda e: e.copy(cq.t[:], W.misc.t[:]), r=[W.misc], w=[cq])
    return cq


def attn_fox(p, c, W, *, S, kT, qT, qk_res, vt, cpos, ci, ot_dst, ot_res, M_NEGC=3):
    NG = S // TG
    T = make_tiles(NG, W, lambda qg: list(range(4 * qg + 3, -1, -1)))
    cqs = {}

    def s_z(i):
        t = T[i]
        if t["idx"] == 0:
            cqs[t["qg"]] = build_cq(p, c, W, cpos, ci, t["qg"], t["qg"] % 2)
        t["zb"] = W.zb[i % len(W.zb)]
        kb, q0 = t["kb"], t["q0"]
        mm(p, t["zb"].t[:], kT[:, kb * 128:(kb + 1) * 128], qT[:, q0:q0 + TG], True, True, r=list(qk_res), w=[t["zb"]])

    def s_add(i):
        t = T[i]
        zb, dl = t["zb"], t["dl"]
        cq = cqs[t["qg"]]
        tt = W.nxt(W.t32, "t")
        t["t"] = tt
        p.dve(lambda e: e.tensor_tensor(tt.t[:], zb.t[:], cq.t[:], ALU.add), r=[zb, cq], w=[tt])
        if t["diag"]:
            p.pool(lambda e: e.tensor_tensor(
                tt.t[:], tt.t[:], W.masks.t[:, M_NEGC, dl + 384:dl + 384 + TG], ALU.add), r=[tt, W.masks], w=[tt])

    def s_exp(i):
        t = T[i]
        tt, kb = t["t"], t["kb"]
        wt = W.nxt(W.wbf, "w")
        t["wt"] = wt
        p.act(lambda e: e.activation(wt.t[:], tt.t[:], AF.Exp, bias=cpos.t[:, kb, ci:ci + 1]), r=[tt, cpos], w=[wt])

    def s_pv(i):
        t = T[i]
        ob, idx, n, q0 = t["ob"], t["idx"], t["n"], t["q0"]
        mm(p, ob.t[:, :], vt.t[:, t["kb"] * 65:t["kb"] * 65 + 128], t["wt"].t[:],
           idx == 0, idx == n - 1, r=[vt, t["wt"]] + ([] if idx == 0 else [ob]), w=[ob])
        if idx == n - 1:
            norm_epilogue(p, c, W, ob, ot_dst[:, q0:q0 + TG], ot_res)

    run_pipeline(len(T), [(2, s_z), (1, s_add), (0, s_exp), (-1, s_pv)])


def attn_ssd(p, c, W, *, S, BT, CT, bc_res, xdt, xdt_flat, cpos, fm, fm_res, zrow, xrow, svec, ygs, ot_dst, ot_res,
             G=4, M_NEGC=3):
    NG = S // TG
    zbs = [c.banks[0], c.banks[1]]
    obs = [c.banks[2], c.banks[3], c.banks[4], c.banks[5]]

    for qg in range(NG):
        q0 = qg * TG
        cqs = [build_cq(p, c, W, cpos, 4 + e, qg, 2 + e) for e in range(4)]
        kbs = list(range(4 * qg + 3, -1, -1))
        n = len(kbs)
        T = [dict(kb=kb, diag=(kb >= 4 * qg), dl=q0 - kb * 128) for kb in kbs]

        def s_z(i):
            t = T[i]
            t["zb"] = zbs[i % 2]
            kb = t["kb"]
            mm(p, t["zb"].t[:], BT[:, kb * 128:(kb + 1) * 128], CT[:, q0:q0 + TG], True, True, r=list(bc_res), w=[t["zb"]])

        def s_dec(i):
            t = T[i]
            kb, dl = t["kb"], t["dl"]
            t["dec"] = []
            for e in range(4):
                if t["diag"]:
                    tt = W.nxt(W.t32, "t")
                    p.pool(lambda e_, tt=tt, e=e: e_.tensor_tensor(
                        tt.t[:], cqs[e].t[:], W.masks.t[:, M_NEGC, dl + 384:dl + 384 + TG], ALU.add),
                        r=[cqs[e], W.masks], w=[tt])
                    src = tt
                else:
                    src = cqs[e]
                dec = W.nxt(W.dec, "d2")
                t["dec"].append(dec)
                p.act(lambda e_, dec=dec, src=src, e=e: e_.activation(
                    dec.t[:], src.t[:], AF.Exp, bias=cpos.t[:, kb, 4 + e:5 + e]), r=[src, cpos], w=[dec])

        def s_mul(i):
            t = T[i]
            zb = t["zb"]
            t["wt"] = []
            for e in range(4):
                wt = W.nxt(W.wts, "w2")
                t["wt"].append(wt)
                dec = t["dec"][e]
                p.dve(lambda e_, wt=wt, dec=dec: e_.tensor_tensor(wt.t[:], zb.t[:], dec.t[:], ALU.mult),
                      r=[zb, dec], w=[wt])

        def s_pv(i):
            t = T[i]
            for e in range(4):
                wt = t["wt"][e]
                o_ = (t["kb"] * 4 + e) * 64
                mm(p, obs[e].t[:, :], xdt_flat[:, o_:o_ + 128], wt.t[:], i == 0, i == n - 1,
                   r=[xdt, wt] + ([] if i == 0 else [obs[e]]), w=[obs[e]])

        run_pipeline(n, [(1, s_z), (1, s_dec), (0, s_mul), (-1, s_pv)])
        for e in range(4):
            zs = W.nxt(W.wbf, "w")
            xsl = W.nxt(W.wbf, "w")
            p.dma("sp", zs.t[0:64, :], fm[zrow + 64 * e:zrow + 64 * e + 64, q0:q0 + TG], r=[fm_res], w=[zs])
            p.dma("sp", xsl.t[0:64, :], fm[xrow + 64 * e:xrow + 64 * e + 64, q0:q0 + TG], r=[fm_res], w=[xsl])
            y = W.nxt(W.osb, "o")
            p.dve(lambda e_, y=y, xsl=xsl, e=e: e_.scalar_tensor_tensor(
                y.t[0:64, :], xsl.t[0:64, :], svec.t[0:64, 12 + e:13 + e], obs[e].t[0:64, :], ALU.mult, ALU.add),
                r=[xsl, svec, obs[e]], w=[y])
            sz = W.nxt(W.t32, "t")
            p.act(lambda e_, sz=sz, zs=zs: e_.activation(sz.t[0:64, :], zs.t[0:64, :], AF.Silu), r=[zs], w=[sz])
            yg = ygs[e]
            p.dve(lambda e_, yg=yg, y=y, sz=sz: e_.tensor_tensor(yg.t[0:64, :], y.t[0:64, :], sz.t[0:64, :], ALU.mult),
                  r=[y, sz], w=[yg])
            sq = W.nxt(W.spb, "s")
            p.act(lambda e_, sq=sq, yg=yg: e_.activation(sq.t[0:64, :], yg.t[0:64, :], AF.Square), r=[yg], w=[sq])
            mm(p, W.misc.t[0:64, :], c.ones_bf.t[0:64, 0:64], sq.t[0:64, :], e == 0, e == 3,
               r=[c.ones_bf, sq] + ([] if e == 0 else [W.misc]), w=[W.misc])
        p.dve(lambda e_: e_.tensor_scalar(W.rden.t[0:64, :], W.misc.t[0:64, :], 1.0 / (64 * G), None, ALU.mult),
              r=[W.misc], w=[W.rden])
        p.dve(lambda e_: e_.tensor_scalar(W.rden.t[0:64, :], W.rden.t[0:64, :], EPS, None, ALU.add),
              r=[W.rden], w=[W.rden])
        p.act(lambda e_: e_.activation(W.rden.t[0:64, :], W.rden.t[0:64, :], AF.Sqrt), r=[W.rden], w=[W.rden])
        p.dve(lambda e_: e_.reciprocal(W.rden.t[0:64, :], W.rden.t[0:64, :]), r=[W.rden], w=[W.rden])
        for e in range(4):
            obf = W.nxt(W.obf, "o")
            p.dve(lambda e_, obf=obf, e=e: e_.scalar_tensor_tensor(
                obf.t[0:64, :], ygs[e].t[0:64, :], svec.t[0:64, 16 + e:17 + e], W.rden.t[0:64, :], ALU.mult, ALU.mult),
                r=[ygs[e], svec, W.rden], w=[obf])
            p.dma("sp", ot_dst[64 * e:64 * e + 64, q0:q0 + TG], obf.t[0:64, :], r=[obf], w=[ot_res])


def build_ab1(S=SEQ, D=D_MODEL):
    KC = D // 128
    NB = S // 128
    nc = bass.Bass("TRN2", target_bir_lowering=False)
    hT_all = nc.dram_tensor("hT_all", [D, S], BF16, kind="ExternalInput").ap()
    w_fm = nc.dram_tensor("w_fm", [D, 1280], F32, kind="ExternalInput").ap()
    w_tm = nc.dram_tensor("w_tm", [D, 264], F32, kind="ExternalInput").ap()
    cst = nc.dram_tensor("consts", [128, 384], F32, kind="ExternalInput").ap()
    acst = nc.dram_tensor("aconst", [128, 384], F32, kind="ExternalInput").ap()
    masks = nc.dram_tensor("masks", [128, 4 * 896], F32, kind="ExternalInput").ap()
    convd = nc.dram_tensor("convw", [128, 20], F32, kind="ExternalInput").ap()
    svd = nc.dram_tensor("svec", [128, 32], F32, kind="ExternalInput").ap()
    OT = nc.dram_tensor("OT", [512, S], BF16, kind="ExternalOutput").ap()
    fm = nc.dram_tensor("fm_scr", [1280, S], BF16).ap()
    vd = nc.dram_tensor("v_scr", [4, 128, NB * 65], BF16).ap()
    p = Prog(nc)
    c = make_common(p, cst)
    arena = p.sb("arena", [128, ARENA_ELEMS], BF16)
    tri32 = p.sb("tri32", [128, 128], F32)
    p.dma("sp", tri32.t[:], acst[:, 256:384], w=[tri32])
    ident_bf = p.sb("ident_bf", [128, 128], BF16)
    p.dve(lambda e: e.tensor_copy(ident_bf.t[:], c.ident.t[:]), r=[c.ident], w=[ident_bf])
    convw = p.sb("convw_sb", [128, 4, 5], F32)
    p.dma("sp", convw.t[:], convd.rearrange("p (a b) -> p a b", a=4), w=[convw])
    svec = p.sb("svec_sb", [128, 32], F32)
    p.dma("sp", svec.t[:], svd, w=[svec])
    fd = p.sb("fd", [128, NB, 8], F32)
    chunks = [dict(w=128, col=0, row=0, scale=0.125), dict(w=128, col=128, row=128, scale=0.125),
              dict(w=128, col=256, row=256), dict(w=128, col=384, row=384),
              dict(w=128, col=512, row=512, conv=0), dict(w=128, col=640, row=640, conv=1)]
    chunks += [dict(w=128, col=768 + 128 * j, row=768 + 128 * j) for j in range(2)]
    chunks += [dict(w=128, col=1024 + 128 * j, row=1024 + 128 * j, conv=2 + j) for j in range(2)]
    fm_res, v_res = p.dram("fm"), p.dram("v")
    phase_a(p, c, S=S, D=D, hT_all=hT_all, hT_res=p.dram("hT_all"), w_fm=w_fm, w_tm=w_tm, NFM=1280, NTM=264,
            fm_chunks=chunks, fm_dst=fm, fm_res=fm_res, v_dst=vd, v_res=v_res, nvh=4, arena=arena,
            fd=fd, nfd=8, convw=convw)
    p.barrier()
    msb = load_masks(p, masks, 4)
    W = AttnWork(p, c, msb, nt32=4)
    W.dg = [p.sb("w_dg%d" % j, [128, 128], F32) for j in range(2)]
    W.cqb = [p.sb("w_cqb%d" % j, [128, TG], F32) for j in range(6)]
    W.dec = [p.sb("w_dec%d" % j, [128, TG], BF16) for j in range(8)]
    W.wts = [p.sb("w_wts%d" % j, [128, TG], BF16) for j in range(8)]
    W.n["d2"] = 0
    W.n["w2"] = 0
    t1 = p.sb("s_t1", [128, NB, 8], F32)
    l8 = p.sb("s_l8", [128, NB, 8], F32)
    vals = p.sb("s_vals", [128, NB, 8], F32)
    cpos = p.sb("s_cpos", [128, NB, 8], F32)
    vsum = p.sb("s_vsum", [128, 8], F32)
    ea = p.sb("s_ea", [128, 4], F32)
    p.dve(lambda e: e.tensor_tensor(t1.t[:], fd.t[:], svec.t[:, 0:8].unsqueeze(1).to_broadcast([128, NB, 8]), ALU.add),
          r=[fd, svec], w=[t1])
    p.act(lambda e: e.activation(t1.t[:, :, 0:4], t1.t[:, :, 0:4], AF.Exp, scale=-1.0), r=[t1], w=[t1])
    p.act(lambda e: e.activation(t1.t[:, :, 4:8], t1.t[:, :, 4:8], AF.Exp), r=[t1], w=[t1])
    p.act(lambda e: e.activation(l8.t[:], t1.t[:], AF.Ln, bias=W.onecol.t[:, 0:1]), r=[t1, W.onecol], w=[l8])
    p.act(lambda e: e.activation(ea.t[:], svec.t[:, 8:12], AF.Exp), r=[svec], w=[ea])
    p.dve(lambda e: e.tensor_copy(vals.t[:, :, 0:4], l8.t[:, :, 0:4]), r=[l8], w=[vals])
    p.dve(lambda e: e.tensor_tensor(vals.t[:, :, 4:8], l8.t[:, :, 4:8],
                                    ea.t[:].unsqueeze(1).to_broadcast([128, NB, 4]), ALU.mult), r=[l8, ea], w=[vals])
    for blk in range(NB):
        mm(p, W.misc.t[:, 0:8], tri32.t[:], vals.t[:, blk, :], True, blk == 0, r=[tri32, vals], w=[W.misc])
        if blk > 0:
            mm(p, W.misc.t[:, 0:8], c.ones32.t[:], vsum.t[:], False, True, r=[c.ones32, vsum, W.misc], w=[W.misc])
        p.dve(lambda e, blk=blk: e.tensor_copy(cpos.t[:, blk, :], W.misc.t[:, 0:8]), r=[W.misc], w=[cpos])
        if blk == 0:
            p.dve(lambda e, blk=blk: e.tensor_copy(vsum.t[:], vals.t[:, blk, :]), r=[vals], w=[vsum])
        elif blk < NB - 1:
            p.dve(lambda e, blk=blk: e.tensor_tensor(vsum.t[:], vsum.t[:], vals.t[:, blk, :], ALU.add),
                  r=[vals, vsum], w=[vsum])
    sets = []
    off = 0
    for i in range(2):
        qv = Res("qT%d" % i, arena.t[:, off:off + S]); off += S
        kv = Res("kT%d" % i, arena.t[:, off:off + S]); off += S
        if K128:
            p.pool(lambda e, qv=qv: e.memset(qv.t[64:128, :], 0.0), w=[qv])
            p.pool(lambda e, kv=kv: e.memset(kv.t[64:128, :], 0.0), w=[kv])
        vv = Res("vt%d" % i, arena.t[:, off:off + NB * 65 + 64]); off += NB * 65 + 64
        sets.append((qv, kv, vv))
    assert off <= ARENA_ELEMS
    ot_res = p.dram("OT")

    def loads(e):
        qv, kv, vv = sets[e % 2]
        p.dma("sp", qv.t[0:64, :], fm[e * 64:e * 64 + 64, :], r=[fm_res], w=[qv])
        p.dma("sp", kv.t[0:64, :], fm[256 + e * 64:256 + e * 64 + 64, :], r=[fm_res], w=[kv])
        p.dma("sp", vv.t[:, 0:NB * 65], vd[e], r=[v_res], w=[vv])

    loads(0)
    for e in range(4):
        if e + 1 < 4:
            loads(e + 1)
        qv, kv, vv = sets[e % 2]
        attn_fox(p, c, W, S=S, kT=kv.t[0:KQ, :], qT=qv.t[0:KQ, :], qk_res=(qv, kv), vt=vv, cpos=cpos, ci=e,
                 ot_dst=OT[e * 64:(e + 1) * 64, :], ot_res=ot_res)
    p.barrier()
    BT = Res("BT", arena.t[:, 0:S])
    CT = Res("CT", arena.t[:, S:2 * S])
    xdt = Res("xdt", arena.t[:, 2 * S:2 * S + NB * 256].rearrange("p (b e d) -> p b e d", e=4, d=64))
    xdt_flat = arena.t[:, 2 * S:2 * S + NB * 256 + 64]
    xst = Res("xst", arena.t[0:64, 2 * S + NB * 256 + 64:3 * S + NB * 256 + 64])
    assert 3 * S + NB * 256 + 64 <= ARENA_ELEMS
    p.dma("sp", BT.t, fm[512:640, :], r=[fm_res], w=[BT])
    p.dma("sp", CT.t, fm[640:768, :], r=[fm_res], w=[CT])
    mbf = W.misc.t[:].bitcast(BF16)
    for e in range(4):
        p.dma("sp", xst.t, fm[1024 + 64 * e:1024 + 64 * e + 64, :], r=[fm_res], w=[xst])
        for blk in range(NB):
            p.pe(lambda e_, blk=blk: e_.transpose(mbf[:, 0:64], xst.t[:, blk * 128:(blk + 1) * 128], ident_bf.t[0:64, 0:64]),
                 r=[xst, ident_bf], w=[W.misc])
            p.dve(lambda e_, blk=blk, e=e: e_.tensor_scalar(
                xdt.t[:, blk, e, :], mbf[:, 0:64], l8.t[:, blk, 4 + e:5 + e], None, ALU.mult), r=[W.misc, l8], w=[xdt])
    ygs = [p.sb("w_yg%d" % i, [128, TG], F32) for i in range(4)]
    attn_ssd(p, c, W, S=S, BT=BT.t, CT=CT.t, bc_res=(BT, CT), xdt=xdt, xdt_flat=xdt_flat, cpos=cpos, fm=fm, fm_res=fm_res,
             zrow=768, xrow=1024, svec=svec, ygs=ygs, ot_dst=OT[256:512, :], ot_res=ot_res)
    p.mark_outputs(ot_res)
    p.emit()
    return nc


WCAST = (("w_out", 2048, 2048), ("w_up", 2048, 8192), ("w_down", 8192, 2048))


def build_p0(NT, D, wcast=True):
    KC = D // 128
    nc = bass.Bass("TRN2", target_bir_lowering=False)
    x = nc.dram_tensor("x", [NT, D], F32, kind="ExternalInput").ap()
    cst = nc.dram_tensor("consts", [128, 384], F32, kind="ExternalInput").ap()
    nvd = nc.dram_tensor("nv", [128, 8 * KC], F32, kind="ExternalInput").ap()
    xT = nc.dram_tensor("xT", [D, NT], F32, kind="ExternalOutput").ap()
    hT = nc.dram_tensor("hT", [D, NT], BF16, kind="ExternalOutput").ap()
    p = Prog(nc)
    c = make_common(p, cst)
    nv = p.sb("nv_sb", [128, 8 * KC], F32)
    p.dma("sp", nv.t[:], nvd, w=[nv])
    phase_c(p, c, D=D, DFF=4 * D, F=D, NT=NT, OT_src=None, OT_res=None, xT=xT, xT_res=p.dram("xT"), w_out=None,
            w_up=None, w_down=None, nv=nv, v_post=0, v_pre=0, v_mpost=0, v_next=0,
            hT_dst=lambda t0: hT[:, t0:t0 + TG], hT_res=p.dram("hT"),
            x_src=lambda t0: x[t0:t0 + TG, :], x_res=p.dram("x"), nwb=1)
    p.mark_outputs(p.dram("xT"))
    p.mark_outputs(p.dram("hT"))
    if wcast:
        stg = [p.sb("wc_stg%d" % i, [128, 8192], BF16) for i in range(2)]
        k = 0
        for l in range(2):
            for name, rows, cols in WCAST:
                rs = rows // NCORES
                src = nc.dram_tensor("%s%d_f32" % (name, l), [rs, cols], F32, kind="ExternalInput").ap()
                dst = nc.dram_tensor("%s%d_bf" % (name, l), [rs, cols], BF16, kind="ExternalOutput").ap()
                sv = src.rearrange("(a p) c -> p a c", p=128)
                dv = dst.rearrange("(a p) c -> p a c", p=128)
                na = rs // 128
                per = max(1, 8192 // cols)
                for a0 in range(0, na, per):
                    a1 = min(na, a0 + per)
                    t = stg[k % 2]
                    k += 1
                    tv = t.t[:, 0:(a1 - a0) * cols].rearrange("p (a c) -> p a c", c=cols)
                    p.dma("pool", tv, sv[:, a0:a1, :], w=[t])
                    p.dma("sp", dv[:, a0:a1, :], tv, r=[t], w=[p.dram("wcast")], out=True)
    p.emit()
    return nc


def build_c(NT, D, DFF, layer, last, wbf=True):
    KC = D // 128
    nc = bass.Bass("TRN2", target_bir_lowering=False)
    OT = nc.dram_tensor("OT", [D, NT], BF16, kind="ExternalInput").ap()
    xT = nc.dram_tensor("xT", [D, NT], F32, kind="ExternalInput").ap()
    cst = nc.dram_tensor("consts", [128, 384], F32, kind="ExternalInput").ap()
    nvd = nc.dram_tensor("nv", [128, 8 * KC], F32, kind="ExternalInput").ap()
    WDT = BF16 if wbf else F32
    w_out = nc.dram_tensor("w_out", [D, D], WDT, kind="ExternalInput").ap()
    w_up = nc.dram_tensor("w_up", [D, DFF], WDT, kind="ExternalInput").ap()
    w_down = nc.dram_tensor("w_down", [DFF, D], WDT, kind="ExternalInput").ap()
    p = Prog(nc)
    c = make_common(p, cst)
    nv = p.sb("nv_sb", [128, 8 * KC], F32)
    p.dma("sp", nv.t[:], nvd, w=[nv])
    V = lambda i: i * KC
    b = 4 * layer
    kw = dict(D=D, DFF=DFF, F=D, NT=NT, OT_src=lambda t0: OT[:, t0:t0 + TG], OT_res=p.dram("OT"),
              xT=xT, xT_res=p.dram("xT"), w_out=w_out, w_up=w_up, w_down=w_down, nv=nv,
              v_post=V(b + 1), v_pre=V(b + 2), v_mpost=V(b + 3), v_next=V((b + 4) % 8),
              wq="pool", nwb=5)
    if last:
        out = nc.dram_tensor("out", [NT, D], F32, kind="ExternalOutput").ap()
        phase_c(p, c, out_dst=lambda t0: out[t0:t0 + TG, :], out_res=p.dram("out"), **kw)
    else:
        xTo = nc.dram_tensor("xTo", [D, NT], F32, kind="ExternalOutput").ap()
        hT = nc.dram_tensor("hT", [D, NT], BF16, kind="ExternalOutput").ap()
        phase_c(p, c, hT_dst=lambda t0: hT[:, t0:t0 + TG], hT_res=p.dram("hT"),
                xT_out=xTo, xT_out_res=p.dram("xTo"), **kw)
        p.mark_outputs(p.dram("xTo"))
        p.mark_outputs(p.dram("hT"))
    p.emit()
    return nc


def _c32(a):
    return np.ascontiguousarray(np.asarray(a, np.float32))


def kernel(x, mix_norm_pre, mix_norm_post, mlp_norm_pre, mlp_norm_post, ab_w_in, ab_w_out,
           cd_w_in, cd_b_f, cd_conv_w, cd_conv_b, cd_dt_bias, cd_a_log, cd_d_skip,
           cd_gate_norm, cd_w_out, mlp_w_up, mlp_w_down):
    x = np.asarray(x, np.float32)
    B, S, D = x.shape
    G = NCORES // B
    NT = S // G
    DFF = mlp_w_up.shape[-1]
    cores = list(range(NCORES))
    xs = x.reshape(NCORES, NT, D)
    vecs = []
    for l in range(2):
        vecs += [mix_norm_pre[l], mix_norm_post[l], mlp_norm_pre[l], mlp_norm_post[l]]
    nv = _nv_layout(vecs)
    consts = _consts()
    aconst = _aconst()
    masks = _mask_strips()

    def gather_h(res, key):
        return [np.ascontiguousarray(np.concatenate([res[b * G + g][key] for g in range(G)], axis=1)) for b in range(B)]

    def scatter_ot(res):
        outs = []
        for b in range(B):
            full = np.concatenate([res[b * G + g]["OT"][0:256] for g in range(G)]
                                  + [res[b * G + g]["OT"][256:512] for g in range(G)], axis=0)
            for g in range(G):
                outs.append(np.ascontiguousarray(full[:, g * NT:(g + 1) * NT]))
        return outs

    wsrc = {"w_out0": ab_w_out[0], "w_out1": cd_w_out[0], "w_up0": mlp_w_up[0], "w_up1": mlp_w_up[1],
            "w_down0": mlp_w_down[0], "w_down1": mlp_w_down[1]}
    im = []
    for i in cores:
        d = dict(x=np.ascontiguousarray(xs[i]), consts=consts, nv=nv)
        for k, wfull in wsrc.items():
            rs = wfull.shape[0] // NCORES
            d[k + "_f32"] = _c32(wfull[i * rs:(i + 1) * rs])
        im.append(d)
    r1 = run_bass_kernel_spmd(build_p0(NT, D), im, core_ids=cores).results
    wbf = {k: np.ascontiguousarray(np.concatenate([r1[i][k + "_bf"] for i in cores], axis=0)) for k in wsrc}
    xT = [r1[i]["xT"] for i in cores]
    hT_all = gather_h(r1, "hT")
    W0 = np.asarray(ab_w_in[0], np.float32)
    sec = [W0[:, i * 1024:(i + 1) * 1024] for i in range(6)]
    im = []
    for i in cores:
        b, g = divmod(i, G)
        sl = slice(256 * g, 256 * g + 256)
        im.append(dict(hT_all=hT_all[b], consts=consts, aconst=aconst, masks=masks,
                       w_fm=np.ascontiguousarray(np.concatenate([sec[0][:, sl], sec[1][:, sl], sec[3][:, sl], sec[4][:, sl]], axis=1)),
                       w_tm=np.ascontiguousarray(np.concatenate([sec[2][:, sl], sec[5][:, sl]], axis=1)),
                       rd=np.stack([_dilated_strip(4 * g + e) for e in range(4)])))
    r2 = run_bass_kernel_spmd(build_ab0(S, D), im, core_ids=cores).results
    ot = scatter_ot(r2)
    r3 = run_bass_kernel_spmd(build_c(NT, D, DFF, 0, False),
                              [dict(OT=ot[i], xT=xT[i], consts=consts, nv=nv, w_out=wbf["w_out0"],
                                    w_up=wbf["w_up0"], w_down=wbf["w_down0"]) for i in cores],
                              core_ids=cores).results
    xT = [r3[i]["xTo"] for i in cores]
    hT_all = gather_h(r3, "hT")
    W1 = np.asarray(cd_w_in[0], np.float32)
    qc, kc, vc = W1[:, 0:1024], W1[:, 1024:2048], W1[:, 2048:3072]
    fr, zz = W1[:, 3072:3088], W1[:, 3088:4112]
    xsw, Bw, Cw, dtw = W1[:, 4112:5136], W1[:, 5136:5648], W1[:, 5648:6160], W1[:, 6160:6176]
    cw = np.asarray(cd_conv_w[0], np.float32)
    cb = np.asarray(cd_conv_b[0], np.float32)
    im = []
    for i in cores:
        b, g = divmod(i, G)
        sl = slice(256 * g, 256 * g + 256)
        s128 = slice(128 * g, 128 * g + 128)
        s4 = slice(4 * g, 4 * g + 4)
        convw = np.zeros((128, 4, 5), np.float32)

        def cwl(ch0, n):
            return np.concatenate([cw[:, ch0:ch0 + n].T, cb[ch0:ch0 + n, None]], axis=1)
        convw[:, 0] = cwl(1024 + 128 * g, 128)
        convw[:, 1] = cwl(1536 + 128 * g, 128)
        svec = np.zeros((128, 32), np.float32)
        svec[:, 0:4] = np.asarray(cd_b_f[0], np.float32)[s4]
        svec[:, 4:8] = np.asarray(cd_dt_bias[0], np.float32)[s4]
        svec[:, 8:12] = np.asarray(cd_a_log[0], np.float32)[s4]
        svec[:, 12:16] = np.asarray(cd_d_skip[0], np.float32)[s4]
        for j in range(2):
            convw[:, 2 + j] = cwl(256 * g + 128 * j, 128)
        for e in range(4):
            svec[0:64, 16 + e] = np.asarray(cd_gate_norm[0], np.float32)[256 * g + 64 * e:256 * g + 64 * e + 64]
        im.append(dict(hT_all=hT_all[b], consts=consts, aconst=aconst, masks=masks,
                       w_fm=np.ascontiguousarray(np.concatenate([qc[:, sl], kc[:, sl], Bw[:, s128], Cw[:, s128],
                                                                 zz[:, sl], xsw[:, sl]], axis=1)),
                       w_tm=np.ascontiguousarray(np.concatenate([vc[:, sl], fr[:, s4], dtw[:, s4]], axis=1)),
                       convw=np.ascontiguousarray(convw.reshape(128, 20)), svec=svec))
    r4 = run_bass_kernel_spmd(build_ab1(S, D), im, core_ids=cores).results
    ot = scatter_ot(r4)
    r5 = run_bass_kernel_spmd(build_c(NT, D, DFF, 1, True),
                              [dict(OT=ot[i], xT=xT[i], consts=consts, nv=nv, w_out=wbf["w_out1"],
                                    w_up=wbf["w_up1"], w_down=wbf["w_down1"]) for i in cores],
                              core_ids=cores).results
    out = np.stack([np.asarray(r5[i]["out"], np.float32) for i in cores], axis=0)
    return out.reshape(B, S, D)
```

```python
import os as _os
import numpy as np
import concourse.bass as bass
import concourse.mybir as mybir
from concourse.bass_utils import run_bass_kernel_spmd

F32 = mybir.dt.float32
BF16 = mybir.dt.bfloat16
AF = mybir.ActivationFunctionType
ALU = mybir.AluOpType
AX = mybir.AxisListType

ENGS = ("pe", "act", "dve", "pool", "sp")
SAME_ENGINE_SYNC = True


class Res:
    __slots__ = ("name", "t", "lw", "rd", "excl")

    def __init__(self, name, t=None, excl=False):
        self.name = name
        self.t = t
        self.excl = excl
        self.lw = None
        self.rd = []


class Op:
    __slots__ = ("eng", "fn", "deps", "dma", "chan", "cval", "signal", "sval", "out", "wres", "hard")

    def __init__(self, eng, fn, dma=False):
        self.eng = eng
        self.fn = fn
        self.deps = set()
        self.hard = set()
        self.dma = dma
        self.chan = None
        self.cval = 0
        self.signal = False
        self.sval = 0
        self.out = False


class Prog:
    def __init__(self, nc, n_dma_chan=24):
        self.nc = nc
        self.ops = []
        self.ctx = []
        self.drams = {}
        self.n_dma_chan = n_dma_chan
        self.chan_last = {}
        self.chan_cnt = {}
        self.rr = 0
        self.rrq = {}
        self._names = 0
        self.bar = set()
        self.last_eng = {}

    def _enter(self, guard):
        t = guard.__enter__()
        self.ctx.append(guard)
        return t

    def sb(self, name, shape, dt):
        return Res(name, self._enter(self.nc.sbuf_tensor(name, list(shape), dt)))

    def ps(self, name, shape, dt):
        return Res(name, self._enter(self.nc.psum_tensor(name, list(shape), dt)), excl=True)

    def dram(self, name):
        if name not in self.drams:
            self.drams[name] = Res(name)
        return self.drams[name]

    def res(self, name):
        return Res(name)

    def _add(self, op, r, w):
        i = len(self.ops)
        r = list(r)
        w = list(w)
        for x in r:
            if x.excl and x not in w:
                w.append(x)
        for x in r:
            if x.lw is not None:
                op.deps.add(x.lw)
                op.hard.add(x.lw)
        for x in w:
            if x.lw is not None:
                op.deps.add(x.lw)
                op.hard.add(x.lw)
            for j in x.rd:
                op.deps.add(j)
        for x in r:
            x.rd.append(i)
        for x in w:
            x.lw = i
            x.rd = []
        op.deps |= self.bar
        op.deps.discard(i)
        self.ops.append(op)
        self.last_eng[op.eng] = i
        return i

    def mark_outputs(self, res):
        for op in self.ops:
            if op.dma and res in getattr(op, "wres", ()):
                op.out = True

    def barrier(self):
        self.bar = set(self.last_eng.values()) | set(self.chan_last.values())

    def scope_mark(self):
        return len(self.ctx)

    def scope_free(self, mark):
        self.barrier()
        while len(self.ctx) > mark:
            self.ctx.pop().__exit__(None, None, None)

    def pe(self, fn, r=(), w=()):
        return self._add(Op("pe", fn), r, w)

    def act(self, fn, r=(), w=()):
        return self._add(Op("act", fn), r, w)

    def dve(self, fn, r=(), w=()):
        return self._add(Op("dve", fn), r, w)

    def pool(self, fn, r=(), w=()):
        return self._add(Op("pool", fn), r, w)

    def dma(self, q, out_ap, in_ap, r=(), w=(), out=False, chan=None, **kw):
        op = Op(q, lambda e: e.dma_start(out=out_ap, in_=in_ap, **kw), dma=True)
        if chan is None:
            nch = 8 if q == "pool" else self.n_dma_chan
            k = self.rrq.get(q, 0)
            self.rrq[q] = k + 1
            chan = (0 if q == "pool" else 1) * 100 + (k % nch)
        op.chan = chan
        op.out = out
        op.wres = tuple(w)
        prev = self.chan_last.get(chan)
        if prev is not None:
            op.deps.add(prev)
        self.chan_cnt[chan] = self.chan_cnt.get(chan, 0) + 16
        op.cval = self.chan_cnt[chan]
        i = self._add(op, r, w)
        self.chan_last[chan] = i
        return i

    def coll(self, kind, in_ap, out_ap, groups, r=(), w=()):
        op = Op("pool", lambda e: e.collective_compute(kind, ALU.bypass, replica_groups=groups,
                                                        ins=[in_ap], outs=[out_ap]), dma=True)
        k = self.rrq.get("coll", 0)
        self.rrq["coll"] = k + 1
        chan = 200 + (k % 4)
        op.chan = chan
        op.wres = tuple(w)
        prev = self.chan_last.get(chan)
        if prev is not None:
            op.deps.add(prev)
        self.chan_cnt[chan] = self.chan_cnt.get(chan, 0) + 16
        op.cval = self.chan_cnt[chan]
        i = self._add(op, r, w)
        self.chan_last[chan] = i
        return i

    def emit(self):
        nc = self.nc
        ops = self.ops
        def needs_sem(p, op, d=None):
            if p.dma:
                return False
            if op.dma:
                return True
            if p.eng != op.eng:
                return True
            if p.eng == "pe":
                return False
            if d is not None and d not in op.hard:
                return False
            return SAME_ENGINE_SYNC

        for i, op in enumerate(ops):
            for d in op.deps:
                if needs_sem(ops[d], op, d):
                    ops[d].signal = True
        cnt = {e: 0 for e in ENGS}
        for op in ops:
            if not op.dma and op.signal:
                cnt[op.eng] += 1
                op.sval = cnt[op.eng]
        sems = {}
        for e in ENGS:
            sems[e] = self._enter(nc.semaphore("s_" + e))
        csems = {}
        for c in sorted(self.chan_cnt):
            csems[c] = self._enter(nc.semaphore("c_%d" % c))
        final_waits = [(csems[op.chan], op.cval) for op in ops if op.dma and op.out]
        per_eng = {e: [] for e in ENGS}
        for i, op in enumerate(ops):
            per_eng[op.eng].append(i)

        def run(eng_name, eng):
            waited = {}
            for i in per_eng[eng_name]:
                op = ops[i]
                need = {}
                for d in op.deps:
                    p = ops[d]
                    if p.dma:
                        key, val = ("c", p.chan), p.cval
                    else:
                        if not needs_sem(p, op, d):
                            continue
                        key, val = ("e", p.eng), p.sval
                    if need.get(key, 0) < val:
                        need[key] = val
                for key, val in need.items():
                    if waited.get(key, 0) >= val:
                        continue
                    waited[key] = val
                    s = csems[key[1]] if key[0] == "c" else sems[key[1]]
                    eng.wait_ge(s, val)
                ins = op.fn(eng)
                if op.dma:
                    ins.then_inc(csems[op.chan], 16)
                elif op.signal:
                    ins.then_inc(sems[eng_name], 1)
            if eng_name == "sp":
                done = {}
                for s, v in final_waits:
                    k = id(s)
                    if k not in done or done[k][1] < v:
                        done[k] = (s, v)
                for s, v in done.values():
                    eng.wait_ge(s, v)

        with nc.Block() as block:
            @block.sync
            def _(e):
                run("sp", e)

            @block.scalar
            def _(e):
                run("act", e)

            @block.vector
            def _(e):
                run("dve", e)

            @block.gpsimd
            def _(e):
                run("pool", e)

            @block.tensor
            def _(e):
                run("pe", e)
        for g in reversed(self.ctx):
            g.__exit__(None, None, None)
        self.ctx = []


D_MODEL = 2048
D_FF = 8192
SEQ = 8192
BATCH = 2
NCORES = 8
EPS = 1e-6
TG = 512


class Ctx:
    pass


def make_common(p, consts_ap):
    c = Ctx()
    c.banks = [p.ps("bank%d" % i, [128, 512], F32) for i in range(8)]
    c.bi = 0
    c.ident = p.sb("ident", [128, 128], F32)
    c.ones_bf = p.sb("ones_bf", [128, 128], BF16)
    c.ones32 = p.sb("ones32", [128, 128], F32)
    p.dma("sp", c.ident.t[:], consts_ap[:, 0:128], w=[c.ident])
    p.dma("sp", c.ones32.t[:], consts_ap[:, 128:256], w=[c.ones32])
    p.dve(lambda e: e.tensor_copy(c.ones_bf.t[:], c.ones32.t[:]), r=[c.ones32], w=[c.ones_bf])
    c.eps = p.sb("eps_col", [128, 1], F32)
    p.dve(lambda e: e.memset(c.eps.t[:], EPS), w=[c.eps])
    return c


def next_bank(c, n=5):
    b = c.banks[c.bi % n]
    c.bi += 1
    return b


def phase_c(p, c, *, D, DFF, F, NT, OT_src, OT_res, xT, xT_res, w_out, w_up, w_down, nv,
            v_post, v_pre, v_mpost, v_next, hT_dst=None, hT_res=None, out_dst=None, out_res=None,
            pfx="c", stop=99, x_src=None, x_res=None, bufs=None, xT_out=None, xT_out_res=None, wq="pool", nwb=3):
    if xT_out is None:
        xT_out, xT_out_res = xT, xT_res
    KC = D // 128
    FKC = F // 128
    FC = DFF // 128
    NH = 2
    FCH = FC // NH
    UW = min(4, FCH)
    DW = min(2, KC)
    OW = min(4, KC)
    WELEMS = max(FKC * OW * 128, KC * UW * 128, FCH * DW * 128)
    if bufs is not None and "xg" in bufs:
        xg, ag, mg, ug, sq, tmp, rbc, wb = (bufs[k] for k in ("xg", "ag", "mg", "ug", "sq", "tmp", "rbc", "wb"))
        NWB = len(wb)
    else:
        xg = p.sb(pfx + "xg", [128, KC, TG], F32)
        ag = p.sb(pfx + "ag", [128, max(KC, FKC), TG], BF16)
        mg = p.sb(pfx + "mg", [128, KC, TG], F32)
        ug = p.sb(pfx + "ug", [128, FCH, TG], BF16)
        sq = [p.sb(pfx + "sq%d" % i, [128, TG], BF16) for i in range(2)]
        tmp = [p.sb(pfx + "tmp%d" % i, [128, TG], F32) for i in range(2)]
        rbc = p.sb(pfx + "rbc", [128, TG], F32)
        NWB = nwb
        wb = [p.sb(pfx + "wb%d" % i, [128, WELEMS], BF16) for i in range(NWB)]
        if bufs is not None:
            bufs.update(xg=xg, ag=ag, mg=mg, ug=ug, sq=sq, tmp=tmp, rbc=rbc, wb=wb)
    st = {"w": 0, "s": 0, "t": 0}
    stat_bank = c.banks[5]
    misc_bank = c.banks[7]

    def wbuf():
        b = wb[st["w"] % NWB]
        st["w"] += 1
        return b

    def stats_accum(src_ap, src_res, first, last):
        s = sq[st["s"] % 2]
        st["s"] += 1
        p.act(lambda e: e.activation(s.t[:], src_ap, AF.Square), r=src_res, w=[s])
        p.pe(lambda e: e.matmul(stat_bank.t[:], c.ones_bf.t[:], s.t[:], start=first, stop=last),
             r=[s, c.ones_bf] + ([] if first else [stat_bank]), w=[stat_bank])

    def rstd_from_stats():
        p.dve(lambda e: e.tensor_scalar(rbc.t[:], stat_bank.t[:], 1.0 / D, None, ALU.mult),
              r=[stat_bank], w=[rbc])
        p.dve(lambda e: e.tensor_scalar(rbc.t[:], rbc.t[:], EPS, None, ALU.add),
              r=[rbc], w=[rbc])
        p.act(lambda e: e.activation(rbc.t[:], rbc.t[:], AF.Ln), r=[rbc], w=[rbc])
        p.act(lambda e: e.activation(rbc.t[:], rbc.t[:], AF.Exp, scale=-0.5), r=[rbc], w=[rbc])

    def residual_update(vcol):
        for cc in range(KC):
            t = tmp[st["t"] % 2]
            st["t"] += 1
            p.dve(lambda e, cc=cc, t=t: e.scalar_tensor_tensor(
                t.t[:], mg.t[:, cc, :], nv.t[:, vcol + cc:vcol + cc + 1], rbc.t[:], ALU.mult, ALU.mult),
                r=[mg, nv, rbc], w=[t])
            p.pool(lambda e, cc=cc, t=t: e.tensor_tensor(xg.t[:, cc, :], xg.t[:, cc, :], t.t[:], ALU.add),
                   r=[xg, t], w=[xg])
            stats_accum(xg.t[:, cc, :], [xg], cc == 0, cc == KC - 1)

    def norm_to_ag(vcol):
        rstd_from_stats()
        for cc in range(KC):
            p.dve(lambda e, cc=cc: e.scalar_tensor_tensor(
                ag.t[:, cc, :], xg.t[:, cc, :], nv.t[:, vcol + cc:vcol + cc + 1], rbc.t[:], ALU.mult, ALU.mult),
                r=[xg, nv, rbc], w=[ag])

    for g in range(NT // TG):
        t0 = g * TG
        if x_src is not None:
            ov = mg.t[:].rearrange("p a b -> p (a b)").rearrange("p (tt d) -> p tt d", d=D)
            p.dma("sp", ov, x_src(t0).rearrange("(tt p) d -> p tt d", p=128), r=[x_res], w=[mg])
            for tt in range(TG // 128):
                for k4 in range(max(1, KC // 4)):
                    nk = min(4, KC)
                    bk = next_bank(c)
                    for j in range(nk):
                        kc = k4 * 4 + j
                        p.pe(lambda e, kc=kc, j=j, tt=tt, bk=bk: e.transpose(
                            bk.t[:, j * 128:(j + 1) * 128], ov[:, tt, kc * 128:(kc + 1) * 128], c.ident.t[:]),
                            r=[mg, c.ident], w=[bk])
                    p.act(lambda e, tt=tt, k4=k4, nk=nk, bk=bk: e.copy(
                        xg.t[:, k4 * 4:k4 * 4 + nk, tt * 128:(tt + 1) * 128],
                        bk.t[:, 0:nk * 128].rearrange("p (a b) -> p a b", a=nk)), r=[bk], w=[xg])
            for cc in range(KC):
                stats_accum(xg.t[:, cc, :], [xg], cc == 0, cc == KC - 1)
            norm_to_ag(v_next)
            p.dma("sp", hT_dst(t0).rearrange("(kc p) t -> p kc t", p=128), ag.t[:, 0:KC, :], r=[ag], w=[hT_res])
            p.dma("sp", xT_out[:, t0:t0 + TG].rearrange("(kc p) t -> p kc t", p=128), xg.t[:], r=[xg], w=[xT_out_res])
            continue
        p.dma("sp", xg.t[:], xT[:, t0:t0 + TG].rearrange("(kc p) t -> p kc t", p=128), r=[xT_res], w=[xg])
        p.dma("sp", ag.t[:, 0:FKC, :], OT_src(t0).rearrange("(kc p) t -> p kc t", p=128), r=[OT_res], w=[ag])
        if stop < 1:
            return
        for q in range(KC // OW):
            wt = wbuf()
            wv = wt.t[:, 0:FKC * OW * 128].rearrange("p (kc c) -> p kc c", kc=FKC)
            p.dma(wq, wv, w_out[:, q * OW * 128:(q + 1) * OW * 128].rearrange("(kc p) c -> p kc c", p=128),
                  w=[wt])
            for j in range(OW):
                cc = q * OW + j
                bk = next_bank(c)
                for kc in range(FKC):
                    p.pe(lambda e, kc=kc, j=j, bk=bk, wv=wv: e.matmul(
                        bk.t[:], wv[:, kc, j * 128:(j + 1) * 128], ag.t[:, kc, :], start=(kc == 0), stop=(kc == FKC - 1)),
                        r=[wt, ag] + ([] if kc == 0 else [bk]), w=[bk])
                p.dve(lambda e, cc=cc, bk=bk: e.tensor_copy(mg.t[:, cc, :], bk.t[:]), r=[bk], w=[mg])
                stats_accum(bk.t[:], [bk], cc == 0, cc == KC - 1)
        if stop < 2:
            return
        rstd_from_stats()
        if stop < 3:
            return
        residual_update(v_post)
        if stop < 4:
            return
        norm_to_ag(v_pre)
        if stop < 5:
            return
        for h in range(NH):
            for q in range(FCH // UW):
                wt = wbuf()
                wv = wt.t[:, 0:KC * UW * 128].rearrange("p (kc c) -> p kc c", kc=KC)
                f0 = (h * FCH + q * UW) * 128
                p.dma(wq, wv, w_up[:, f0:f0 + UW * 128].rearrange("(kc p) c -> p kc c", p=128), w=[wt])
                for j in range(UW):
                    fl = q * UW + j
                    bk = next_bank(c)
                    for kc in range(KC):
                        p.pe(lambda e, kc=kc, j=j, bk=bk, wv=wv: e.matmul(
                            bk.t[:], wv[:, kc, j * 128:(j + 1) * 128], ag.t[:, kc, :], start=(kc == 0), stop=(kc == KC - 1)),
                            r=[wt, ag] + ([] if kc == 0 else [bk]), w=[bk])
                    t = tmp[st["t"] % 2]
                    st["t"] += 1
                    p.act(lambda e, bk=bk, t=t: e.activation(t.t[:], bk.t[:], AF.Relu), r=[bk], w=[t])
                    p.dve(lambda e, fl=fl, t=t: e.tensor_tensor(ug.t[:, fl, :], t.t[:], t.t[:], ALU.mult), r=[t], w=[ug])
            for q in range(KC // DW):
                wt = wbuf()
                wv = wt.t[:, 0:FCH * DW * 128].rearrange("p (fc c) -> p fc c", fc=FCH)
                r0 = h * FCH * 128
                p.dma(wq, wv, w_down[r0:r0 + FCH * 128, q * DW * 128:(q + 1) * DW * 128].rearrange(
                    "(fc p) c -> p fc c", p=128), w=[wt])
                for j in range(DW):
                    cc = q * DW + j
                    bk = next_bank(c)
                    for fc in range(FCH):
                        p.pe(lambda e, fc=fc, j=j, bk=bk, wv=wv: e.matmul(
                            bk.t[:], wv[:, fc, j * 128:(j + 1) * 128], ug.t[:, fc, :], start=(fc == 0), stop=(fc == FCH - 1)),
                            r=[wt, ug] + ([] if fc == 0 else [bk]), w=[bk])
                    if h == 0:
                        p.dve(lambda e, cc=cc, bk=bk: e.tensor_copy(mg.t[:, cc, :], bk.t[:]), r=[bk], w=[mg])
                    else:
                        p.dve(lambda e, cc=cc, bk=bk: e.tensor_tensor(mg.t[:, cc, :], mg.t[:, cc, :], bk.t[:], ALU.add),
                              r=[bk, mg], w=[mg])
                        stats_accum(mg.t[:, cc, :], [mg], cc == 0, cc == KC - 1)
        rstd_from_stats()
        residual_update(v_mpost)
        if hT_dst is not None:
            norm_to_ag(v_next)
            p.dma("sp", hT_dst(t0).rearrange("(kc p) t -> p kc t", p=128), ag.t[:, 0:KC, :], r=[ag], w=[hT_res])
            p.dma("sp", xT_out[:, t0:t0 + TG].rearrange("(kc p) t -> p kc t", p=128), xg.t[:], r=[xg], w=[xT_out_res])
        if out_dst is not None:
            ov = mg.t[:].rearrange("p a b -> p (a b)").rearrange("p (tt d) -> p tt d", d=D)
            for tt in range(TG // 128):
                for k4 in range(KC // 4 if KC >= 4 else 1):
                    nk = min(4, KC)
                    bk = next_bank(c)
                    for j in range(nk):
                        kc = k4 * 4 + j
                        p.pe(lambda e, kc=kc, j=j, tt=tt, bk=bk: e.transpose(
                            bk.t[:, j * 128:(j + 1) * 128], xg.t[:, kc, tt * 128:(tt + 1) * 128], c.ident.t[:]),
                            r=[xg, c.ident], w=[bk])
                    p.act(lambda e, tt=tt, k4=k4, nk=nk, bk=bk: e.copy(
                        ov[:, tt, k4 * 512:k4 * 512 + nk * 128], bk.t[:, 0:nk * 128]), r=[bk], w=[mg])
            p.dma("sp", out_dst(t0).rearrange("(tt p) d -> p tt d", p=128), ov, r=[mg], w=[out_res], out=True)


def _consts():
    return np.concatenate([np.eye(128, dtype=np.float32), np.ones((128, 128), np.float32),
                           np.zeros((128, 128), np.float32)], axis=1)


def _nv_layout(vecs):
    return np.ascontiguousarray(np.concatenate([np.asarray(v, np.float32).reshape(-1, 128).T for v in vecs], axis=1))


def build_program_skeleton(NT=SEQ * BATCH // NCORES, D=D_MODEL, DFF=D_FF):
    KC = D // 128
    nc = bass.Bass("TRN2", target_bir_lowering=False)
    x = nc.dram_tensor("x", [NT, D], F32, kind="ExternalInput").ap()
    cst = nc.dram_tensor("consts", [128, 384], F32, kind="ExternalInput").ap()
    nvd = nc.dram_tensor("nv", [128, 8 * KC], F32, kind="ExternalInput").ap()
    w_out = [nc.dram_tensor("w_out%d" % l, [D, D], F32, kind="ExternalInput").ap() for l in range(2)]
    w_up = [nc.dram_tensor("w_up%d" % l, [D, DFF], F32, kind="ExternalInput").ap() for l in range(2)]
    w_down = [nc.dram_tensor("w_down%d" % l, [DFF, D], F32, kind="ExternalInput").ap() for l in range(2)]
    out = nc.dram_tensor("out", [NT, D], F32, kind="ExternalOutput").ap()
    xT = nc.dram_tensor("xT_scr", [D, NT], F32).ap()
    hT = [nc.dram_tensor("hT_scr%d" % l, [D, NT], BF16).ap() for l in range(2)]
    p = Prog(nc)
    c = make_common(p, cst)
    nv = p.sb("nv_sb", [128, 8 * KC], F32)
    p.dma("sp", nv.t[:], nvd, w=[nv])
    bufs = {}
    common = dict(D=D, DFF=DFF, F=D, NT=NT, xT=xT, xT_res=p.dram("xT"), nv=nv, bufs=bufs)
    V = lambda i: i * KC
    phase_c(p, c, OT_src=None, OT_res=None, w_out=None, w_up=None, w_down=None,
            v_post=0, v_pre=0, v_mpost=0, v_next=V(0),
            hT_dst=lambda t0: hT[0][:, t0:t0 + TG], hT_res=p.dram("hT0"),
            x_src=lambda t0: x[t0:t0 + TG, :], x_res=p.dram("x"), pfx="c", **common)
    phase_c(p, c, OT_src=lambda t0: hT[0][:, t0:t0 + TG], OT_res=p.dram("hT0"),
            w_out=w_out[0], w_up=w_up[0], w_down=w_down[0],
            v_post=V(1), v_pre=V(2), v_mpost=V(3), v_next=V(4),
            hT_dst=lambda t0: hT[1][:, t0:t0 + TG], hT_res=p.dram("hT1"), pfx="c", **common)
    phase_c(p, c, OT_src=lambda t0: hT[1][:, t0:t0 + TG], OT_res=p.dram("hT1"),
            w_out=w_out[1], w_up=w_up[1], w_down=w_down[1],
            v_post=V(5), v_pre=V(6), v_mpost=V(7), v_next=0,
            out_dst=lambda t0: out[t0:t0 + TG, :], out_res=p.dram("out"), pfx="c", **common)
    p.emit()
    return nc


N_DUMMY = int(_os.environ.get("DUMMY", "0"))


def pe_warm(p, c, W, n=None):
    for _ in range(N_DUMMY if n is None else n):
        p.pe(lambda e: e.matmul(c.banks[6].t[:], c.ones_bf.t[:], W.wbf[0].t[:], start=True, stop=True,
                                skip_group_check=True), r=[], w=[])


def mm(p, out_ap, lhsT_ap, rhs_ap, start, stop, r, w):
    return p.pe(lambda e: e.matmul(out_ap, lhsT_ap, rhs_ap, start=start, stop=stop, skip_group_check=True), r=r, w=w)


def phase_a(p, c, *, S, D, hT_all, hT_res, w_fm, w_tm, NFM, NTM, fm_chunks, fm_dst, fm_res,
            v_dst, v_res, nvh, arena, fd=None, nfd=0, convw=None):
    KC = D // 128
    NG = S // TG
    o1 = KC * NFM
    o2 = o1 + KC * NTM
    o3 = o2 + 2 * KC * TG
    assert o3 <= arena.t.shape[1], (o3, arena.t.shape)
    wfm = Res("a_wfm", arena.t[:, 0:o1].rearrange("p (kc c) -> p kc c", kc=KC))
    wtm = Res("a_wtm", arena.t[:, o1:o2].rearrange("p (kc c) -> p kc c", kc=KC))
    KP = min(4, KC)
    wfm_parts = [Res("a_wfm_p%d" % i, wfm.t) for i in range(KC // KP)]
    wtm_parts = [Res("a_wtm_p%d" % i, wtm.t) for i in range(KC // KP)]
    for i in range(KC // KP):
        p.dma("pool", wfm.t[:, i * KP:(i + 1) * KP, :],
              w_fm[i * KP * 128:(i + 1) * KP * 128, :].rearrange("(kc p) c -> p kc c", p=128), w=[wfm_parts[i]])
        p.dma("pool", wtm.t[:, i * KP:(i + 1) * KP, :],
              w_tm[i * KP * 128:(i + 1) * KP * 128, :].rearrange("(kc p) c -> p kc c", p=128), w=[wtm_parts[i]])
    hgs = [Res("a_hg%d" % i, arena.t[:, o2 + i * KC * TG:o2 + (i + 1) * KC * TG].rearrange("p (kc t) -> p kc t", kc=KC))
           for i in range(2)]
    vgs = [p.sb("a_vg%d" % i, [128, TG // 128, nvh, 65], BF16) for i in range(2)]
    ots = [p.sb("a_ot%d" % i, [128, TG], BF16) for i in range(3)]
    nconv = sum(1 for ch in fm_chunks if ch.get("conv") is not None)
    cbs = [p.sb("a_cb%d" % i, [128, TG + 3], F32) for i in range(nconv)]
    accs = [p.sb("a_acc%d" % i, [128, TG], F32) for i in range(2)]
    for cb in cbs:
        p.dve(lambda e, cb=cb: e.memset(cb.t[:], 0.0), w=[cb])
    for vg in vgs:
        p.dve(lambda e, vg=vg: e.memset(vg.t[:], 1.0), w=[vg])
    st = {"o": 0, "a": 0}
    v_view = v_dst.rearrange("h p (b x) -> p b h x", x=65)
    for g in range(NG):
        t0 = g * TG
        hg = hgs[g % 2]
        p.dma("sp", hg.t[:], hT_all[:, t0:t0 + TG].rearrange("(kc p) t -> p kc t", p=128), r=[hT_res], w=[hg])
        for ch in fm_chunks:
            wd, col0, row0 = ch["w"], ch["col"], ch["row"]
            bk = next_bank(c)
            for kc in range(KC):
                mm(p, bk.t[0:wd, :], wfm.t[:, kc, col0:col0 + wd], hg.t[:, kc, :], kc == 0, kc == KC - 1,
                   r=[wfm_parts[kc // KP], hg] + ([] if kc == 0 else [bk]), w=[bk])
            ot = ots[st["o"] % 3]
            st["o"] += 1
            ci = ch.get("conv")
            if ci is None:
                sc = ch.get("scale", 1.0)
                if sc == 1.0:
                    p.act(lambda e, ot=ot, bk=bk, wd=wd: e.copy(ot.t[0:wd, :], bk.t[0:wd, :]), r=[bk], w=[ot])
                else:
                    p.act(lambda e, ot=ot, bk=bk, wd=wd, sc=sc: e.mul(ot.t[0:wd, :], bk.t[0:wd, :], sc), r=[bk], w=[ot])
            else:
                cb = cbs[ci]
                acc = accs[st["a"] % 2]
                st["a"] += 1
                p.dve(lambda e, cb=cb, wd=wd: e.tensor_copy(cb.t[0:wd, 0:3], cb.t[0:wd, TG:TG + 3]), r=[cb], w=[cb])
                p.act(lambda e, cb=cb, bk=bk, wd=wd: e.copy(cb.t[0:wd, 3:TG + 3], bk.t[0:wd, :]), r=[bk], w=[cb])
                p.dve(lambda e, cb=cb, acc=acc, wd=wd, ci=ci: e.tensor_scalar(
                    acc.t[0:wd, :], cb.t[0:wd, 3:TG + 3], convw.t[0:wd, ci, 3:4], convw.t[0:wd, ci, 4:5],
                    ALU.mult, ALU.add), r=[cb, convw], w=[acc])
                for j in (2, 1, 0):
                    p.dve(lambda e, cb=cb, acc=acc, wd=wd, ci=ci, j=j: e.scalar_tensor_tensor(
                        acc.t[0:wd, :], cb.t[0:wd, j:j + TG], convw.t[0:wd, ci, j:j + 1], acc.t[0:wd, :],
                        ALU.mult, ALU.add), r=[cb, convw, acc], w=[acc])
                p.act(lambda e, ot=ot, acc=acc, wd=wd: e.activation(ot.t[0:wd, :], acc.t[0:wd, :], AF.Silu),
                      r=[acc], w=[ot])
            p.dma("sp", fm_dst[row0:row0 + wd, t0:t0 + TG], ot.t[0:wd, :], r=[ot], w=[fm_res])
        vg = vgs[g % 2]
        for tt in range(TG // 128):
            blk = g * (TG // 128) + tt
            bk = next_bank(c)
            for kc in range(KC):
                mm(p, bk.t[:, 0:NTM], hg.t[:, kc, tt * 128:(tt + 1) * 128], wtm.t[:, kc, :], kc == 0, kc == KC - 1,
                   r=[wtm_parts[kc // KP], hg] + ([] if kc == 0 else [bk]), w=[bk])
            p.act(lambda e, tt=tt, bk=bk, vg=vg: e.copy(
                vg.t[:, tt, 0:nvh, 0:64], bk.t[:, 0:nvh * 64].rearrange("p (h d) -> p h d", h=nvh)), r=[bk], w=[vg])
            if fd is not None:
                p.dve(lambda e, blk=blk, bk=bk: e.tensor_copy(fd.t[:, blk, :], bk.t[:, nvh * 64:nvh * 64 + nfd]),
                      r=[bk], w=[fd])
        nb = TG // 128
        for h in range(nvh):
            p.dma("sp", v_view[:, g * nb:(g + 1) * nb, h, :], vg.t[:, :, h, :], r=[vg], w=[v_res])


class AttnWork:
    def __init__(self, p, c, masks_sb, sid=0, zbanks=(0, 1, 2, 3), obanks=(4, 5), misc=7, nt32=3, light=False):
        sfx = "_s%d" % sid
        self.t32 = [p.sb("w_t32_%d%s" % (i, sfx), [128, TG], F32) for i in range(nt32)]
        self.e32 = [p.sb("w_e32_%d%s" % (i, sfx), [128, TG], F32) for i in range(1 if light else 3)]
        self.wbf = [p.sb("w_wbf_%d%s" % (i, sfx), [128, TG], BF16) for i in range(4)]
        self.spb = [p.sb("w_spb_%d%s" % (i, sfx), [128, TG], BF16) for i in range(1 if light else 3)]
        self.accs = [p.sb("w_acc%d%s" % (i, sfx), [128, TG], BF16) for i in range(1 if light else 2)]
        self.osb = [p.sb("w_osb_%d%s" % (i, sfx), [128, TG], F32) for i in range(2)]
        self.obf = [p.sb("w_obf_%d%s" % (i, sfx), [128, TG], BF16) for i in range(2)]
        self.rden = p.sb("w_rden" + sfx, [128, TG], F32)
        self.masks = masks_sb
        self.onecol = p.sb("w_onecol" + sfx, [128, 1], F32)
        p.dve(lambda e: e.memset(self.onecol.t[:], 1.0), w=[self.onecol])
        self.n = {"t": 0, "e": 0, "w": 0, "s": 0, "o": 0, "z": 0, "d": 0, "g": 0}
        self.zb = [c.banks[i] for i in zbanks]
        self.ob = [c.banks[i] for i in obanks]
        self.misc = c.banks[misc]

    def nxt(self, lst, key):
        x = lst[self.n[key] % len(lst)]
        self.n[key] += 1
        return x


def load_masks(p, masks_ap, nmask):
    m = p.sb("w_masks", [128, nmask, 896], F32)
    p.dma("sp", m.t[:], masks_ap.rearrange("p (m x) -> p m x", m=nmask), w=[m])
    return m


def run_streams(gens):
    gens = list(gens)
    while gens:
        for g in list(gens):
            try:
                next(g)
            except StopIteration:
                gens.remove(g)


def chain(*gens):
    for g in gens:
        yield from g


def run_pipeline(n, stages):
    lo = min(o for o, _ in stages)
    hi = max(o for o, _ in stages)
    for s_ in range(-hi, n - lo):
        for o, fn in stages:
            t = s_ + o
            if 0 <= t < n:
                fn(t)


def make_tiles(NG, W, kb_fn):
    T = []
    for qg in range(NG):
        kbs = kb_fn(qg)
        n = len(kbs)
        for idx, kb in enumerate(kbs):
            T.append(dict(qg=qg, q0=qg * TG, idx=idx, n=n, kb=kb, diag=(kb >= 4 * qg), dl=qg * TG - kb * 128,
                          ob=W.ob[qg % len(W.ob)]))
    return T


def attn_sb(p, c, W, *, S, kT, qT, qk_res, vt, negU, negones, ot_dst, ot_res, M_RS=0, M_NEGS=1):
    NG = S // TG
    T = make_tiles(NG, W, lambda qg: list(range(4 * qg + 3, -1, -1)))
    accst = {"acc": None}

    def s_z(i):
        t = T[i]
        t["zb"] = W.zb[i % len(W.zb)]
        kb, q0 = t["kb"], t["q0"]
        mm(p, t["zb"].t[:], kT[:, kb * 128:(kb + 1) * 128], qT[:, q0:q0 + TG], True, False, r=list(qk_res), w=[t["zb"]])

    def s_ln(i):
        t = T[i]
        zb, dl = t["zb"], t["dl"]
        e = W.nxt(W.e32, "e")
        p.act(lambda e_: e_.activation(e.t[:], zb.t[:], AF.Exp), r=[zb], w=[e])
        spb = W.nxt(W.spb, "s")
        t["spb"] = spb
        if t["diag"]:
            tt = W.nxt(W.t32, "t")
            p.act(lambda e_: e_.activation(tt.t[:], e.t[:], AF.Ln, bias=W.onecol.t[:, 0:1]), r=[e, W.onecol], w=[tt])
            p.dve(lambda e_: e_.tensor_tensor(
                spb.t[:], tt.t[:], W.masks.t[:, M_RS, dl + 384:dl + 384 + TG], ALU.mult), r=[tt, W.masks], w=[spb])
        else:
            p.act(lambda e_: e_.activation(spb.t[:], e.t[:], AF.Ln, bias=W.onecol.t[:, 0:1]), r=[e, W.onecol], w=[spb])

    def s_cs(i):
        t = T[i]
        pe_warm(p, c, W)
        zb, spb, idx, n = t["zb"], t["spb"], t["idx"], t["n"]
        mm(p, zb.t[:], negU.t[:], spb.t[:], False, idx == 0, r=[negU, spb, zb], w=[zb])
        acc = accst["acc"]
        if idx > 0:
            mm(p, zb.t[:], negones.t[:], acc.t[:], False, True, r=[negones, acc, zb], w=[zb])
        if idx < n - 1:
            if idx == 0:
                accst["acc"] = spb
            else:
                nacc = W.nxt(W.accs, "g")
                p.pool(lambda e_: e_.tensor_tensor(nacc.t[:], acc.t[:], spb.t[:], ALU.add), r=[spb, acc], w=[nacc])
                accst["acc"] = nacc

    def s_fin(i):
        t = T[i]
        zb, dl = t["zb"], t["dl"]
        wt = W.nxt(W.wbf, "w")
        t["wt"] = wt
        if t["diag"]:
            tt = W.nxt(W.t32, "t")
            p.dve(lambda e_: e_.tensor_tensor(tt.t[:], zb.t[:], W.masks.t[:, M_NEGS, dl + 384:dl + 384 + TG], ALU.add),
                  r=[zb, W.masks], w=[tt])
            p.act(lambda e_: e_.activation(wt.t[:], tt.t[:], AF.Exp), r=[tt], w=[wt])
        else:
            p.act(lambda e_: e_.activation(wt.t[:], zb.t[:], AF.Exp), r=[zb], w=[wt])

    def s_pv(i):
        t = T[i]
        ob, idx, n, q0 = t["ob"], t["idx"], t["n"], t["q0"]
        mm(p, ob.t[:, :], vt.t[:, t["kb"] * 65:t["kb"] * 65 + 128], t["wt"].t[:],
           idx == 0, idx == n - 1, r=[vt, t["wt"]] + ([] if idx == 0 else [ob]), w=[ob])
        if idx == n - 1:
            obf = W.nxt(W.obf, "o")
            p.act(lambda e_: e_.copy(obf.t[0:64, :], ob.t[0:64, :]), r=[ob], w=[obf])
            p.dma("sp", ot_dst[:, q0:q0 + TG], obf.t[0:64, :], r=[obf], w=[ot_res])

    run_pipeline(len(T), [(2, s_z), (1, s_ln), (0, s_cs), (-1, s_fin), (-2, s_pv)])


def norm_epilogue(p, c, W, ob, ot_dst_ap, ot_res):
    osb = W.nxt(W.osb, "o")
    p.dve(lambda e: e.tensor_copy(osb.t[0:65, :], ob.t[0:65, :]), r=[ob], w=[osb])
    p.act(lambda e: e.activation(W.rden.t[64:65, :], osb.t[64:65, :], AF.Ln), r=[osb], w=[W.rden])
    p.act(lambda e: e.activation(W.rden.t[64:65, :], W.rden.t[64:65, :], AF.Exp, scale=-1.0), r=[W.rden], w=[W.rden])
    mm(p, W.misc.t[0:64, :], c.ones32.t[64:65, 0:64], W.rden.t[64:65, :], True, True, r=[c.ones32, W.rden], w=[W.misc])
    obf = W.obf[W.n["o"] % 2]
    p.dve(lambda e: e.tensor_tensor(obf.t[0:64, :], osb.t[0:64, :], W.misc.t[0:64, :], ALU.mult),
          r=[osb, W.misc], w=[obf])
    p.dma("sp", ot_dst_ap, obf.t[0:64, :], r=[obf], w=[ot_res])


def attn_win(p, c, W, *, S, kT, qT, qk_res, vt, rd, ot_dst, ot_res):
    NG = S // TG
    T = make_tiles(NG, W, lambda qg: [kb for kb in range(4 * qg + 3, 4 * qg - 17, -1) if kb >= 0])

    def s_z(i):
        t = T[i]
        t["zb"] = W.zb[i % len(W.zb)]
        kb, q0 = t["kb"], t["q0"]
        mm(p, t["zb"].t[:], kT[:, kb * 128:(kb + 1) * 128], qT[:, q0:q0 + TG], True, True, r=list(qk_res), w=[t["zb"]])

    def s_exp(i):
        t = T[i]
        zb = t["zb"]
        e = W.nxt(W.e32, "e")
        t["e"] = e
        p.act(lambda e_: e_.activation(e.t[:], zb.t[:], AF.Exp), r=[zb], w=[e])

    def s_mul(i):
        t = T[i]
        e, dl = t["e"], t["dl"]
        wt = W.nxt(W.wbf, "w")
        t["wt"] = wt
        p.dve(lambda e_: e_.tensor_tensor(wt.t[:], e.t[:], rd.t[:, dl + 384:dl + 384 + TG], ALU.mult), r=[e, rd], w=[wt])

    def s_pv(i):
        t = T[i]
        ob, idx, n, q0 = t["ob"], t["idx"], t["n"], t["q0"]
        mm(p, ob.t[:, :], vt.t[:, t["kb"] * 65:t["kb"] * 65 + 128], t["wt"].t[:],
           idx == 0, idx == n - 1, r=[vt, t["wt"]] + ([] if idx == 0 else [ob]), w=[ob])
        if idx == n - 1:
            norm_epilogue(p, c, W, ob, ot_dst[:, q0:q0 + TG], ot_res)

    run_pipeline(len(T), [(2, s_z), (1, s_exp), (0, s_mul), (-1, s_pv)])


HEAD_DIM = 64
NEG = -30000.0


def _mask_strips():
    k = np.arange(128)[:, None]
    x = np.arange(896)[None, :]
    d = x - 384 - k
    rs = (d >= 1).astype(np.float32)
    rc = (d >= 0).astype(np.float32)
    return np.ascontiguousarray(np.concatenate([rs, (rs - 1) * (-NEG), rc, (rc - 1) * (-NEG)], axis=1).astype(np.float32))


def _dilated_strip(head, n_heads=16):
    slope = 2.0 ** (-8.0 * (head + 1) / n_heads)
    k = np.arange(128)[:, None]
    x = np.arange(2944)[None, :]
    d = (x - 384 - k).astype(np.int64)
    mult = ((d >= 0) & (d <= 128)).astype(np.float64) + ((d >= 0) & (d <= 512) & (d % 4 == 0)) \
        + ((d >= 0) & (d <= 2048) & (d % 16 == 0))
    g = mult * np.exp(-slope * np.maximum(d, 0).astype(np.float64))
    return np.ascontiguousarray(g.astype(np.float32))


def _aconst():
    kp = np.arange(128)[:, None]
    k = np.arange(128)[None, :]
    return np.ascontiguousarray(np.concatenate([-(kp >= k).astype(np.float32), -np.ones((128, 128), np.float32),
                                                (kp <= k).astype(np.float32)], axis=1))


ARENA_ELEMS = 41344
K128 = bool(int(_os.environ.get("K128", "1")))
KQ = 128 if K128 else 64


def build_ab0(S=SEQ, D=D_MODEL):
    KC = D // 128
    NB = S // 128
    nc = bass.Bass("TRN2", target_bir_lowering=False)
    hT_all = nc.dram_tensor("hT_all", [D, S], BF16, kind="ExternalInput").ap()
    w_fm = nc.dram_tensor("w_fm", [D, 1024], F32, kind="ExternalInput").ap()
    w_tm = nc.dram_tensor("w_tm", [D, 512], F32, kind="ExternalInput").ap()
    cst = nc.dram_tensor("consts", [128, 384], F32, kind="ExternalInput").ap()
    acst = nc.dram_tensor("aconst", [128, 384], F32, kind="ExternalInput").ap()
    masks = nc.dram_tensor("masks", [128, 4 * 896], F32, kind="ExternalInput").ap()
    rdd = nc.dram_tensor("rd", [4, 128, 2944], F32, kind="ExternalInput").ap()
    OT = nc.dram_tensor("OT", [512, S], BF16, kind="ExternalOutput").ap()
    fm = nc.dram_tensor("fm_scr", [1024, S], BF16).ap()
    vd = nc.dram_tensor("v_scr", [8, 128, NB * 65], BF16).ap()
    p = Prog(nc)
    c = make_common(p, cst)
    arena = p.sb("arena", [128, ARENA_ELEMS], BF16)
    a32 = p.sb("ac32", [128, 256], F32)
    negU = p.sb("negU", [128, 128], BF16)
    negones = p.sb("negones", [128, 128], BF16)
    p.dma("sp", a32.t[:], acst[:, 0:256], w=[a32])
    p.dve(lambda e: e.tensor_copy(negU.t[:], a32.t[:, 0:128]), r=[a32], w=[negU])
    p.dve(lambda e: e.tensor_copy(negones.t[:], a32.t[:, 128:256]), r=[a32], w=[negones])
    chunks = [dict(w=128, col=i * 128, row=i * 128, scale=(0.125 if (i // 2) % 2 == 0 else 1.0)) for i in range(8)]
    fm_res, v_res = p.dram("fm"), p.dram("v")
    phase_a(p, c, S=S, D=D, hT_all=hT_all, hT_res=p.dram("hT_all"), w_fm=w_fm, w_tm=w_tm, NFM=1024, NTM=512,
            fm_chunks=chunks, fm_dst=fm, fm_res=fm_res, v_dst=vd, v_res=v_res, nvh=8, arena=arena)
    p.barrier()
    msb = load_masks(p, masks, 4)
    W = AttnWork(p, c, msb)
    rds = [p.sb("rd%d" % i, [128, 2944], F32) for i in range(2)]
    sets = []
    off = 0
    for i in range(2):
        qv = Res("qT%d" % i, arena.t[:, off:off + S]); off += S
        kv = Res("kT%d" % i, arena.t[:, off:off + S]); off += S
        if K128:
            p.pool(lambda e, qv=qv: e.memset(qv.t[64:128, :], 0.0), w=[qv])
            p.pool(lambda e, kv=kv: e.memset(kv.t[64:128, :], 0.0), w=[kv])
        vv = Res("vt%d" % i, arena.t[:, off:off + NB * 65 + 64]); off += NB * 65 + 64
        sets.append((qv, kv, vv))
    assert off <= ARENA_ELEMS
    ot_res = p.dram("OT")
    order = [0, 4, 1, 5, 2, 6, 3, 7]

    def loads(k):
        hh = order[k]
        qv, kv, vv = sets[k % 2]
        e = hh % 4
        if hh < 4:
            qrow, krow = e * 64, 256 + e * 64
        else:
            qrow, krow = 512 + e * 64, 768 + e * 64
        p.dma("sp", qv.t[0:64, :], fm[qrow:qrow + 64, :], r=[fm_res], w=[qv])
        p.dma("sp", kv.t[0:64, :], fm[krow:krow + 64, :], r=[fm_res], w=[kv])
        p.dma("sp", vv.t[:, 0:NB * 65], vd[hh], r=[v_res], w=[vv])
        if hh >= 4:
            p.dma("sp", rds[k % 2].t[:], rdd[e], w=[rds[k % 2]])

    loads(0)
    for k, hh in enumerate(order):
        if k + 1 < len(order):
            loads(k + 1)
        qv, kv, vv = sets[k % 2]
        if hh < 4:
            attn_sb(p, c, W, S=S, kT=kv.t[0:KQ, :], qT=qv.t[0:KQ, :], qk_res=(qv, kv), vt=vv, negU=negU, negones=negones,
                    ot_dst=OT[hh * 64:(hh + 1) * 64, :], ot_res=ot_res)
        else:
            attn_win(p, c, W, S=S, kT=kv.t[0:KQ, :], qT=qv.t[0:KQ, :], qk_res=(qv, kv), vt=vv, rd=rds[k % 2],
                     ot_dst=OT[hh * 64:(hh + 1) * 64, :], ot_res=ot_res)
    p.mark_outputs(ot_res)
    p.emit()
    return nc


def build_cq(p, c, W, cpos, ci, qg, slot):
    for j in range(TG // 128):
        blk = qg * (TG // 128) + j
        dg = W.dg[W.n["d"] % 2]
        W.n["d"] += 1
        p.dve(lambda e, dg=dg, blk=blk: e.tensor_scalar(
            dg.t[:], c.ident.t[:], cpos.t[:, blk, ci:ci + 1], -1.0, ALU.mult, ALU.mult), r=[c.ident, cpos], w=[dg])
        mm(p, W.misc.t[:, j * 128:(j + 1) * 128], c.ones32.t[:], dg.t[:], True, True, r=[c.ones32, dg], w=[W.misc])
    cq = W.cqb[slot]
    p.act(lambda e: e.copy(cq.t[:], W.misc.t[:]), r=[W.misc], w=[cq])
    return cq


def attn_fox(p, c, W, *, S, kT, qT, qk_res, vt, cpos, ci, ot_dst, ot_res, M_NEGC=3):
    NG = S // TG
    T = make_tiles(NG, W, lambda qg: list(range(4 * qg + 3, -1, -1)))
    cqs = {}

    def s_z(i):
        t = T[i]
        if t["idx"] == 0:
            cqs[t["qg"]] = build_cq(p, c, W, cpos, ci, t["qg"], t["qg"] % 2)
        t["zb"] = W.zb[i % len(W.zb)]
        kb, q0 = t["kb"], t["q0"]
        mm(p, t["zb"].t[:], kT[:, kb * 128:(kb + 1) * 128], qT[:, q0:q0 + TG], True, True, r=list(qk_res), w=[t["zb"]])

    def s_add(i):
        t = T[i]
        zb, dl = t["zb"], t["dl"]
        cq = cqs[t["qg"]]
        tt = W.nxt(W.t32, "t")
        t["t"] = tt
        p.dve(lambda e: e.tensor_tensor(tt.t[:], zb.t[:], cq.t[:], ALU.add), r=[zb, cq], w=[tt])
        if t["diag"]:
            p.pool(lambda e: e.tensor_tensor(
                tt.t[:], tt.t[:], W.masks.t[:, M_NEGC, dl + 384:dl + 384 + TG], ALU.add), r=[tt, W.masks], w=[tt])

    def s_exp(i):
        t = T[i]
        tt, kb = t["t"], t["kb"]
        wt = W.nxt(W.wbf, "w")
        t["wt"] = wt
        p.act(lambda e: e.activation(wt.t[:], tt.t[:], AF.Exp, bias=cpos.t[:, kb, ci:ci + 1]), r=[tt, cpos], w=[wt])

    def s_pv(i):
        t = T[i]
        ob, idx, n, q0 = t["ob"], t["idx"], t["n"], t["q0"]
        mm(p, ob.t[:, :], vt.t[:, t["kb"] * 65:t["kb"] * 65 + 128], t["wt"].t[:],
           idx == 0, idx == n - 1, r=[vt, t["wt"]] + ([] if idx == 0 else [ob]), w=[ob])
        if idx == n - 1:
            norm_epilogue(p, c, W, ob, ot_dst[:, q0:q0 + TG], ot_res)

    run_pipeline(len(T), [(2, s_z), (1, s_add), (0, s_exp), (-1, s_pv)])


def attn_ssd(p, c, W, *, S, BT, CT, bc_res, xdt, xdt_flat, cpos, fm, fm_res, zrow, xrow, svec, ygs, ot_dst, ot_res,
             G=4, M_NEGC=3):
    NG = S // TG
    zbs = [c.banks[0], c.banks[1]]
    obs = [c.banks[2], c.banks[3], c.banks[4], c.banks[5]]

    for qg in range(NG):
        q0 = qg * TG
        cqs = [build_cq(p, c, W, cpos, 4 + e, qg, 2 + e) for e in range(4)]
        kbs = list(range(4 * qg + 3, -1, -1))
        n = len(kbs)
        T = [dict(kb=kb, diag=(kb >= 4 * qg), dl=q0 - kb * 128) for kb in kbs]

        def s_z(i):
            t = T[i]
            t["zb"] = zbs[i % 2]
            kb = t["kb"]
            mm(p, t["zb"].t[:], BT[:, kb * 128:(kb + 1) * 128], CT[:, q0:q0 + TG], True, True, r=list(bc_res), w=[t["zb"]])

        def s_dec(i):
            t = T[i]
            kb, dl = t["kb"], t["dl"]
            t["dec"] = []
            for e in range(4):
                if t["diag"]:
                    tt = W.nxt(W.t32, "t")
                    p.pool(lambda e_, tt=tt, e=e: e_.tensor_tensor(
                        tt.t[:], cqs[e].t[:], W.masks.t[:, M_NEGC, dl + 384:dl + 384 + TG], ALU.add),
                        r=[cqs[e], W.masks], w=[tt])
                    src = tt
                else:
                    src = cqs[e]
                dec = W.nxt(W.dec, "d2")
                t["dec"].append(dec)
                p.act(lambda e_, dec=dec, src=src, e=e: e_.activation(
                    dec.t[:], src.t[:], AF.Exp, bias=cpos.t[:, kb, 4 + e:5 + e]), r=[src, cpos], w=[dec])

        def s_mul(i):
            t = T[i]
            zb = t["zb"]
            t["wt"] = []
            for e in range(4):
                wt = W.nxt(W.wts, "w2")
                t["wt"].append(wt)
                dec = t["dec"][e]
                p.dve(lambda e_, wt=wt, dec=dec: e_.tensor_tensor(wt.t[:], zb.t[:], dec.t[:], ALU.mult),
                      r=[zb, dec], w=[wt])

        def s_pv(i):
            t = T[i]
            for e in range(4):
                wt = t["wt"][e]
                o_ = (t["kb"] * 4 + e) * 64
                mm(p, obs[e].t[:, :], xdt_flat[:, o_:o_ + 128], wt.t[:], i == 0, i == n - 1,
                   r=[xdt, wt] + ([] if i == 0 else [obs[e]]), w=[obs[e]])

        run_pipeline(n, [(1, s_z), (1, s_dec), (0, s_mul), (-1, s_pv)])
        for e in range(4):
            zs = W.nxt(W.wbf, "w")
            xsl = W.nxt(W.wbf, "w")
            p.dma("sp", zs.t[0:64, :], fm[zrow + 64 * e:zrow + 64 * e + 64, q0:q0 + TG], r=[fm_res], w=[zs])
            p.dma("sp", xsl.t[0:64, :], fm[xrow + 64 * e:xrow + 64 * e + 64, q0:q0 + TG], r=[fm_res], w=[xsl])
            y = W.nxt(W.osb, "o")
            p.dve(lambda e_, y=y, xsl=xsl, e=e: e_.scalar_tensor_tensor(
                y.t[0:64, :], xsl.t[0:64, :], svec.t[0:64, 12 + e:13 + e], obs[e].t[0:64, :], ALU.mult, ALU.add),
                r=[xsl, svec, obs[e]], w=[y])
            sz = W.nxt(W.t32, "t")
            p.act(lambda e_, sz=sz, zs=zs: e_.activation(sz.t[0:64, :], zs.t[0:64, :], AF.Silu), r=[zs], w=[sz])
            yg = ygs[e]
            p.dve(lambda e_, yg=yg, y=y, sz=sz: e_.tensor_tensor(yg.t[0:64, :], y.t[0:64, :], sz.t[0:64, :], ALU.mult),
                  r=[y, sz], w=[yg])
            sq = W.nxt(W.spb, "s")
            p.act(lambda e_, sq=sq, yg=yg: e_.activation(sq.t[0:64, :], yg.t[0:64, :], AF.Square), r=[yg], w=[sq])
            mm(p, W.misc.t[0:64, :], c.ones_bf.t[0:64, 0:64], sq.t[0:64, :], e == 0, e == 3,
               r=[c.ones_bf, sq] + ([] if e == 0 else [W.misc]), w=[W.misc])
        p.dve(lambda e_: e_.tensor_scalar(W.rden.t[0:64, :], W.misc.t[0:64, :], 1.0 / (64 * G), None, ALU.mult),
              r=[W.misc], w=[W.rden])
        p.dve(lambda e_: e_.tensor_scalar(W.rden.t[0:64, :], W.rden.t[0:64, :], EPS, None, ALU.add),
              r=[W.rden], w=[W.rden])
        p.act(lambda e_: e_.activation(W.rden.t[0:64, :], W.rden.t[0:64, :], AF.Ln), r=[W.rden], w=[W.rden])
        p.act(lambda e_: e_.activation(W.rden.t[0:64, :], W.rden.t[0:64, :], AF.Exp, scale=-0.5), r=[W.rden], w=[W.rden])
        for e in range(4):
            obf = W.nxt(W.obf, "o")
            p.dve(lambda e_, obf=obf, e=e: e_.scalar_tensor_tensor(
                obf.t[0:64, :], ygs[e].t[0:64, :], svec.t[0:64, 16 + e:17 + e], W.rden.t[0:64, :], ALU.mult, ALU.mult),
                r=[ygs[e], svec, W.rden], w=[obf])
            p.dma("sp", ot_dst[64 * e:64 * e + 64, q0:q0 + TG], obf.t[0:64, :], r=[obf], w=[ot_res])


def build_ab1(S=SEQ, D=D_MODEL):
    KC = D // 128
    NB = S // 128
    nc = bass.Bass("TRN2", target_bir_lowering=False)
    hT_all = nc.dram_tensor("hT_all", [D, S], BF16, kind="ExternalInput").ap()
    w_fm = nc.dram_tensor("w_fm", [D, 1280], F32, kind="ExternalInput").ap()
    w_tm = nc.dram_tensor("w_tm", [D, 264], F32, kind="ExternalInput").ap()
    cst = nc.dram_tensor("consts", [128, 384], F32, kind="ExternalInput").ap()
    acst = nc.dram_tensor("aconst", [128, 384], F32, kind="ExternalInput").ap()
    masks = nc.dram_tensor("masks", [128, 4 * 896], F32, kind="ExternalInput").ap()
    convd = nc.dram_tensor("convw", [128, 20], F32, kind="ExternalInput").ap()
    svd = nc.dram_tensor("svec", [128, 32], F32, kind="ExternalInput").ap()
    OT = nc.dram_tensor("OT", [512, S], BF16, kind="ExternalOutput").ap()
    fm = nc.dram_tensor("fm_scr", [1280, S], BF16).ap()
    vd = nc.dram_tensor("v_scr", [4, 128, NB * 65], BF16).ap()
    p = Prog(nc)
    c = make_common(p, cst)
    arena = p.sb("arena", [128, ARENA_ELEMS], BF16)
    tri32 = p.sb("tri32", [128, 128], F32)
    p.dma("sp", tri32.t[:], acst[:, 256:384], w=[tri32])
    ident_bf = p.sb("ident_bf", [128, 128], BF16)
    p.dve(lambda e: e.tensor_copy(ident_bf.t[:], c.ident.t[:]), r=[c.ident], w=[ident_bf])
    convw = p.sb("convw_sb", [128, 4, 5], F32)
    p.dma("sp", convw.t[:], convd.rearrange("p (a b) -> p a b", a=4), w=[convw])
    svec = p.sb("svec_sb", [128, 32], F32)
    p.dma("sp", svec.t[:], svd, w=[svec])
    fd = p.sb("fd", [128, NB, 8], F32)
    chunks = [dict(w=128, col=0, row=0, scale=0.125), dict(w=128, col=128, row=128, scale=0.125),
              dict(w=128, col=256, row=256), dict(w=128, col=384, row=384),
              dict(w=128, col=512, row=512, conv=0), dict(w=128, col=640, row=640, conv=1)]
    chunks += [dict(w=128, col=768 + 128 * j, row=768 + 128 * j) for j in range(2)]
    chunks += [dict(w=128, col=1024 + 128 * j, row=1024 + 128 * j, conv=2 + j) for j in range(2)]
    fm_res, v_res = p.dram("fm"), p.dram("v")
    phase_a(p, c, S=S, D=D, hT_all=hT_all, hT_res=p.dram("hT_all"), w_fm=w_fm, w_tm=w_tm, NFM=1280, NTM=264,
            fm_chunks=chunks, fm_dst=fm, fm_res=fm_res, v_dst=vd, v_res=v_res, nvh=4, arena=arena,
            fd=fd, nfd=8, convw=convw)
    p.barrier()
    msb = load_masks(p, masks, 4)
    W = AttnWork(p, c, msb, nt32=4)
    W.dg = [p.sb("w_dg%d" % j, [128, 128], F32) for j in range(2)]
    W.cqb = [p.sb("w_cqb%d" % j, [128, TG], F32) for j in range(6)]
    W.dec = [p.sb("w_dec%d" % j, [128, TG], BF16) for j in range(8)]
    W.wts = [p.sb("w_wts%d" % j, [128, TG], BF16) for j in range(8)]
    W.n["d2"] = 0
    W.n["w2"] = 0
    t1 = p.sb("s_t1", [128, NB, 8], F32)
    l8 = p.sb("s_l8", [128, NB, 8], F32)
    vals = p.sb("s_vals", [128, NB, 8], F32)
    cpos = p.sb("s_cpos", [128, NB, 8], F32)
    vsum = p.sb("s_vsum", [128, 8], F32)
    ea = p.sb("s_ea", [128, 4], F32)
    p.dve(lambda e: e.tensor_tensor(t1.t[:], fd.t[:], svec.t[:, 0:8].unsqueeze(1).to_broadcast([128, NB, 8]), ALU.add),
          r=[fd, svec], w=[t1])
    p.act(lambda e: e.activation(t1.t[:, :, 0:4], t1.t[:, :, 0:4], AF.Exp, scale=-1.0), r=[t1], w=[t1])
    p.act(lambda e: e.activation(t1.t[:, :, 4:8], t1.t[:, :, 4:8], AF.Exp), r=[t1], w=[t1])
    p.act(lambda e: e.activation(l8.t[:], t1.t[:], AF.Ln, bias=W.onecol.t[:, 0:1]), r=[t1, W.onecol], w=[l8])
    p.act(lambda e: e.activation(ea.t[:], svec.t[:, 8:12], AF.Exp), r=[svec], w=[ea])
    p.dve(lambda e: e.tensor_copy(vals.t[:, :, 0:4], l8.t[:, :, 0:4]), r=[l8], w=[vals])
    p.dve(lambda e: e.tensor_tensor(vals.t[:, :, 4:8], l8.t[:, :, 4:8],
                                    ea.t[:].unsqueeze(1).to_broadcast([128, NB, 4]), ALU.mult), r=[l8, ea], w=[vals])
    for blk in range(NB):
        mm(p, W.misc.t[:, 0:8], tri32.t[:], vals.t[:, blk, :], True, blk == 0, r=[tri32, vals], w=[W.misc])
        if blk > 0:
            mm(p, W.misc.t[:, 0:8], c.ones32.t[:], vsum.t[:], False, True, r=[c.ones32, vsum, W.misc], w=[W.misc])
        p.dve(lambda e, blk=blk: e.tensor_copy(cpos.t[:, blk, :], W.misc.t[:, 0:8]), r=[W.misc], w=[cpos])
        if blk == 0:
            p.dve(lambda e, blk=blk: e.tensor_copy(vsum.t[:], vals.t[:, blk, :]), r=[vals], w=[vsum])
        elif blk < NB - 1:
            p.dve(lambda e, blk=blk: e.tensor_tensor(vsum.t[:], vsum.t[:], vals.t[:, blk, :], ALU.add),
                  r=[vals, vsum], w=[vsum])
    sets = []
    off = 0
    for i in range(2):
        qv = Res("qT%d" % i, arena.t[:, off:off + S]); off += S
        kv = Res("kT%d" % i, arena.t[:, off:off + S]); off += S
        if K128:
            p.pool(lambda e, qv=qv: e.memset(qv.t[64:128, :], 0.0), w=[qv])
            p.pool(lambda e, kv=kv: e.memset(kv.t[64:128, :], 0.0), w=[kv])
        vv = Res("vt%d" % i, arena.t[:, off:off + NB * 65 + 64]); off += NB * 65 + 64
        sets.append((qv, kv, vv))
    assert off <= ARENA_ELEMS
    ot_res = p.dram("OT")

    def loads(e):
        qv, kv, vv = sets[e % 2]
        p.dma("sp", qv.t[0:64, :], fm[e * 64:e * 64 + 64, :], r=[fm_res], w=[qv])
        p.dma("sp", kv.t[0:64, :], fm[256 + e * 64:256 + e * 64 + 64, :], r=[fm_res], w=[kv])
        p.dma("sp", vv.t[:, 0:NB * 65], vd[e], r=[v_res], w=[vv])

    loads(0)
    for e in range(4):
        if e + 1 < 4:
            loads(e + 1)
        qv, kv, vv = sets[e % 2]
        attn_fox(p, c, W, S=S, kT=kv.t[0:KQ, :], qT=qv.t[0:KQ, :], qk_res=(qv, kv), vt=vv, cpos=cpos, ci=e,
                 ot_dst=OT[e * 64:(e + 1) * 64, :], ot_res=ot_res)
    p.barrier()
    BT = Res("BT", arena.t[:, 0:S])
    CT = Res("CT", arena.t[:, S:2 * S])
    xdt = Res("xdt", arena.t[:, 2 * S:2 * S + NB * 256].rearrange("p (b e d) -> p b e d", e=4, d=64))
    xdt_flat = arena.t[:, 2 * S:2 * S + NB * 256 + 64]
    xst = Res("xst", arena.t[0:64, 2 * S + NB * 256 + 64:3 * S + NB * 256 + 64])
    assert 3 * S + NB * 256 + 64 <= ARENA_ELEMS
    p.dma("sp", BT.t, fm[512:640, :], r=[fm_res], w=[BT])
    p.dma("sp", CT.t, fm[640:768, :], r=[fm_res], w=[CT])
    mbf = W.misc.t[:].bitcast(BF16)
    for e in range(4):
        p.dma("sp", xst.t, fm[1024 + 64 * e:1024 + 64 * e + 64, :], r=[fm_res], w=[xst])
        for blk in range(NB):
            p.pe(lambda e_, blk=blk: e_.transpose(mbf[:, 0:64], xst.t[:, blk * 128:(blk + 1) * 128], ident_bf.t[0:64, 0:64]),
                 r=[xst, ident_bf], w=[W.misc])
            p.dve(lambda e_, blk=blk, e=e: e_.tensor_scalar(
                xdt.t[:, blk, e, :], mbf[:, 0:64], l8.t[:, blk, 4 + e:5 + e], None, ALU.mult), r=[W.misc, l8], w=[xdt])
    ygs = [p.sb("w_yg%d" % i, [128, TG], F32) for i in range(4)]
    attn_ssd(p, c, W, S=S, BT=BT.t, CT=CT.t, bc_res=(BT, CT), xdt=xdt, xdt_flat=xdt_flat, cpos=cpos, fm=fm, fm_res=fm_res,
             zrow=768, xrow=1024, svec=svec, ygs=ygs, ot_dst=OT[256:512, :], ot_res=ot_res)
    p.mark_outputs(ot_res)
    p.emit()
    return nc


WCAST = (("w_out", 2048, 2048), ("w_up", 2048, 8192), ("w_down", 8192, 2048))


def build_p0(NT, D, wcast=True):
    KC = D // 128
    nc = bass.Bass("TRN2", target_bir_lowering=False)
    x = nc.dram_tensor("x", [NT, D], F32, kind="ExternalInput").ap()
    cst = nc.dram_tensor("consts", [128, 384], F32, kind="ExternalInput").ap()
    nvd = nc.dram_tensor("nv", [128, 8 * KC], F32, kind="ExternalInput").ap()
    xT = nc.dram_tensor("xT", [D, NT], F32, kind="ExternalOutput").ap()
    hT = nc.dram_tensor("hT", [D, NT], BF16, kind="ExternalOutput").ap()
    p = Prog(nc)
    c = make_common(p, cst)
    nv = p.sb("nv_sb", [128, 8 * KC], F32)
    p.dma("sp", nv.t[:], nvd, w=[nv])
    phase_c(p, c, D=D, DFF=4 * D, F=D, NT=NT, OT_src=None, OT_res=None, xT=xT, xT_res=p.dram("xT"), w_out=None,
            w_up=None, w_down=None, nv=nv, v_post=0, v_pre=0, v_mpost=0, v_next=0,
            hT_dst=lambda t0: hT[:, t0:t0 + TG], hT_res=p.dram("hT"),
            x_src=lambda t0: x[t0:t0 + TG, :], x_res=p.dram("x"), nwb=1)
    p.mark_outputs(p.dram("xT"))
    p.mark_outputs(p.dram("hT"))
    if wcast:
        stg = [p.sb("wc_stg%d" % i, [128, 8192], BF16) for i in range(2)]
        k = 0
        for l in range(2):
            for name, rows, cols in WCAST:
                rs = rows // NCORES
                src = nc.dram_tensor("%s%d_f32" % (name, l), [rs, cols], F32, kind="ExternalInput").ap()
                dst = nc.dram_tensor("%s%d_bf" % (name, l), [rs, cols], BF16, kind="ExternalOutput").ap()
                sv = src.rearrange("(a p) c -> p a c", p=128)
                dv = dst.rearrange("(a p) c -> p a c", p=128)
                na = rs // 128
                per = max(1, 8192 // cols)
                for a0 in range(0, na, per):
                    a1 = min(na, a0 + per)
                    t = stg[k % 2]
                    k += 1
                    tv = t.t[:, 0:(a1 - a0) * cols].rearrange("p (a c) -> p a c", c=cols)
                    p.dma("pool", tv, sv[:, a0:a1, :], w=[t])
                    p.dma("sp", dv[:, a0:a1, :], tv, r=[t], w=[p.dram("wcast")], out=True)
    p.emit()
    return nc


def build_c(NT, D, DFF, layer, last, wbf=True):
    KC = D // 128
    nc = bass.Bass("TRN2", target_bir_lowering=False)
    OT = nc.dram_tensor("OT", [D, NT], BF16, kind="ExternalInput").ap()
    xT = nc.dram_tensor("xT", [D, NT], F32, kind="ExternalInput").ap()
    cst = nc.dram_tensor("consts", [128, 384], F32, kind="ExternalInput").ap()
    nvd = nc.dram_tensor("nv", [128, 8 * KC], F32, kind="ExternalInput").ap()
    WDT = BF16 if wbf else F32
    w_out = nc.dram_tensor("w_out", [D, D], WDT, kind="ExternalInput").ap()
    w_up = nc.dram_tensor("w_up", [D, DFF], WDT, kind="ExternalInput").ap()
    w_down = nc.dram_tensor("w_down", [DFF, D], WDT, kind="ExternalInput").ap()
    p = Prog(nc)
    c = make_common(p, cst)
    nv = p.sb("nv_sb", [128, 8 * KC], F32)
    p.dma("sp", nv.t[:], nvd, w=[nv])
    V = lambda i: i * KC
    b = 4 * layer
    kw = dict(D=D, DFF=DFF, F=D, NT=NT, OT_src=lambda t0: OT[:, t0:t0 + TG], OT_res=p.dram("OT"),
              xT=xT, xT_res=p.dram("xT"), w_out=w_out, w_up=w_up, w_down=w_down, nv=nv,
              v_post=V(b + 1), v_pre=V(b + 2), v_mpost=V(b + 3), v_next=V((b + 4) % 8),
              wq="pool", nwb=5)
    if last:
        out = nc.dram_tensor("out", [NT, D], F32, kind="ExternalOutput").ap()
        phase_c(p, c, out_dst=lambda t0: out[t0:t0 + TG, :], out_res=p.dram("out"), **kw)
    else:
        xTo = nc.dram_tensor("xTo", [D, NT], F32, kind="ExternalOutput").ap()
        hT = nc.dram_tensor("hT", [D, NT], BF16, kind="ExternalOutput").ap()
        phase_c(p, c, hT_dst=lambda t0: hT[:, t0:t0 + TG], hT_res=p.dram("hT"),
                xT_out=xTo, xT_out_res=p.dram("xTo"), **kw)
        p.mark_outputs(p.dram("xTo"))
        p.mark_outputs(p.dram("hT"))
    p.emit()
    return nc


def _c32(a):
    return np.ascontiguousarray(np.asarray(a, np.float32))


def kernel(x, mix_norm_pre, mix_norm_post, mlp_norm_pre, mlp_norm_post, ab_w_in, ab_w_out,
           cd_w_in, cd_b_f, cd_conv_w, cd_conv_b, cd_dt_bias, cd_a_log, cd_d_skip,
           cd_gate_norm, cd_w_out, mlp_w_up, mlp_w_down):
    x = np.asarray(x, np.float32)
    B, S, D = x.shape
    G = NCORES // B
    NT = S // G
    DFF = mlp_w_up.shape[-1]
    cores = list(range(NCORES))
    xs = x.reshape(NCORES, NT, D)
    vecs = []
    for l in range(2):
        vecs += [mix_norm_pre[l], mix_norm_post[l], mlp_norm_pre[l], mlp_norm_post[l]]
    nv = _nv_layout(vecs)
    consts = _consts()
    aconst = _aconst()
    masks = _mask_strips()

    def gather_h(res, key):
        return [np.ascontiguousarray(np.concatenate([res[b * G + g][key] for g in range(G)], axis=1)) for b in range(B)]

    def scatter_ot(res):
        outs = []
        for b in range(B):
            full = np.concatenate([res[b * G + g]["OT"][0:256] for g in range(G)]
                                  + [res[b * G + g]["OT"][256:512] for g in range(G)], axis=0)
            for g in range(G):
                outs.append(np.ascontiguousarray(full[:, g * NT:(g + 1) * NT]))
        return outs

    wsrc = {"w_out0": ab_w_out[0], "w_out1": cd_w_out[0], "w_up0": mlp_w_up[0], "w_up1": mlp_w_up[1],
            "w_down0": mlp_w_down[0], "w_down1": mlp_w_down[1]}
    im = []
    for i in cores:
        d = dict(x=np.ascontiguousarray(xs[i]), consts=consts, nv=nv)
        for k, wfull in wsrc.items():
            rs = wfull.shape[0] // NCORES
            d[k + "_f32"] = _c32(wfull[i * rs:(i + 1) * rs])
        im.append(d)
    r1 = run_bass_kernel_spmd(build_p0(NT, D), im, core_ids=cores).results
    wbf = {k: np.ascontiguousarray(np.concatenate([r1[i][k + "_bf"] for i in cores], axis=0)) for k in wsrc}
    xT = [r1[i]["xT"] for i in cores]
    hT_all = gather_h(r1, "hT")
    W0 = np.asarray(ab_w_in[0], np.float32)
    sec = [W0[:, i * 1024:(i + 1) * 1024] for i in range(6)]
    im = []
    for i in cores:
        b, g = divmod(i, G)
        sl = slice(256 * g, 256 * g + 256)
        im.append(dict(hT_all=hT_all[b], consts=consts, aconst=aconst, masks=masks,
                       w_fm=np.ascontiguousarray(np.concatenate([sec[0][:, sl], sec[1][:, sl], sec[3][:, sl], sec[4][:, sl]], axis=1)),
                       w_tm=np.ascontiguousarray(np.concatenate([sec[2][:, sl], sec[5][:, sl]], axis=1)),
                       rd=np.stack([_dilated_strip(4 * g + e) for e in range(4)])))
    r2 = run_bass_kernel_spmd(build_ab0(S, D), im, core_ids=cores).results
    ot = scatter_ot(r2)
    r3 = run_bass_kernel_spmd(build_c(NT, D, DFF, 0, False),
                              [dict(OT=ot[i], xT=xT[i], consts=consts, nv=nv, w_out=wbf["w_out0"],
                                    w_up=wbf["w_up0"], w_down=wbf["w_down0"]) for i in cores],
                              core_ids=cores).results
    xT = [r3[i]["xTo"] for i in cores]
    hT_all = gather_h(r3, "hT")
    W1 = np.asarray(cd_w_in[0], np.float32)
    qc, kc, vc = W1[:, 0:1024], W1[:, 1024:2048], W1[:, 2048:3072]
    fr, zz = W1[:, 3072:3088], W1[:, 3088:4112]
    xsw, Bw, Cw, dtw = W1[:, 4112:5136], W1[:, 5136:5648], W1[:, 5648:6160], W1[:, 6160:6176]
    cw = np.asarray(cd_conv_w[0], np.float32)
    cb = np.asarray(cd_conv_b[0], np.float32)
    im = []
    for i in cores:
        b, g = divmod(i, G)
        sl = slice(256 * g, 256 * g + 256)
        s128 = slice(128 * g, 128 * g + 128)
        s4 = slice(4 * g, 4 * g + 4)
        convw = np.zeros((128, 4, 5), np.float32)

        def cwl(ch0, n):
            return np.concatenate([cw[:, ch0:ch0 + n].T, cb[ch0:ch0 + n, None]], axis=1)
        convw[:, 0] = cwl(1024 + 128 * g, 128)
        convw[:, 1] = cwl(1536 + 128 * g, 128)
        svec = np.zeros((128, 32), np.float32)
        svec[:, 0:4] = np.asarray(cd_b_f[0], np.float32)[s4]
        svec[:, 4:8] = np.asarray(cd_dt_bias[0], np.float32)[s4]
        svec[:, 8:12] = np.asarray(cd_a_log[0], np.float32)[s4]
        svec[:, 12:16] = np.asarray(cd_d_skip[0], np.float32)[s4]
        for j in range(2):
            convw[:, 2 + j] = cwl(256 * g + 128 * j, 128)
        for e in range(4):
            svec[0:64, 16 + e] = np.asarray(cd_gate_norm[0], np.float32)[256 * g + 64 * e:256 * g + 64 * e + 64]
        im.append(dict(hT_all=hT_all[b], consts=consts, aconst=aconst, masks=masks,
                       w_fm=np.ascontiguousarray(np.concatenate([qc[:, sl], kc[:, sl], Bw[:, s128], Cw[:, s128],
                                                                 zz[:, sl], xsw[:, sl]], axis=1)),
                       w_tm=np.ascontiguousarray(np.concatenate([vc[:, sl], fr[:, s4], dtw[:, s4]], axis=1)),
                       convw=np.ascontiguousarray(convw.reshape(128, 20)), svec=svec))
    r4 = run_bass_kernel_spmd(build_ab1(S, D), im, core_ids=cores).results
    ot = scatter_ot(r4)
    r5 = run_bass_kernel_spmd(build_c(NT, D, DFF, 1, True),
                              [dict(OT=ot[i], xT=xT[i], consts=consts, nv=nv, w_out=wbf["w_out1"],
                                    w_up=wbf["w_up1"], w_down=wbf["w_down1"]) for i in cores],
                              core_ids=cores).results
    out = np.stack([np.asarray(r5[i]["out"], np.float32) for i in cores], axis=0)
    return out.reshape(B, S, D)
```
